# Optimizing a Trainium2 kernel written in Bass

```python
import math
import jax
import jax.numpy as jnp
from jax import lax
import numpy as np

D_MODEL = 1024
BATCH = 16
SEQ = 256
DEPTH = 4
DEC_BATCH = 4
DEC_SEQ = 2048
PAST_LEN = 256

F32 = jnp.float32
GRID_W = 64
N_AB = (DEPTH + 1) // 2
N_C = DEPTH // 2
GLA_W = D_MODEL // 2
GLA_HEADS = 4
GLA_DV = GLA_W // GLA_HEADS
GLA_DK = GLA_DV // 2
GLA_QK = GLA_HEADS * GLA_DK
GLA_LORA = 16
GLA_NORMALIZER = 16.0
GLA_CHUNK = 64
GLA_COLS = 2 * GLA_QK + 2 * GLA_W + 2 * GLA_LORA
GLA_SPLITS = [GLA_QK, 2 * GLA_QK, 2 * GLA_QK + GLA_W, 2 * GLA_QK + 2 * GLA_W]
RWKV_W = D_MODEL - GLA_W
RWKV_N = 64
RWKV_HEADS = RWKV_W // RWKV_N
DECAY_LORA = 64
A_LORA = 64
G_LORA = 160
RWKV_COLS = 3 * RWKV_W + 2 * DECAY_LORA + 2 * A_LORA + G_LORA
RWKV_SPLITS = [RWKV_W, 2 * RWKV_W, 3 * RWKV_W, 3 * RWKV_W + 2 * DECAY_LORA, 3 * RWKV_W + 2 * DECAY_LORA + 2 * A_LORA]
AB_IN = GLA_COLS + RWKV_COLS
HY_W = D_MODEL
HY_EMB = 33
HY_BANDS = (HY_EMB - 1) // 2
HY_FO = 64
HY_TARGET = 1e-2
HY_FAST = 0.3
HY_SLOW = 1.5
D_FF = ((8 * D_MODEL // 3 + 255) // 256) * 256
ALPHA = (2 * DEPTH) ** 0.25
BETA = (8 * DEPTH) ** -0.25
LN_EPS = 1e-5
GN_EPS = 64e-5

kernel_name = 'hybrid_gla_rwkv7_hyena_diffusion_step'


def layer_norm(x, g, b):
    xf = x.astype(F32)
    mu = xf.mean(-1, keepdims=True)
    var = jnp.square(xf - mu).mean(-1, keepdims=True)
    return ((xf - mu) * lax.rsqrt(var + LN_EPS) * g + b).astype(x.dtype)


def prev_tok(x):
    return jnp.pad(x, ((0, 0), (1, 0), (0, 0)))[:, :-1]


def next_tok(x):
    return jnp.pad(x, ((0, 0), (0, 1), (0, 0)))[:, 1:]


def flip(t):
    return jnp.flip(t, axis=1)


def grid_pos_embed(n_tok):
    rows = n_tok // GRID_W
    idx = jnp.arange(rows * GRID_W)
    quarter = D_MODEL // 4
    omega = 1.0 / (10000.0 ** (jnp.arange(quarter, dtype=F32) / quarter))

    def sincos(pos):
        ang = pos.astype(F32)[:, None] * omega[None, :]
        return jnp.concatenate([jnp.sin(ang), jnp.cos(ang)], -1)

    return jnp.concatenate([sincos(idx // GRID_W), sincos(idx % GRID_W)], -1)


def gla_chunked(q, k, v, log_a, s0):
    bsz, n_tok, nh, _ = q.shape
    dv = v.shape[-1]
    n = n_tok // GLA_CHUNK

    def chunks(t):
        return t.astype(F32).reshape(bsz, n, GLA_CHUNK, nh, t.shape[-1])

    q, k, v, log_a = chunks(q), chunks(k), chunks(v), chunks(log_a)
    cum = jnp.cumsum(log_a, axis=2)
    last = cum[:, :, -1:]
    q_in = q * jnp.exp(cum)
    k_in = k * jnp.exp(-cum)
    k_out = k * jnp.exp(last - cum)
    mask = jnp.tril(jnp.ones((GLA_CHUNK, GLA_CHUNK), dtype=bool))
    scores = jnp.where(mask, jnp.einsum('bnchk,bnshk->bnhcs', q_in, k_in), 0.0)
    o_intra = jnp.einsum('bnhcs,bnshv->bnchv', scores, v)
    kv = jnp.einsum('bnchk,bnchv->bnhkv', k_out, v)
    dec = jnp.exp(last[:, :, 0])

    def step(s, inp):
        d, kv_n = inp
        return d[..., None] * s + kv_n, s

    s_fin, s_start = lax.scan(step, s0.astype(F32), (jnp.moveaxis(dec, 1, 0), jnp.moveaxis(kv, 1, 0)))
    o_inter = jnp.einsum('bnchk,nbhkv->bnchv', q_in, s_start)
    return (o_intra + o_inter).reshape(bsz, n_tok, nh, dv), s_fin


def rwkv7_scan(r, w, k, v, kk, a, s0):
    def step(s, inp):
        r_t, w_t, k_t, v_t, kk_t, a_t = inp
        sa = jnp.einsum('bhij,bhj->bhi', s, -kk_t)
        s = (s * w_t[:, :, None, :] + sa[..., None] * (kk_t * a_t)[:, :, None, :]
             + v_t[..., None] * k_t[:, :, None, :])
        return s, jnp.einsum('bhij,bhj->bhi', s, r_t)

    xs = tuple(jnp.moveaxis(t.astype(F32), 1, 0) for t in (r, w, k, v, kk, a))
    s_fin, ys = lax.scan(step, s0.astype(F32), xs)
    return jnp.moveaxis(ys, 0, 1), s_fin


def gla_mix(u, p, s0):
    bsz, n_tok, _ = u.shape
    q, k, v, g, lr = jnp.split(u, GLA_SPLITS, axis=-1)
    lr = lr.reshape(bsz, n_tok, 2, GLA_LORA)
    log_a = jax.nn.log_sigmoid((jnp.einsum('bldr,drk->bldk', lr, p['gla_w_decay'])
                                + p['gla_b_decay']).astype(F32)) / GLA_NORMALIZER
    log_a = log_a.reshape(bsz, n_tok, 2, GLA_HEADS, GLA_DK)
    q = q.reshape(bsz, n_tok, GLA_HEADS, GLA_DK) * (GLA_DK ** -0.5)
    k = k.reshape(bsz, n_tok, GLA_HEADS, GLA_DK)
    v = v.reshape(bsz, n_tok, GLA_HEADS, GLA_DV)
    o_f, s_f = gla_chunked(q, k, v, log_a[:, :, 0], s0[:, 0])
    o_b, s_b = gla_chunked(flip(q), flip(k), flip(v), flip(log_a[:, :, 1]), s0[:, 1])
    o = o_f + flip(o_b)
    o = o * lax.rsqrt(jnp.square(o).mean(-1, keepdims=True) + LN_EPS) * p['gla_norm_w']
    o = o * jax.nn.silu(g.astype(F32).reshape(bsz, n_tok, GLA_HEADS, GLA_DV))
    return o.reshape(bsz, n_tok, GLA_W), jnp.stack([s_f, s_b], axis=1)


def rwkv_mix(u, p, s0):
    bsz, n_tok, _ = u.shape
    mu = p['rwkv_shift_mu']
    u = u + mu[0] * (prev_tok(u) - u) + mu[1] * (next_tok(u) - u)
    r, k, v, wl, al, gl = jnp.split(u, RWKV_SPLITS, axis=-1)
    wl = jnp.tanh(wl.reshape(bsz, n_tok, 2, DECAY_LORA))
    al = al.reshape(bsz, n_tok, 2, A_LORA)
    w_log = -jax.nn.softplus(-(p['rwkv_w0'] + jnp.einsum('bldr,drc->bldc', wl, p['rwkv_w2'])).astype(F32)) - 0.5
    decay = jnp.exp(-jnp.exp(w_log))
    a = jax.nn.sigmoid((p['rwkv_a0'] + jnp.einsum('bldr,drc->bldc', al, p['rwkv_a2'])).astype(F32))
    g = jax.nn.sigmoid(gl) @ p['rwkv_g2']

    def hd(t):
        return t.reshape(bsz, n_tok, RWKV_HEADS, RWKV_N)

    kk = hd(k * p['rwkv_k_k']).astype(F32)
    kk = kk / jnp.maximum(jnp.sqrt(jnp.sum(jnp.square(kk), -1, keepdims=True)), 1e-12)
    k_dir = k[:, :, None, :].astype(F32) * (1.0 + (a - 1.0) * p['rwkv_k_a'])
    rh, vh = hd(r), hd(v)
    o_f, s_f = rwkv7_scan(rh, hd(decay[:, :, 0]), hd(k_dir[:, :, 0]), vh, kk, hd(a[:, :, 0]), s0[:, 0])
    o_b, s_b = rwkv7_scan(flip(rh), flip(hd(decay[:, :, 1])), flip(hd(k_dir[:, :, 1])), flip(vh), flip(kk),
                          flip(hd(a[:, :, 1])), s0[:, 1])
    o = o_f + flip(o_b)
    mean = o.mean(-1, keepdims=True)
    var = jnp.square(o - mean).mean(-1, keepdims=True)
    o = ((o - mean) * lax.rsqrt(var + GN_EPS)).reshape(bsz, n_tok, RWKV_W) * p['rwkv_ln_w'] + p['rwkv_ln_b']
    bonus = jnp.sum(rh * hd(k) * p['rwkv_r_k'], -1, keepdims=True) * vh
    o = (o + bonus.reshape(bsz, n_tok, RWKV_W)) * g
    return o, jnp.stack([s_f, s_b], axis=1)


def ab_mix(h, p, s0_gla, s0_rwkv):
    u = h @ p['ab_w_in']
    o_gla, s_gla = gla_mix(u[..., :GLA_COLS], p, s0_gla)
    o_rw, s_rw = rwkv_mix(u[..., GLA_COLS:], p, s0_rwkv)
    out = jnp.concatenate([o_gla, o_rw.astype(F32)], -1).astype(h.dtype) @ p['ab_w_out']
    return out, (s_gla, s_rw)


def hyena_filter(n_tok, p):
    t = jnp.linspace(0.0, 1.0, n_tok, dtype=F32)[:, None]
    w = 2.0 * math.pi * jnp.arange(n_tok, dtype=F32)[:, None] / n_tok
    f = jnp.linspace(1e-4, HY_BANDS - 1, HY_BANDS, dtype=F32)[None, :]
    z = jnp.concatenate([t, jnp.cos(f * w), -jnp.sin(f * w)], -1)
    fr = p['hy_f_freq']
    hh = jnp.sin(fr[0] * (z @ p['hy_f_w1'] + p['hy_f_b1']))
    hh = jnp.sin(fr[1] * (hh @ p['hy_f_w2'] + p['hy_f_b2']))
    hh = jnp.sin(fr[2] * (hh @ p['hy_f_w3'] + p['hy_f_b3']))
    hh = (hh @ p['hy_f_w4']).astype(F32).reshape(n_tok, 2, HY_W)
    max_decay = math.log(HY_TARGET) / HY_FAST
    min_decay = math.log(HY_TARGET) / HY_SLOW
    deltas = jnp.abs(jnp.linspace(min_decay, max_decay, HY_W, dtype=F32))
    hh = hh * jnp.exp(-t * deltas[None, :])[:, None, :]
    return hh[:, 0], hh[:, 1]


def bidir_fftconv(u, h_fwd, h_bwd):
    n_tok = u.shape[1]
    filt = jnp.concatenate([h_fwd, jnp.zeros((1, h_fwd.shape[1]), F32), jnp.flip(h_bwd[1:], 0)], 0)
    spec = jnp.fft.rfft(u.astype(F32), n=2 * n_tok, axis=1) * jnp.fft.rfft(filt, n=2 * n_tok, axis=0)[None]
    return jnp.fft.irfft(spec, n=2 * n_tok, axis=1)[:, :n_tok]


def hyena_mix(h, p):
    n_tok = h.shape[1]
    u = h @ p['hy_w_in']
    cw = p['hy_conv_w']
    u = prev_tok(u) * cw[0] + u * cw[1] + next_tok(u) * cw[2] + p['hy_conv_b']
    x0, x1, v = jnp.split(u.astype(F32), 3, axis=-1)
    h_fwd, h_bwd = hyena_filter(n_tok, p)
    v = v * x1
    y = bidir_fftconv(v, h_fwd, h_bwd) + v * p['hy_bias']
    return (y * x0).astype(h.dtype) @ p['hy_w_out']


def swiglu(h, lp):
    return (jax.nn.silu(h @ lp['w_gate']) * (h @ lp['w_up'])) @ lp['w_down']


def sublayers(x, cond, lp, mixer):
    mod = jnp.einsum('bd,de->be', jax.nn.silu(cond), lp['w_mod']) + lp['b_mod']
    sh1, sc1, g1, sh2, sc2, g2 = jnp.split(mod[:, None, :], 6, axis=-1)
    m, aux = mixer(x * (1.0 + sc1) + sh1)
    x = layer_norm(ALPHA * x + g1 * m, lp['ln_g'][0], lp['ln_b'][0])
    f = swiglu(x * (1.0 + sc2) + sh2, lp)
    x = layer_norm(ALPHA * x + g2 * f, lp['ln_g'][1], lp['ln_b'][1])
    return x, aux


def setup_inputs(seed: int = 0) -> dict:
    key = jax.random.key(seed)
    ks = iter(jax.random.split(key, 64))

    def nrm(shape, scale):
        return scale * jax.random.normal(next(ks), shape, F32)

    def unif(shape, lo, hi):
        return jax.random.uniform(next(ks), shape, F32, lo, hi)

    D = D_MODEL
    return {
        'x_prompt': nrm((BATCH, SEQ, D), 1.0),
        'x_sample': nrm((DEC_BATCH, DEC_SEQ, D), 1.0),
        'state_gla': nrm((DEC_BATCH, N_AB, 2, GLA_HEADS, GLA_DK, GLA_DV), 1.0),
        'state_rwkv': nrm((DEC_BATCH, N_AB, 2, RWKV_HEADS, RWKV_N, RWKV_N), 1.0),
        'c': nrm((DEC_BATCH, D), 1.0),
        'c_ctx': nrm((D,), 1.0),
        'w_mod': nrm((DEPTH, D, 6 * D), 0.5 * D ** -0.5),
        'b_mod': nrm((DEPTH, 6 * D), 0.02),
        'ln_g': 1.0 + nrm((DEPTH, 2, D), 0.02),
        'ln_b': nrm((DEPTH, 2, D), 0.02),
        'ffn_w_gate': nrm((DEPTH, D, D_FF), D ** -0.5),
        'ffn_w_up': nrm((DEPTH, D, D_FF), D ** -0.5),
        'ffn_w_down': nrm((DEPTH, D_FF, D), BETA * D_FF ** -0.5),
        'ab_w_in': nrm((N_AB, D, AB_IN), D ** -0.5),
        'ab_w_out': nrm((N_AB, D, D), BETA * D ** -0.5),
        'gla_w_decay': nrm((N_AB, 2, GLA_LORA, GLA_QK), GLA_LORA ** -0.5),
        'gla_b_decay': nrm((N_AB, 2, GLA_QK), 0.1),
        'gla_norm_w': 1.0 + nrm((N_AB, GLA_DV), 0.02),
        'rwkv_shift_mu': unif((N_AB, 2, RWKV_COLS), 0.0, 0.5),
        'rwkv_w0': unif((N_AB, 2, RWKV_W), -6.0, 1.0),
        'rwkv_w2': nrm((N_AB, 2, DECAY_LORA, RWKV_W), 0.5 * DECAY_LORA ** -0.5),
        'rwkv_a0': nrm((N_AB, 2, RWKV_W), 0.1),
        'rwkv_a2': nrm((N_AB, 2, A_LORA, RWKV_W), 0.5 * A_LORA ** -0.5),
        'rwkv_g2': nrm((N_AB, G_LORA, RWKV_W), G_LORA ** -0.5),
        'rwkv_k_k': 0.85 + nrm((N_AB, RWKV_W), 0.05),
        'rwkv_k_a': 1.0 + nrm((N_AB, RWKV_W), 0.05),
        'rwkv_r_k': nrm((N_AB, RWKV_HEADS, RWKV_N), 0.1),
        'rwkv_ln_w': 1.0 + nrm((N_AB, RWKV_W), 0.02),
        'rwkv_ln_b': nrm((N_AB, RWKV_W), 0.02),
        'hy_w_in': nrm((N_C, D, 3 * HY_W), D ** -0.5),
        'hy_conv_w': nrm((N_C, 3, 3 * HY_W), 3 ** -0.5),
        'hy_conv_b': nrm((N_C, 3 * HY_W), 0.02),
        'hy_f_w1': nrm((N_C, HY_EMB, HY_FO), HY_EMB ** -0.5),
        'hy_f_b1': nrm((N_C, HY_FO), 0.1),
        'hy_f_w2': nrm((N_C, HY_FO, HY_FO), HY_FO ** -0.5),
        'hy_f_b2': nrm((N_C, HY_FO), 0.1),
        'hy_f_w3': nrm((N_C, HY_FO, HY_FO), HY_FO ** -0.5),
        'hy_f_b3': nrm((N_C, HY_FO), 0.1),
        'hy_f_freq': 1.0 + nrm((N_C, 3, HY_FO), 0.1),
        'hy_f_w4': nrm((N_C, HY_FO, 2 * HY_W), 0.1 * HY_FO ** -0.5),
        'hy_bias': nrm((N_C, HY_W), 0.5),
        'hy_w_out': nrm((N_C, D, D), BETA * D ** -0.5),
    }


def reference(x_prompt, x_sample, state_gla, state_rwkv, c, c_ctx, w_mod, b_mod, ln_g, ln_b,
              ffn_w_gate, ffn_w_up, ffn_w_down, ab_w_in, ab_w_out, gla_w_decay, gla_b_decay, gla_norm_w,
              rwkv_shift_mu, rwkv_w0, rwkv_w2, rwkv_a0, rwkv_a2, rwkv_g2, rwkv_k_k, rwkv_k_a, rwkv_r_k,
              rwkv_ln_w, rwkv_ln_b, hy_w_in, hy_conv_w, hy_conv_b, hy_f_w1, hy_f_b1, hy_f_w2, hy_f_b2,
              hy_f_w3, hy_f_b3, hy_f_freq, hy_f_w4, hy_bias, hy_w_out):
    x_ctx = x_prompt
    x_lat = x_sample + grid_pos_embed(x_sample.shape[1]).astype(x_sample.dtype)
    n_ctx = x_prompt.shape[0]
    cond_ctx = c_ctx[None, :]
    gla_new = []
    rwkv_new = []
    for layer in range(DEPTH):
        lp = {'w_mod': w_mod[layer], 'b_mod': b_mod[layer], 'ln_g': ln_g[layer], 'ln_b': ln_b[layer],
              'w_gate': ffn_w_gate[layer], 'w_up': ffn_w_up[layer], 'w_down': ffn_w_down[layer]}
        i = layer // 2
        if layer % 2 == 0:
            ap = {'ab_w_in': ab_w_in[i], 'ab_w_out': ab_w_out[i], 'gla_w_decay': gla_w_decay[i],
                  'gla_b_decay': gla_b_decay[i], 'gla_norm_w': gla_norm_w[i], 'rwkv_shift_mu': rwkv_shift_mu[i],
                  'rwkv_w0': rwkv_w0[i], 'rwkv_w2': rwkv_w2[i], 'rwkv_a0': rwkv_a0[i], 'rwkv_a2': rwkv_a2[i],
                  'rwkv_g2': rwkv_g2[i], 'rwkv_k_k': rwkv_k_k[i], 'rwkv_k_a': rwkv_k_a[i], 'rwkv_r_k': rwkv_r_k[i],
                  'rwkv_ln_w': rwkv_ln_w[i], 'rwkv_ln_b': rwkv_ln_b[i]}
            zero_gla = jnp.zeros((n_ctx, 2, GLA_HEADS, GLA_DK, GLA_DV), F32)
            zero_rwkv = jnp.zeros((n_ctx, 2, RWKV_HEADS, RWKV_N, RWKV_N), F32)
            x_ctx, (s_gla, s_rwkv) = sublayers(x_ctx, cond_ctx, lp,
                                               lambda h: ab_mix(h, ap, zero_gla, zero_rwkv))
            x_lat, _ = sublayers(x_lat, c, lp,
                                 lambda h: ab_mix(h, ap, state_gla[:, i], state_rwkv[:, i]))
            gla_new.append(s_gla)
            rwkv_new.append(s_rwkv)
        else:
            hp = {'hy_w_in': hy_w_in[i], 'hy_conv_w': hy_conv_w[i], 'hy_conv_b': hy_conv_b[i],
                  'hy_f_w1': hy_f_w1[i], 'hy_f_b1': hy_f_b1[i], 'hy_f_w2': hy_f_w2[i], 'hy_f_b2': hy_f_b2[i],
                  'hy_f_w3': hy_f_w3[i], 'hy_f_b3': hy_f_b3[i], 'hy_f_freq': hy_f_freq[i], 'hy_f_w4': hy_f_w4[i],
                  'hy_bias': hy_bias[i], 'hy_w_out': hy_w_out[i]}
            x_ctx, _ = sublayers(x_ctx, cond_ctx, lp, lambda h: (hyena_mix(h, hp), None))
            x_lat, _ = sublayers(x_lat, c, lp, lambda h: (hyena_mix(h, hp), None))
    new_state_gla = jnp.stack(gla_new, axis=1)
    new_state_rwkv = jnp.stack(rwkv_new, axis=1)
    return (x_ctx, x_lat, new_state_gla, new_state_rwkv)
```

```python
import math
from contextlib import ExitStack
import numpy as np
import ml_dtypes
import concourse.bass as bass
import concourse.mybir as mybir
from concourse.bass_utils import run_bass_kernel_spmd

F32 = mybir.dt.float32
BF16 = mybir.dt.bfloat16
AF = mybir.ActivationFunctionType
ALU = mybir.AluOpType
NPBF = ml_dtypes.bfloat16

D = 1024
T = 2048
DEPTH = 4
DFF = 2816
NM = DFF // 128
ALPHA = (2 * DEPTH) ** 0.25
LN_EPS = 1e-5
NT = T // 512
SEG = 256
NSEG = T // SEG


class Prog:
    ENGS = ("pe", "act", "dve", "pool", "sp")
    NDS = 12

    def __init__(self, nc, same_eng_sync=False):
        self.nc = nc
        self.ops = []
        self.per_eng = {e: [] for e in self.ENGS}
        self.lastw = {}
        self.rd_c = {}
        self.rd_d = {}
        self.dma_rr = {e: 0 for e in self.ENGS}
        self.dma_last = {}
        self.same_eng_sync = same_eng_sync
        self.out_dmas = []
        self.fence_id = None
        self.ns = None

    def op(self, eng, fn, reads=(), writes=(), dma=False, is_out=False):
        i = len(self.ops)
        if self.ns is not None:
            reads = [(self.ns, r) for r in reads]
            writes = [(self.ns, w) for w in writes]
        deps = set()
        for r in reads:
            w = self.lastw.get(r)
            if w is not None:
                deps.add(w)
        for w in writes:
            lw = self.lastw.get(w)
            if lw is not None:
                deps.add(lw)
            deps.update(self.rd_c.get(w, {}).values())
            deps.update(self.rd_d.get(w, ()))
        if self.fence_id is not None:
            deps.add(self.fence_id)
        semslot = None
        if dma:
            k = self.dma_rr[eng]
            self.dma_rr[eng] = (k + 1) % self.NDS
            prev = self.dma_last.get((eng, k))
            if prev is not None:
                deps.add(prev)
            self.dma_last[(eng, k)] = i
            semslot = (eng, k)
        fdeps = set()
        for d in deps:
            od = self.ops[d]
            if od["eng"] == eng and not od["dma"] and not dma and not self.same_eng_sync:
                continue
            if od["eng"] == eng and not od["dma"] and eng == "pe":
                continue
            fdeps.add(d)
            od["flag"] = True
        o = dict(id=i, eng=eng, fn=fn, deps=fdeps, dma=dma, flag=False, semslot=semslot)
        for r in reads:
            if dma:
                self.rd_d.setdefault(r, set()).add(i)
            else:
                self.rd_c.setdefault(r, {})[eng] = i
        for w in writes:
            self.lastw[w] = i
            self.rd_c[w] = {}
            self.rd_d[w] = set()
        self.ops.append(o)
        self.per_eng[eng].append(i)
        if is_out:
            self.out_dmas.append(i)
        return i

    def emit(self):
        nc = self.nc
        ops = self.ops
        fin = dict(id=len(ops), eng="sp", fn=None, deps=set(self.out_dmas), dma=False, flag=False, semslot=None)
        ops.append(fin)
        self.per_eng["sp"].append(fin["id"])
        with ExitStack() as es:
            esem = {e: es.enter_context(nc.semaphore("s_" + e)) for e in ("pe", "act", "dve", "pool")}
            dsem = {}
            for q in ("sp", "pool", "act"):
                for k in range(self.NDS):
                    if (q, k) in self.dma_last:
                        dsem[(q, k)] = es.enter_context(nc.semaphore("d_%s%d" % (q, k)))
            cnt = {e: 0 for e in esem}
            dcnt = {k: 0 for k in dsem}
            semof = {}
            for o in ops:
                if o["dma"]:
                    dcnt[o["semslot"]] += 16
                    semof[o["id"]] = (dsem[o["semslot"]], dcnt[o["semslot"]])
                elif o["flag"]:
                    cnt[o["eng"]] += 1
                    semof[o["id"]] = (esem[o["eng"]], cnt[o["eng"]])
            block = es.enter_context(nc.Block())
            names = dict(pe="tensor", act="scalar", dve="vector", pool="gpsimd", sp="sync")
            for eng in self.ENGS:
                lst = self.per_eng[eng]
                if not lst:
                    continue

                def body(e, lst=lst):
                    waited = {}
                    for i in lst:
                        o = ops[i]
                        need = {}
                        for d in o["deps"]:
                            s, v = semof[d]
                            if need.get(id(s), (None, 0))[1] < v:
                                need[id(s)] = (s, v)
                        for s, v in need.values():
                            if waited.get(id(s), 0) >= v:
                                continue
                            e.wait_ge(s, v)
                            waited[id(s)] = v
                        if o["fn"] is None:
                            continue
                        ins = o["fn"](e)
                        if o["dma"]:
                            ins.then_inc(semof[i][0], 16)
                        elif o["flag"]:
                            ins.then_inc(semof[i][0], 1)

                getattr(block, names[eng])(body)


def drain(*gens):
    alive = list(gens)
    while alive:
        for g in list(alive):
            try:
                next(g)
            except StopIteration:
                alive.remove(g)


def _pk(w, kparts=128):
    K, N = w.shape
    return np.ascontiguousarray(w.reshape(K // kparts, kparts, N).transpose(1, 0, 2))


def _col(v):
    return np.ascontiguousarray(v.reshape(-1, 128).T)


class Builder:
    def __init__(self, cfg):
        self.cfg = cfg
        self.nc = bass.Bass("TRN2", target_bir_lowering=False)
        self.P = Prog(self.nc, same_eng_sync=cfg.get("same_eng_sync", False))
        self.es = ExitStack()
        self.din = {}
        self.dout = {}
        self.uid = 0

    def inp(self, name, shape, dt=F32):
        self.din[name] = self.nc.dram_tensor(name, list(shape), dt, kind="ExternalInput").ap()
        return self.din[name]

    def outp(self, name, shape, dt=F32):
        self.dout[name] = self.nc.dram_tensor(name, list(shape), dt, kind="ExternalOutput").ap()
        return self.dout[name]

    def scratch(self, name, shape, dt=F32):
        return self.nc.dram_tensor(name, list(shape), dt).ap()

    def sb(self, name, shape, dt=F32):
        self.uid += 1
        return self.es.enter_context(self.nc.sbuf_tensor("%s_%d" % (name, self.uid), list(shape), dt))

    def ps(self, name, shape, dt=F32):
        self.uid += 1
        return self.es.enter_context(self.nc.psum_tensor("%s_%d" % (name, self.uid), list(shape), dt))

    def dma(self, q, out, in_, reads=(), writes=(), is_out=False):
        return self.P.op(q, lambda e, out=out, in_=in_: e.dma_start(out=out, in_=in_), reads, writes, dma=True, is_out=is_out)

    def mm(self, out, lhsT, rhs, start, stop, reads=(), writes=()):
        return self.P.op("pe", lambda e, out=out, lhsT=lhsT, rhs=rhs, start=start, stop=stop:
                         e.matmul(out, lhsT, rhs, start=start, stop=stop), reads, writes)

    def tr(self, out, in_, ident, reads=(), writes=()):
        return self.P.op("pe", lambda e, out=out, in_=in_, ident=ident: e.transpose(out, in_, ident), reads, writes)

    def act(self, out, in_, func, bias=0.0, scale=1.0, reads=(), writes=(), eng="act"):
        return self.P.op(eng, lambda e, out=out, in_=in_, func=func, bias=bias, scale=scale:
                         e.activation(out=out, in_=in_, func=func, bias=bias, scale=scale), reads, writes)

    def ts(self, eng, out, in0, s1, s2, op0, op1=None, reads=(), writes=()):
        if op1 is None:
            return self.P.op(eng, lambda e, out=out, in0=in0, s1=s1, op0=op0:
                             e.tensor_scalar(out=out, in0=in0, scalar1=s1, scalar2=None, op0=op0), reads, writes)
        return self.P.op(eng, lambda e, out=out, in0=in0, s1=s1, s2=s2, op0=op0, op1=op1:
                         e.tensor_scalar(out=out, in0=in0, scalar1=s1, scalar2=s2, op0=op0, op1=op1), reads, writes)

    def tt(self, eng, out, in0, in1, op, reads=(), writes=()):
        return self.P.op(eng, lambda e, out=out, in0=in0, in1=in1, op=op:
                         e.tensor_tensor(out=out, in0=in0, in1=in1, op=op), reads, writes)

    def stt(self, eng, out, in0, scalar, in1, op0, op1, reads=(), writes=()):
        return self.P.op(eng, lambda e, out=out, in0=in0, scalar=scalar, in1=in1, op0=op0, op1=op1:
                         e.scalar_tensor_tensor(out=out, in0=in0, scalar=scalar, in1=in1, op0=op0, op1=op1), reads, writes)

    def cp(self, eng, out, in_, reads=(), writes=()):
        if eng == "act":
            return self.P.op(eng, lambda e, out=out, in_=in_: e.copy(out=out, in_=in_), reads, writes)
        return self.P.op(eng, lambda e, out=out, in_=in_: e.tensor_copy(out=out, in_=in_), reads, writes)

    def recip(self, out, in_, reads=(), writes=()):
        return self.P.op("dve", lambda e, out=out, in_=in_: e.reciprocal(out=out, in_=in_), reads, writes)

    def memset(self, eng, ap, val, writes=()):
        return self.P.op(eng, lambda e, ap=ap, val=val: e.memset(ap, val), (), writes)

    def fence(self):
        P = self.P
        deps_tokens = list(P.lastw.keys())
        i = P.op("dve", lambda e, ap=self.dummy[:, 0:1]: e.memset(ap, 0.0), reads=(), writes=tuple(deps_tokens))
        P.lastw = {}
        P.rd_c = {}
        P.rd_d = {}
        P.fence_id = i

    def R(self, *toks):
        return tuple(toks)


def build(cfg):
    B = Builder(cfg)
    nc, P = B.nc, B.P
    mixers = cfg.get("mixers", True)
    nlayers = cfg.get("nlayers", DEPTH)

    xT = B.inp("xT", [D, T])
    posT = B.inp("posT", [D, T])
    condc = B.inp("condc", [128, 8])
    flags = B.inp("flags", [128, 4])
    wmod = B.inp("wmod", [DEPTH, 12, 128, 8, 512])
    bmod = B.inp("bmod", [DEPTH, 128, 48])
    lng = B.inp("lng", [128, DEPTH * 2 * 8])
    lnb = B.inp("lnb", [128, DEPTH * 2 * 8])
    wg = B.inp("wg", [DEPTH, 11, 128, 8, 256])
    wu = B.inp("wu", [DEPTH, 11, 128, 8, 256])
    wd = B.inp("wd", [DEPTH, 8, 128, NM, 128])
    yT = B.outp("yT", [D, T])
    B.io = dict(xT=xT, posT=posT, condc=condc, flags=flags)

    X = B.sb("X", [128, 8, T], F32)
    HB = B.sb("HB", [128, 8, T], BF16)
    modc = B.sb("modc", [128, DEPTH * 48], F32)
    lngs = B.sb("lngs", [128, DEPTH * 16], F32)
    lnbs = B.sb("lnbs", [128, DEPTH * 16], F32)
    flg = B.sb("flg", [128, 4], F32)
    ones_bf = B.sb("ones_bf", [128, 128], BF16)
    B.dummy = B.sb("fdummy", [128, 2], F32)
    B.eps_ln = B.sb("eps_ln", [128, 3], F32)
    B.memset("dve", B.eps_ln[:, 0:1], LN_EPS, writes=("epsln",))
    B.memset("dve", B.eps_ln[:, 1:2], 1e-24, writes=("epsln",))
    B.memset("dve", B.eps_ln[:, 2:3], 64e-5, writes=("epsln",))
    B.X, B.HB, B.modc, B.flg, B.ones_bf = X, HB, modc, flg, ones_bf
    B.lngs, B.lnbs = lngs, lnbs

    def xtok(k, tt):
        return "X%d:%d" % (k, tt)

    def htok(k, tt):
        return "H%d:%d" % (k, tt)
    B.xtok, B.htok = xtok, htok

    B.dma("sp", lngs[:], lng, writes=("lng",))
    B.dma("sp", lnbs[:], lnb, writes=("lnb",))
    B.dma("sp", flg[:], flags, writes=("flg",))
    B.memset("dve", ones_bf[:], 1.0 / 1024.0, writes=("ones",))
    xv = xT.rearrange("(k p) t -> p k t", p=128)
    pv = posT.rearrange("(k p) t -> p k t", p=128)
    with ExitStack() as ph:
        B.es, old_es = ph, B.es
        pst = [B.sb("pst%d" % i, [128, T], F32) for i in range(2)]
        cnd = B.sb("cnd", [128, 8], F32)
        scb = B.sb("scb", [128, 8], BF16)
        bms = B.sb("bms", [128, DEPTH * 48], F32)
        wms = [B.sb("wms%d" % i, [128, 8, 512], BF16) for i in range(2)]
        psM = B.ps("psM", [128, 512], F32)
        for k in range(8):
            B.dma("sp", X[:, k, :], xv[:, k, :], writes=[xtok(k, tt) for tt in range(NT)])
            B.dma("sp", pst[k % 2][:], pv[:, k, :], writes=("pst%d" % (k % 2),))
            B.tt("dve", X[:, k, :], X[:, k, :], pst[k % 2][:], ALU.add,
                 reads=["pst%d" % (k % 2)] + [xtok(k, tt) for tt in range(NT)], writes=[xtok(k, tt) for tt in range(NT)])
        B.dma("sp", cnd[:], condc, writes=("cnd",))
        for l in range(DEPTH):
            B.dma("sp", bms[:, l * 48:(l + 1) * 48], bmod[l], writes=("bms%d" % l,))
        B.act(scb[:], cnd[:], AF.Silu, reads=("cnd",), writes=("scb",))
        n = 0
        for l in range(DEPTH):
            for pn in range(12):
                s = n % 2
                n += 1
                B.dma("pool", wms[s][:], wmod[l, pn], writes=("wms%d" % s,))
                for jj in range(4):
                    col = l * 48 + pn * 4 + jj
                    for k in range(8):
                        B.mm(psM[:, col:col + 1], wms[s][:, k, jj * 128:(jj + 1) * 128], scb[:, k:k + 1], k == 0, k == 7,
                             reads=("wms%d" % s, "scb"), writes=("psM",))
        B.tt("dve", modc[:], psM[:, 0:DEPTH * 48], bms[:], ALU.add,
             reads=["psM"] + ["bms%d" % l for l in range(DEPTH)], writes=("modc",))
        for l in range(DEPTH):
            for g in (1, 4):
                c0 = l * 48 + g * 8
                B.ts("dve", modc[:, c0:c0 + 8], modc[:, c0:c0 + 8], 1.0, None, ALU.add, reads=("modc",), writes=("modc",))
        B.fence()
    B.es = old_es
    make_ident(B)
    if mixers and nlayers > 1:
        hyena_filters(B)
    for k in range(8):
        B.ts("dve", HB[:, k, :], X[:, k, :], modc[:, 8 + k:9 + k], modc[:, k:k + 1], ALU.mult, ALU.add,
             reads=["modc"] + [xtok(k, tt) for tt in range(NT)], writes=[htok(k, tt) for tt in range(NT)])

    def ln_tile(lnps, l, which, tt, nxt):
        for _ in ln_tile_gen(lnps, l, which, tt, nxt):
            pass

    def ln_tile_gen(lnps, l, which, tt, nxt):
        ybf, ysq, mean, msq, rstd, nmr, tmp, ps1, ps2 = lnps
        tsl = slice(tt * 512, (tt + 1) * 512)
        for k in range(8):
            B.cp("act", ybf[:, k, :], X[:, k, tsl], reads=B.R(xtok(k, tt)), writes=("ybf%d" % k,))
            B.act(ysq[:, k, :], X[:, k, tsl], AF.Square, reads=B.R(xtok(k, tt)), writes=("ysq%d" % k,))
            if k % 2 == 1:
                yield
        for k in range(8):
            B.mm(ps1[:], ones_bf[:], ybf[:, k, :], k == 0, k == 7, reads=B.R("ones", "ybf%d" % k), writes=("lnps1",))
        for k in range(8):
            B.mm(ps2[:], ones_bf[:], ysq[:, k, :], k == 0, k == 7, reads=B.R("ones", "ysq%d" % k), writes=("lnps2",))
        yield
        B.cp("act", mean[:], ps1[:], reads=B.R("lnps1"), writes=("mean",))
        B.tt("dve", msq[:], mean[:], mean[:], ALU.mult, reads=B.R("mean"), writes=("msq",))
        B.tt("dve", rstd[:], ps2[:], msq[:], ALU.subtract, reads=B.R("lnps2", "msq"), writes=("rstd",))
        B.act(rstd[:], rstd[:], AF.Ln, bias=B.eps_ln[:, 0:1], reads=B.R("rstd"), writes=("rstd",))
        B.act(rstd[:], rstd[:], AF.Exp, scale=-0.5, reads=B.R("rstd"), writes=("rstd",))
        B.stt("dve", nmr[:], mean[:], -1.0, rstd[:], ALU.mult, ALU.mult, reads=B.R("mean", "rstd"), writes=("nmr",))
        yield
        for k in range(8):
            tk = "lntmp%d" % (k % 2)
            tm = tmp[k % 2]
            B.tt("dve", tm[:], X[:, k, tsl], rstd[:], ALU.mult, reads=B.R(xtok(k, tt), "rstd"), writes=(tk,))
            B.tt("dve", tm[:], tm[:], nmr[:], ALU.add, reads=B.R(tk, "nmr"), writes=(tk,))
            c = l * 16 + which * 8 + k
            B.act(X[:, k, tsl], tm[:], AF.Identity, bias=lnbs[:, c:c + 1], scale=lngs[:, c:c + 1],
                  reads=B.R(tk, "lng", "lnb"), writes=(xtok(k, tt),))
            if nxt is not None:
                nl, g = nxt
                c1 = nl * 48 + (g + 1) * 8 + k
                c0 = nl * 48 + g * 8 + k
                B.ts("dve", HB[:, k, tsl], X[:, k, tsl], modc[:, c1:c1 + 1], modc[:, c0:c0 + 1], ALU.mult, ALU.add,
                     reads=B.R(xtok(k, tt), "modc"), writes=(htok(k, tt),))
            yield

    def alloc_ln():
        ybf = B.sb("ybf", [128, 8, 512], BF16)
        ysq = B.sb("ysq", [128, 8, 512], BF16)
        mean = B.sb("mean", [128, 512], F32)
        msq = B.sb("msq", [128, 512], F32)
        rstd = B.sb("rstd", [128, 512], F32)
        nmr = B.sb("nmr", [128, 512], F32)
        tmp = [B.sb("lntmp%d" % i, [128, 512], F32) for i in range(2)]
        ps1 = B.ps("lnps1", [128, 512], F32)
        ps2 = B.ps("lnps2", [128, 512], F32)
        return (ybf, ysq, mean, msq, rstd, nmr, tmp, ps1, ps2)
    B.ln_tile, B.alloc_ln, B.ln_tile_gen = ln_tile, alloc_ln, ln_tile_gen

    def ffn(l):
        with ExitStack() as ph:
            B.es, old = ph, B.es
            lnps = alloc_ln()
            A = B.sb("A", [128, NM, 1024], BF16)
            wgs = [B.sb("wgs%d" % i, [128, 8, 256], BF16) for i in range(2)]
            wus = [B.sb("wus%d" % i, [128, 8, 256], BF16) for i in range(2)]
            wds = [B.sb("wds%d" % i, [128, NM, 128], BF16) for i in range(2)]
            sg = [B.sb("sg%d" % i, [128, 512], F32) for i in range(2)]
            dtm = [B.sb("dtm%d" % i, [128, 512], F32) for i in range(2)]
            psG = [B.ps("psG%d" % i, [128, 512], F32) for i in range(2)]
            psU = [B.ps("psU%d" % i, [128, 512], F32) for i in range(2)]
            psD = [B.ps("psD%d" % i, [128, 512], F32) for i in range(2)]
            st = dict(wn=0, pn=0, dn=0)

            def G(half):
                tiles = (2 * half, 2 * half + 1)
                for pn in range(11):
                    s = st["wn"] % 2
                    st["wn"] += 1
                    B.dma("pool", wgs[s][:], wg[l, pn], reads=B.R(), writes=("wgs%d" % s,))
                    B.dma("pool", wus[s][:], wu[l, pn], reads=B.R(), writes=("wus%d" % s,))
                    for mi in range(2):
                        m = pn * 2 + mi
                        for ti, tt in enumerate(tiles):
                            tsl = slice(tt * 512, (tt + 1) * 512)
                            q = st["pn"] % 2
                            st["pn"] += 1
                            for k in range(8):
                                B.mm(psG[q][:], wgs[s][:, k, mi * 128:(mi + 1) * 128], HB[:, k, tsl], k == 0, k == 7,
                                     reads=B.R("wgs%d" % s, htok(k, tt)), writes=("psG%d" % q,))
                            for k in range(8):
                                B.mm(psU[q][:], wus[s][:, k, mi * 128:(mi + 1) * 128], HB[:, k, tsl], k == 0, k == 7,
                                     reads=B.R("wus%d" % s, htok(k, tt)), writes=("psU%d" % q,))
                            B.act(sg[q][:], psG[q][:], AF.Silu, reads=B.R("psG%d" % q), writes=("sg%d" % q,))
                            B.tt("dve", A[:, m, ti * 512:(ti + 1) * 512], sg[q][:], psU[q][:], ALU.mult,
                                 reads=B.R("sg%d" % q, "psU%d" % q), writes=("A%d:%d" % (m, ti),))
                            yield

            def Dn(half):
                tiles = (2 * half, 2 * half + 1)
                for dch in range(8):
                    s = st["dn"] % 2
                    st["dn"] += 1
                    B.dma("pool", wds[s][:], wd[l, dch], reads=B.R(), writes=("wds%d" % s,))
                    for ti, tt in enumerate(tiles):
                        tsl = slice(tt * 512, (tt + 1) * 512)
                        q = st["pn"] % 2
                        st["pn"] += 1
                        for m in range(NM):
                            B.mm(psD[q][:], wds[s][:, m, :], A[:, m, ti * 512:(ti + 1) * 512], m == 0, m == NM - 1,
                                 reads=B.R("wds%d" % s, "A%d:%d" % (m, ti)), writes=("psD%d" % q,))
                        c = l * 48 + 5 * 8 + dch
                        B.act(dtm[q][:], psD[q][:], AF.Identity, scale=modc[:, c:c + 1], reads=B.R("psD%d" % q, "modc"), writes=("dtm%d" % q,))
                        B.stt("dve", X[:, dch, tsl], X[:, dch, tsl], ALPHA, dtm[q][:], ALU.mult, ALU.add,
                              reads=B.R(xtok(dch, tt), "dtm%d" % q), writes=(xtok(dch, tt),))
                        yield

            def LNh(half):
                for tt in (2 * half, 2 * half + 1):
                    yield from ln_tile_gen(lnps, l, 1, tt, (l + 1, 0) if l + 1 < DEPTH else None)

            drain(G(0))
            drain(Dn(0))
            drain(G(1), LNh(0))
            drain(Dn(1))
            drain(LNh(1))
            B.fence()
        B.es = old
    B.ffn = ffn

    def null_mixer(l):
        with ExitStack() as ph:
            B.es, old = ph, B.es
            lnps = alloc_ln()
            for tt in range(NT):
                tsl = slice(tt * 512, (tt + 1) * 512)
                for k in range(8):
                    B.ts("dve", X[:, k, tsl], X[:, k, tsl], ALPHA, None, ALU.mult, reads=B.R(xtok(k, tt)), writes=(xtok(k, tt),))
                ln_tile(lnps, l, 0, tt, (l, 3))
            B.fence()
        B.es = old

    for l in range(nlayers):
        if mixers and l % 2 == 0 and "ab_mixer" in globals():
            ab_mixer(B, l)
        elif mixers and l % 2 == 1 and "hyena_mixer" in globals():
            hyena_mixer(B, l)
        else:
            null_mixer(l)
        ffn(l)

    yv = yT.rearrange("(k p) t -> p k t", p=128)
    for k in range(8):
        B.dma("sp", yv[:, k, :], X[:, k, :], reads=B.R(*[xtok(k, tt) for tt in range(NT)]), is_out=True)
    P.emit()
    return B


_CACHE = {}


def _grid_pos_T():
    quarter = D // 4
    omega = (1.0 / (10000.0 ** (np.arange(quarter, dtype=np.float32) / np.float32(quarter)))).astype(np.float32)
    idx = np.arange(T)

    def sincos(pos):
        ang = pos.astype(np.float32)[:, None] * omega[None, :]
        return np.concatenate([np.sin(ang), np.cos(ang)], -1)
    pe = np.concatenate([sincos(idx // 64), sincos(idx % 64)], -1).astype(np.float32)
    return np.ascontiguousarray(pe.T)


def prep_shared(inp, cfg):
    f = lambda a: np.ascontiguousarray(np.asarray(a, dtype=np.float32))
    sh = {}
    sh["wmod"] = f(inp["w_mod"].reshape(DEPTH, 8, 128, 12, 512).transpose(0, 3, 2, 1, 4))
    sh["bmod"] = f(inp["b_mod"].reshape(DEPTH, 48, 128).transpose(0, 2, 1))
    sh["lng"] = f(inp["ln_g"].reshape(DEPTH, 2, 8, 128).transpose(3, 0, 1, 2).reshape(128, DEPTH * 16))
    sh["lnb"] = f(inp["ln_b"].reshape(DEPTH, 2, 8, 128).transpose(3, 0, 1, 2).reshape(128, DEPTH * 16))
    sh["wg"] = f(inp["ffn_w_gate"].reshape(DEPTH, 8, 128, 11, 256).transpose(0, 3, 2, 1, 4))
    sh["wu"] = f(inp["ffn_w_up"].reshape(DEPTH, 8, 128, 11, 256).transpose(0, 3, 2, 1, 4))
    sh["wd"] = f(inp["ffn_w_down"].reshape(DEPTH, NM, 128, 8, 128).transpose(0, 3, 2, 1, 4))
    wi = inp["ab_w_in"]
    abwin = np.zeros((2, NG, 128, 8, 128), np.float32)
    abmu = np.zeros((2, 128, NG * 2), np.float32)
    for n, (nm, c0, wdt) in enumerate(AB_GROUPS):
        abwin[:, n, :, :, 0:wdt] = wi[:, :, c0:c0 + wdt].reshape(2, 8, 128, wdt).transpose(0, 2, 1, 3)
        if nm[0] == "r":
            r0 = c0 - 1568
            abmu[:, 0:wdt, 2 * n] = inp["rwkv_shift_mu"][:, 0, r0:r0 + wdt]
            abmu[:, 0:wdt, 2 * n + 1] = inp["rwkv_shift_mu"][:, 1, r0:r0 + wdt]
    sh["abwin"] = abwin
    sh["abmu"] = abmu
    wo = inp["ab_w_out"]
    sh["abwo_g"] = f(wo[:, 0:512, :].reshape(2, 4, 128, 8, 128).transpose(0, 3, 2, 1, 4))
    sh["abwo_r"] = f(wo[:, 512:1024, :].reshape(2, 8, 64, 8, 128).transpose(0, 3, 2, 1, 4))
    w2bd = np.zeros((2, 128, 1024), np.float32)
    w2bd[:, 0:64, 0:512] = inp["rwkv_w2"][:, 0]
    w2bd[:, 64:128, 512:1024] = inp["rwkv_w2"][:, 1]
    sh["w2bd"] = w2bd
    sh["w0row"] = f(inp["rwkv_w0"].reshape(2, 1, 1024))
    sh["a2s"] = f(inp["rwkv_a2"].reshape(2, 128, 512))
    sh["a0c"] = f(inp["rwkv_a0"].reshape(2, 2, 8, 64).transpose(0, 3, 1, 2).reshape(2, 64, 16))
    sh["g2a"] = f(inp["rwkv_g2"][:, 0:128])
    sh["g2b"] = f(inp["rwkv_g2"][:, 128:160])
    wdbd = np.zeros((2, 32, 512), np.float32)
    wdbd[:, 0:16, 0:256] = inp["gla_w_decay"][:, 0]
    wdbd[:, 16:32, 256:512] = inp["gla_w_decay"][:, 1]
    sh["wdbd"] = wdbd
    sh["gbrow"] = f(inp["gla_b_decay"].reshape(2, 1, 512))
    rcols = np.zeros((2, 64, 40), np.float32)
    rcols[:, :, 0:8] = inp["rwkv_k_k"].reshape(2, 8, 64).transpose(0, 2, 1)
    rcols[:, :, 8:16] = inp["rwkv_k_a"].reshape(2, 8, 64).transpose(0, 2, 1)
    rcols[:, :, 16:24] = inp["rwkv_r_k"].transpose(0, 2, 1)
    rcols[:, :, 24:32] = inp["rwkv_ln_w"].reshape(2, 8, 64).transpose(0, 2, 1)
    rcols[:, :, 32:40] = inp["rwkv_ln_b"].reshape(2, 8, 64).transpose(0, 2, 1)
    sh["rcols"] = rcols
    sh["gnw"] = f(inp["gla_norm_w"].reshape(2, 128, 1))
    ii = np.arange(128)
    same = (ii[:, None] // 64) == (ii[None, :] // 64)
    S_, T_ = ii[:, None], ii[None, :]
    sh["trimats"] = np.stack([(same & (S_ <= T_)), (same & (S_ < T_)), (same & (S_ > T_)), (same & (S_ >= T_))]).astype(np.float32)
    jj = np.arange(64)
    Rw, Cl = jj[:, None], jj[None, :]
    mk = np.stack([(Rw < Cl), (Rw <= Cl), (Rw > Cl), (Rw >= Cl)]).astype(np.float32)
    sh["cmasks"] = np.ascontiguousarray(np.tile(mk, (1, 1, 8))).astype(NPBF)
    sh["identrep"] = np.ascontiguousarray(np.tile(np.eye(64, dtype=np.float32), (1, 8))).astype(NPBF)
    lv = np.stack([((Rw // (2 << k)) == (Cl // (2 << k))) & ((Rw // (1 << k)) != (Cl // (1 << k))) for k in range(6)], 1)
    sh["lvlmask"] = np.ascontiguousarray(lv.astype(np.float32)).astype(NPBF)
    sh["hwin"] = f(inp["hy_w_in"].reshape(2, 8, 128, 24, 128).transpose(0, 3, 2, 1, 4))
    sh["hcw"] = f(inp["hy_conv_w"].reshape(2, 3, 24, 128).transpose(0, 3, 2, 1).reshape(2, 128, 72))
    sh["hcb"] = f(inp["hy_conv_b"].reshape(2, 24, 128).transpose(0, 2, 1))
    sh["hwout"] = f(inp["hy_w_out"].reshape(2, 8, 128, 8, 128).transpose(0, 3, 2, 1, 4))
    sh["hfw1"] = f(inp["hy_f_w1"])
    sh["hfw2"] = f(inp["hy_f_w2"])
    sh["hfw3"] = f(inp["hy_f_w3"])
    sh["hfw4"] = f(inp["hy_f_w4"])
    sh["hffq"] = f(inp["hy_f_freq"].transpose(0, 2, 1))
    sh["hffb"] = f(np.stack([inp["hy_f_b1"], inp["hy_f_b2"], inp["hy_f_b3"]], -1))
    sh["hbias"] = f(np.broadcast_to(inp["hy_bias"][:, None, :], (2, 128, D)))
    return sh


def _hy_consts(kind):
    key = "hyc" + kind
    if key in _CACHE:
        return _CACHE[key]
    L = T if kind == "s" else SEG
    nseg = T // L
    N = 2 * L
    tl = np.linspace(0.0, 1.0, L, dtype=np.float32)
    w = (2.0 * np.pi * np.arange(L, dtype=np.float32) / np.float32(L)).astype(np.float32)
    f = np.linspace(1e-4, 15.0, 16, dtype=np.float32)[None, :]
    z = np.concatenate([tl[:, None], np.cos(f * w[:, None]), -np.sin(f * w[:, None])], -1).astype(np.float32)
    z = np.tile(z, (nseg, 1))
    tfull = np.tile(tl, nseg)
    first = (np.arange(T) % L == 0)
    c = {}
    c["ztab"] = np.ascontiguousarray(z.T)
    c["tcol"] = np.ascontiguousarray(tfull.reshape(16, 128).T)
    c["m0col"] = np.ascontiguousarray((~first).astype(np.float32).reshape(16, 128).T)
    c["d0col"] = np.ascontiguousarray(first.astype(np.float32).reshape(16, 128).T)
    mx = math.log(1e-2) / 0.3
    mn = math.log(1e-2) / 1.5
    deltas = np.abs(np.linspace(mn, mx, D, dtype=np.float32))
    c["ndelta"] = np.ascontiguousarray(np.tile(-deltas[None, :], (128, 1)).astype(np.float32))
    tt = np.arange(L, dtype=np.int64)[:, None]
    ff = np.arange(L, dtype=np.int64)[None, :]
    ang = (((2 * ff + 1) * tt) % (2 * N)).astype(np.float64) * (2.0 * np.pi / (2 * N))
    Fc = np.zeros((T, T), np.float32)
    Fsn = np.zeros((T, T), np.float32)
    for sgi in range(nseg):
        sl = slice(sgi * L, (sgi + 1) * L)
        Fc[sl, sl] = np.cos(ang)
        Fsn[sl, sl] = -np.sin(ang)
    F2 = np.stack([Fc, Fsn], 0)
    Fh = F2.reshape(2, 16, 128, 16, 128).transpose(3, 2, 0, 1, 4)
    c["Fh"] = np.ascontiguousarray(Fh).astype(NPBF)
    G = np.concatenate([Fc.T, Fsn.T], 0) * np.float32(2.0 / N)
    c["Gh"] = np.ascontiguousarray(G.reshape(32, 128, T)).astype(NPBF)
    _CACHE[key] = c
    return c


def core_assignment():
    return [("s", 0), ("s", 1), ("s", 2), ("s", 3), ("p", 0), ("p", 1), ("p", 0), ("p", 1)]


def prep_core(inp, kind, idx, cfg):
    f = lambda a: np.ascontiguousarray(np.asarray(a, dtype=np.float32))
    m = {}
    if kind == "s":
        m["xT"] = f(inp["x_sample"][idx].T)
        m["posT"] = _CACHE.setdefault("pos", _grid_pos_T())
        m["condc"] = _col(f(inp["c"][idx]))
        m["gstate"] = f(inp["state_gla"][idx].transpose(0, 1, 3, 2, 4))
        m["rstate"] = f(inp["state_rwkv"][idx].transpose(0, 1, 4, 2, 3))
        pm = 0.0
    else:
        xs = inp["x_prompt"][idx * 8:(idx + 1) * 8].reshape(T, D)
        m["xT"] = f(xs.T)
        m["posT"] = _CACHE.setdefault("zpos", np.zeros((D, T), np.float32))
        m["condc"] = _col(f(inp["c_ctx"]))
        m["gstate"] = _CACHE.setdefault("zg", np.zeros((2, 2, 64, 4, 128), np.float32))
        m["rstate"] = _CACHE.setdefault("zr", np.zeros((2, 2, 64, 8, 64), np.float32))
        pm = 1.0
    m.update(_hy_consts(kind))
    fl = np.zeros((128, 4), np.float32)
    fl[:, 0] = pm
    fl[:, 1] = -pm
    fl[:, 2] = 1.0 - pm
    m["flags"] = fl
    return m


def run(inputs, cfg):
    key = repr(sorted(cfg.items()))
    if key not in _CACHE:
        _CACHE[key] = build(cfg)
    B = _CACHE[key]
    sh = prep_shared(inputs, cfg)
    in_maps = []
    for kind, idx in core_assignment():
        m = dict(sh)
        m.update(prep_core(inputs, kind, idx, cfg))
        in_maps.append({k: v for k, v in m.items() if k in B.din})
    res = run_bass_kernel_spmd(B.nc, in_maps, core_ids=list(range(8)))
    return res.results


def assemble(results, inputs):
    y_sample = np.stack([results[b]["yT"].T for b in range(4)], 0).astype(np.float32)
    yp = [results[4 + c]["yT"].T.reshape(8, SEG, D) for c in range(2)]
    y_prompt = np.concatenate(yp, 0).astype(np.float32)
    if "gst_out" not in results[4]:
        return y_prompt, y_sample
    gs = np.concatenate([results[4 + c]["gst_out"].transpose(2, 0, 1, 4, 3, 5) for c in range(2)], 0).astype(np.float32)
    rs = np.concatenate([results[4 + c]["rst_out"].transpose(2, 0, 1, 4, 5, 3) for c in range(2)], 0).astype(np.float32)
    return y_prompt, y_sample, np.ascontiguousarray(gs), np.ascontiguousarray(rs)


def kernel(**inputs):
    inputs = {k: np.asarray(v) for k, v in inputs.items()}
    cfg = dict(mixers=True)
    results = run(inputs, cfg)
    return assemble(results, inputs)


PI = float(np.pi)


def make_ident(B):
    idf = B.sb("idf", [128, 128], F32)
    idb = B.sb("idb", [128, 128], BF16)
    B.memset("pool", idf[:], 0.0, writes=("idf",))
    B.P.op("pool", lambda e: e.affine_select(out=idf[:], in_=idf[:], pattern=[[-1, 128]], compare_op=ALU.not_equal,
                                             fill=1.0, base=0, channel_multiplier=1), reads=("idf",), writes=("idf",))
    B.cp("pool", idb[:], idf[:], reads=("idf",), writes=("idb",))
    B.idf, B.idb = idf, idb


def hyena_filters(B):
    nc = B.nc
    ztab = B.inp("ztab", [33, T])
    tcol = B.inp("tcol", [128, 16])
    m0col = B.inp("m0col", [128, 16])
    d0col = B.inp("d0col", [128, 16])
    ndelta = B.inp("ndelta", [128, D])
    hfw1 = B.inp("hfw1", [2, 33, 64])
    hfw2 = B.inp("hfw2", [2, 64, 64])
    hfw3 = B.inp("hfw3", [2, 64, 64])
    hfw4 = B.inp("hfw4", [2, 64, 2 * D])
    hffq = B.inp("hffq", [2, 64, 3])
    hffb = B.inp("hffb", [2, 64, 3])
    hbias = B.inp("hbias", [2, 128, D])
    Fh = B.inp("Fh", [16, 128, 2, 16, 128], BF16)
    B.Fh = Fh
    if B.cfg.get("debug"):
        HF = B.outp("HF", [2, 16, 128, 2, D], BF16)
    else:
        HF = B.scratch("HF", [2, 16, 128, 2, D], BF16)
    B.HF = HF
    with ExitStack() as ph:
        B.es, old = ph, B.es
        zt = B.sb("zt", [33, T], F32)
        tc = B.sb("tc", [128, 16], F32)
        m0 = B.sb("m0", [128, 16], F32)
        d0 = B.sb("d0", [128, 16], F32)
        ndl = B.sb("ndl", [128, D], F32)
        hA = B.sb("hA", [64, T], F32)
        hBf = B.sb("hBf", [64, T], F32)
        w1 = B.sb("w1", [33, 64], F32)
        w2 = B.sb("w2", [64, 64], F32)
        w3 = B.sb("w3", [64, 64], F32)
        w4 = B.sb("w4", [64, 2 * D], F32)
        fq = B.sb("fq", [64, 3], F32)
        fb = B.sb("fb", [64, 3], F32)
        hbb = B.sb("hbb", [128, D], F32)
        dec = B.sb("dec", [128, D], F32)
        hf = B.sb("hf", [128, D], F32)
        hb = B.sb("hb", [128, D], F32)
        arg = [B.sb("arg0", [64, 512], F32)] * 2
        wr = [B.sb("wr0", [64, 512], F32)] * 2
        HS = B.HB[:].rearrange("p k (h c) -> p (k h) c", c=D)
        HD = B.sb("HD", [128, 16, D], BF16)
        Fs = [B.sb("Fs%d" % i, [128, 2, 16, 128], BF16) for i in range(2)]
        hst = [B.sb("hst0", [128, 2, D], BF16)] * 2
        psm = [B.ps("psm%d" % i, [128, 512], F32) for i in range(2)]
        ps4 = [B.ps("ps4%d" % i, [128, 512], F32) for i in range(4)]
        B.dma("sp", zt[:], ztab, writes=("zt",))
        B.dma("sp", tc[:], tcol, writes=("tc",))
        B.dma("sp", m0[:], m0col, writes=("m0",))
        B.dma("sp", d0[:], d0col, writes=("d0",))
        B.dma("sp", ndl[:], ndelta, writes=("ndl",))
        fcnt = 0
        for i in range(2):
            B.dma("sp", w1[:], hfw1[i], writes=("w1",))
            B.dma("sp", w2[:], hfw2[i], writes=("w2",))
            B.dma("sp", w3[:], hfw3[i], writes=("w3",))
            B.dma("sp", w4[:], hfw4[i], writes=("w4",))
            B.dma("sp", fq[:], hffq[i], writes=("fq",))
            B.dma("sp", fb[:], hffb[i], writes=("fb",))
            B.dma("sp", hbb[:], hbias[i], writes=("hbb",))
            B.tt("dve", fb[:], fb[:], fq[:], ALU.mult, reads=("fb", "fq"), writes=("fb",))
            src = zt
            stok = "zt"
            wts = [(w1, "w1", 33), (w2, "w2", 64), (w3, "w3", 64)]
            dsts = [(hA, "hA"), (hBf, "hBf"), (hA, "hA")]
            mcnt = 0
            for li in range(3):
                wt, wtok, kk = wts[li]
                dst, dtok = dsts[li]
                for tt in range(NT):
                    tsl = slice(tt * 512, (tt + 1) * 512)
                    q = mcnt % 2
                    mcnt += 1
                    B.mm(psm[q][0:64, :], wt[0:kk, :], src[0:kk, tsl], True, True, reads=(wtok, stok + ":%d" % tt if stok != "zt" else "zt"),
                         writes=("psm%d" % q,))
                    a, w_ = arg[0], wr[0]
                    B.ts("dve", a[:], psm[q][0:64, :], fq[:, li:li + 1], fb[:, li:li + 1], ALU.mult, ALU.add,
                         reads=("psm%d" % q, "fq", "fb"), writes=("arg0",))
                    q = 0
                    for rep in range(2):
                        B.ts("dve", w_[:], a[:], PI, -2.0 * PI, ALU.is_gt, ALU.mult, reads=("arg%d" % q,), writes=("wr%d" % q,))
                        B.tt("dve", a[:], a[:], w_[:], ALU.add, reads=("arg%d" % q, "wr%d" % q), writes=("arg%d" % q,))
                        B.ts("dve", w_[:], a[:], -PI, 2.0 * PI, ALU.is_lt, ALU.mult, reads=("arg%d" % q,), writes=("wr%d" % q,))
                        B.tt("dve", a[:], a[:], w_[:], ALU.add, reads=("arg%d" % q, "wr%d" % q), writes=("arg%d" % q,))
                    B.act(dst[:, tsl], a[:], AF.Sin, reads=("arg%d" % q,), writes=(dtok + ":%d" % tt,))
                src, stok = dst, dtok
            for t16 in range(16):
                tt = t16 // 4
                B.act(dec[:], ndl[:], AF.Exp, scale=tc[:, t16:t16 + 1], reads=("ndl", "tc"), writes=("dec",))
                for cg in range(4):
                    B.mm(ps4[cg][:], hA[:, t16 * 128:(t16 + 1) * 128], w4[:, cg * 512:(cg + 1) * 512], True, True,
                         reads=("hA:%d" % tt, "w4"), writes=("ps4%d" % cg,))
                for cg in range(2):
                    csl = slice(cg * 512, (cg + 1) * 512)
                    B.tt("dve", hf[:, csl], ps4[cg][:], dec[:, csl], ALU.mult, reads=("ps4%d" % cg, "dec"), writes=("hf%d" % cg,))
                    B.stt("dve", hb[:, csl], ps4[2 + cg][:], m0[:, t16:t16 + 1], dec[:, csl], ALU.mult, ALU.mult,
                          reads=("ps4%d" % (2 + cg), "dec", "m0"), writes=("hb%d" % cg,))
                B.stt("dve", hf[:], hbb[:], d0[:, t16:t16 + 1], hf[:], ALU.mult, ALU.add, reads=("hbb", "d0", "hf0", "hf1"), writes=("hf0", "hf1"))
                B.tt("dve", HS[:, t16, :], hf[:], hb[:], ALU.add, reads=("hf0", "hf1", "hb0", "hb1"), writes=("HS:%d" % t16,))
                B.tt("dve", HD[:, t16, :], hf[:], hb[:], ALU.subtract, reads=("hf0", "hf1", "hb0", "hb1"), writes=("HD:%d" % t16,))
            for g in range(16):
                s = fcnt % 2
                fcnt += 1
                B.dma("sp", Fs[s][:], Fh[g], writes=("Fs%d" % s,))
                for part, (src_, stk) in enumerate(((HS, "HS"), (HD, "HD"))):
                    for hh in range(2):
                        pp = ps4[part * 2 + hh]
                        for t16 in range(16):
                            B.mm(pp[:], Fs[s][:, part, t16, :], src_[:, t16, hh * 512:(hh + 1) * 512], t16 == 0, t16 == 15,
                                 reads=("Fs%d" % s, stk + ":%d" % t16), writes=("ps4%d" % (part * 2 + hh),))
                        B.cp("act", hst[0][:, part, hh * 512:(hh + 1) * 512], pp[:], reads=("ps4%d" % (part * 2 + hh),), writes=("hst0",))
                B.dma("sp", HF[i, g], hst[0][:], reads=("hst0",), writes=("HF%d:%d" % (i, g),))
        B.fence()
    B.es = old


def hyena_mixer(B, l):
    i = l // 2
    X, HB, modc, flg = B.X, B.HB, B.modc, B.flg
    xtok, htok = B.xtok, B.htok
    if not hasattr(B, "hy_in"):
        B.hy_in = dict(
            hwin=B.inp("hwin", [2, 24, 128, 8, 128]),
            hcw=B.inp("hcw", [2, 128, 72]),
            hcb=B.inp("hcb", [2, 128, 24]),
            hwout=B.inp("hwout", [2, 8, 128, 8, 128]),
            Gh=B.inp("Gh", [32, 128, T], BF16),
            X0S=B.scratch("X0S", [8, 128, T], BF16),
        )
    hwin, hcw, hcb, hwout, Gh, X0S = (B.hy_in[k] for k in ("hwin", "hcw", "hcb", "hwout", "Gh", "X0S"))
    Fh, HF = B.Fh, B.HF
    Z = HB
    with ExitStack() as mix:
        B.es, old_mix = mix, B.es
        VX = B.sb("VX", [128, 16, D], BF16)
        with ExitStack() as ph:
            B.es = ph
            cw = B.sb("cw", [128, 72], F32)
            cb = B.sb("cb", [128, 24], F32)
            ncw = B.sb("ncw", [128, 72], F32)
            U = [B.sb("U%d" % k, [128, T + 2], F32) for k in range(3)]
            acc = [B.sb("acc%d" % k, [128, T], F32) for k in range(3)]
            vxb = [B.sb("vxb%d" % k, [128, T], BF16) for k in range(2)]
            x0b = [B.sb("x0b%d" % k, [128, T], BF16) for k in range(2)]
            wps = [B.sb("wps%d" % k, [128, 8, 128], BF16) for k in range(3)]
            psI = [B.ps("psI%d" % k, [128, 512], F32) for k in range(4)]
            psT = [B.ps("psT%d" % k, [128, 4, 128], BF16) for k in range(2)]
            B.dma("sp", cw[:], hcw[i], writes=("cw",))
            B.dma("sp", cb[:], hcb[i], writes=("cb",))
            B.ts("dve", ncw[:], cw[:], flg[:, 1:2], None, ALU.mult, reads=("cw", "flg"), writes=("ncw",))
            for k in range(3):
                B.memset("dve", U[k][:, 0:1], 0.0, writes=("U%d" % k,))
                B.memset("dve", U[k][:, T + 1:T + 2], 0.0, writes=("U%d" % k,))
            pc = 0
            tc_ = 0
            for j in range(8):
                for kind, c in ((0, 8 + j), (1, 16 + j), (2, j)):
                    Uk, ak = U[kind], acc[kind]
                    ut, at = "U%d" % kind, "acc%d" % kind
                    B.dma("pool", wps[kind][:], hwin[i, c], writes=("wps%d" % kind,))
                    for tt in range(NT):
                        tsl = slice(tt * 512, (tt + 1) * 512)
                        q = pc % 4
                        pc += 1
                        for k in range(8):
                            B.mm(psI[q][:], wps[kind][:, k, :], HB[:, k, tsl], k == 0, k == 7,
                                 reads=("wps%d" % kind, htok(k, tt)), writes=("psI%d" % q,))
                        B.cp("act", Uk[:, 1 + tt * 512:1 + (tt + 1) * 512], psI[q][:], reads=("psI%d" % q,), writes=(ut,))
                    B.act(ak[:], Uk[:, 1:T + 1], AF.Identity, bias=cb[:, c:c + 1], scale=cw[:, c * 3 + 1:c * 3 + 2],
                          reads=(ut, "cw", "cb"), writes=(at,))
                    B.stt("dve", ak[:], Uk[:, 0:T], cw[:, c * 3:c * 3 + 1], ak[:], ALU.mult, ALU.add, reads=(ut, at, "cw"), writes=(at,))
                    B.stt("dve", ak[:], Uk[:, 2:T + 2], cw[:, c * 3 + 2:c * 3 + 3], ak[:], ALU.mult, ALU.add, reads=(ut, at, "cw"), writes=(at,))
                    B.stt("dve", ak[:, SEG:T:SEG], Uk[:, SEG:T:SEG], ncw[:, c * 3:c * 3 + 1], ak[:, SEG:T:SEG], ALU.mult, ALU.add,
                          reads=(ut, at, "ncw"), writes=(at,))
                    B.stt("dve", ak[:, SEG - 1:T - 1:SEG], Uk[:, SEG + 1:T + 1:SEG], ncw[:, c * 3 + 2:c * 3 + 3], ak[:, SEG - 1:T - 1:SEG],
                          ALU.mult, ALU.add, reads=(ut, at, "ncw"), writes=(at,))
                    if kind == 1:
                        vb = vxb[j % 2]
                        vt = "vxb%d" % (j % 2)
                        B.tt("dve", vb[:], acc[1][:], acc[0][:], ALU.mult, reads=("acc0", "acc1"), writes=(vt,))
                        for t4 in range(4):
                            pq = tc_ % 2
                            tc_ += 1
                            for qq in range(4):
                                t16 = t4 * 4 + qq
                                B.tr(psT[pq][:, qq, :], vb[:, t16 * 128:(t16 + 1) * 128], B.idb[:], reads=(vt, "idb"), writes=("psT%d" % pq,))
                            B.cp("act", VX[:, t4 * 4:(t4 + 1) * 4, j * 128:(j + 1) * 128], psT[pq][:], reads=("psT%d" % pq,),
                                 writes=["VX:%d:%d" % (t4 * 4 + qq, j // 4) for qq in range(4)])
                    if kind == 2:
                        xb_ = x0b[j % 2]
                        xt_ = "x0b%d" % (j % 2)
                        B.cp("act", xb_[:], acc[2][:], reads=("acc2",), writes=(xt_,))
                        B.dma("sp", X0S[j], xb_[:], reads=(xt_,), writes=("X0S%d" % j,))
            B.fence()
        for half in range(2):
            hsl = slice(half * 512, (half + 1) * 512)
            with ExitStack() as ph2:
                B.es = ph2
                YS = B.sb("YS", [128, 32, 512], BF16)
                with ExitStack() as ph:
                    B.es = ph
                    Fs = [B.sb("Fsd%d" % k, [128, 2, 16, 128], BF16) for k in range(2)]
                    Hs = [B.sb("Hs%d" % k, [128, 2, 512], BF16) for k in range(2)]
                    tp = [B.sb("tp%d" % k, [128, 512], F32) for k in range(4)]
                    psV = [B.ps("psV%d" % k, [128, 512], F32) for k in range(4)]
                    for g in range(16):
                        s = g % 2
                        B.dma("sp", Fs[s][:], Fh[g], writes=("Fs%d" % s,))
                        B.dma("sp", Hs[s][:], HF[i, g][:, :, hsl], reads=("HF%d:%d" % (i, g),), writes=("Hs%d" % s,))
                        for part in range(2):
                            pp = psV[s * 2 + part]
                            for t16 in range(16):
                                B.mm(pp[:], Fs[s][:, part, t16, :], VX[:, t16, hsl], t16 == 0, t16 == 15,
                                     reads=("Fs%d" % s, "VX:%d:%d" % (t16, half)), writes=("psV%d" % (s * 2 + part),))
                        vre, vim = psV[s * 2], psV[s * 2 + 1]
                        rt, it = "psV%d" % (s * 2), "psV%d" % (s * 2 + 1)
                        B.tt("dve", tp[0][:], vre[:], Hs[s][:, 0, :], ALU.mult, reads=(rt, "Hs%d" % s), writes=("tp0",))
                        B.tt("dve", tp[1][:], vim[:], Hs[s][:, 1, :], ALU.mult, reads=(it, "Hs%d" % s), writes=("tp1",))
                        B.tt("dve", tp[2][:], vre[:], Hs[s][:, 1, :], ALU.mult, reads=(rt, "Hs%d" % s), writes=("tp2",))
                        B.tt("dve", tp[3][:], vim[:], Hs[s][:, 0, :], ALU.mult, reads=(it, "Hs%d" % s), writes=("tp3",))
                        B.tt("pool", YS[:, g, :], tp[0][:], tp[1][:], ALU.subtract, reads=("tp0", "tp1"), writes=("YS:%d" % g,))
                        B.tt("pool", YS[:, 16 + g, :], tp[2][:], tp[3][:], ALU.add, reads=("tp2", "tp3"), writes=("YS:%d" % (16 + g),))
                    B.fence()
                with ExitStack() as ph:
                    B.es = ph
                    Gs = [B.sb("Gs%d" % k, [128, 1024], BF16) for k in range(3)]
                    x0s = [B.sb("x0s%d" % k, [128, T], BF16) for k in range(4)]
                    psY = [B.ps("psY%d" % k, [128, 512], F32) for k in range(8)]
                    for cc in range(4):
                        B.dma("sp", x0s[cc][:], X0S[half * 4 + cc], reads=("X0S%d" % (half * 4 + cc),), writes=("x0s%d" % cc,))
                    gc = 0
                    for tp2 in range(2):
                        for fc in range(32):
                            s = gc % 3
                            gc += 1
                            B.dma("sp", Gs[s][:], Gh[fc][:, tp2 * 1024:(tp2 + 1) * 1024], writes=("Gs%d" % s,))
                            for t2 in range(2):
                                for cc in range(4):
                                    B.mm(psY[t2 * 4 + cc][:], YS[:, fc, cc * 128:(cc + 1) * 128], Gs[s][:, t2 * 512:(t2 + 1) * 512],
                                         fc == 0, fc == 31, reads=("YS:%d" % fc, "Gs%d" % s), writes=("psY%d" % (t2 * 4 + cc),))
                        for t2 in range(2):
                            tt = tp2 * 2 + t2
                            tsl = slice(tt * 512, (tt + 1) * 512)
                            for cc in range(4):
                                B.tt("dve", Z[:, half * 4 + cc, tsl], psY[t2 * 4 + cc][:], x0s[cc][:, tsl], ALU.mult,
                                     reads=("psY%d" % (t2 * 4 + cc), "x0s%d" % cc), writes=(htok(half * 4 + cc, tt),))
                    B.fence()
        if B.cfg.get("debug") and l == 1:
            dz = B.outp("dbgZ", [8, 128, T], BF16)
            for cc in range(8):
                B.dma("sp", dz[cc], Z[:, cc, :], reads=[htok(cc, tt) for tt in range(NT)], is_out=True)
            B.fence()
        with ExitStack() as ph:
            B.es = ph
            lnps = B.alloc_ln()
            wos = [B.sb("wos%d" % k, [128, 8, 128], BF16) for k in range(8)]
            otm = [B.sb("otm%d" % k, [128, 512], F32) for k in range(2)]
            psO = [B.ps("psO%d" % k, [128, 512], F32) for k in range(4)]
            for dch in range(8):
                B.dma("pool", wos[dch][:], hwout[i, dch], writes=("wos%d" % dch,))
            oc = [0]

            def OP(tt):
                tsl = slice(tt * 512, (tt + 1) * 512)
                for dch in range(8):
                    q = oc[0] % 4
                    oc[0] += 1
                    for cc in range(8):
                        B.mm(psO[q][:], wos[dch][:, cc, :], Z[:, cc, tsl], cc == 0, cc == 7, reads=("wos%d" % dch, htok(cc, tt)), writes=("psO%d" % q,))
                    c = l * 48 + 2 * 8 + dch
                    B.act(otm[q % 2][:], psO[q][:], AF.Identity, scale=modc[:, c:c + 1], reads=("psO%d" % q, "modc"), writes=("otm%d" % (q % 2),))
                    B.stt("dve", X[:, dch, tsl], X[:, dch, tsl], ALPHA, otm[q % 2][:], ALU.mult, ALU.add,
                          reads=(xtok(dch, tt), "otm%d" % (q % 2)), writes=(xtok(dch, tt),))
                    yield

            def LNt(tt):
                yield from B.ln_tile_gen(lnps, l, 0, tt, (l, 3))
            drain(OP(0))
            for tt in range(1, NT):
                drain(OP(tt), LNt(tt - 1))
            drain(LNt(NT - 1))
            B.fence()
    B.es = old_mix


CH = 64
NG = 45
EXPC = float(math.exp(-0.5))
GN_EPS = 64e-5
def _ab_groups():
    g = []
    for h in range(4):
        g.append(("gq%d" % h, 64 * h, 64))
    for h in range(4):
        g.append(("gk%d" % h, 256 + 64 * h, 64))
    for h in range(4):
        g.append(("gv%d" % h, 512 + 128 * h, 128))
    for h in range(4):
        g.append(("gg%d" % h, 1024 + 128 * h, 128))
    g.append(("glr", 1536, 32))
    r0 = 1568
    g.append(("rwl", r0 + 1536, 128))
    g.append(("ral", r0 + 1664, 128))
    g.append(("rgl0", r0 + 1792, 128))
    g.append(("rgl1", r0 + 1920, 32))
    for h in range(8):
        g.append(("rr%d" % h, r0 + 64 * h, 64))
        g.append(("rk%d" % h, r0 + 512 + 64 * h, 64))
        g.append(("rv%d" % h, r0 + 1024 + 64 * h, 64))
    assert len(g) == NG
    return g


AB_GROUPS = _ab_groups()


def ab_inputs(B):
    if hasattr(B, "ab_in"):
        return B.ab_in
    d = dict(
        abwin=B.inp("abwin", [2, NG, 128, 8, 128]),
        abmu=B.inp("abmu", [2, 128, NG * 2]),
        abwo_g=B.inp("abwo_g", [2, 8, 128, 4, 128]),
        abwo_r=B.inp("abwo_r", [2, 8, 64, 8, 128]),
        w2bd=B.inp("w2bd", [2, 128, 1024]),
        w0row=B.inp("w0row", [2, 1, 1024]),
        a2s=B.inp("a2s", [2, 128, 512]),
        a0c=B.inp("a0c", [2, 64, 16]),
        g2a=B.inp("g2a", [2, 128, 512]),
        g2b=B.inp("g2b", [2, 32, 512]),
        wdbd=B.inp("wdbd", [2, 32, 512]),
        gbrow=B.inp("gbrow", [2, 1, 512]),
        rcols=B.inp("rcols", [2, 64, 40]),
        gnw=B.inp("gnw", [2, 128, 1]),
        trimats=B.inp("trimats", [4, 128, 128]),
        cmasks=B.inp("cmasks", [4, 64, 512], BF16),
        identrep=B.inp("identrep", [64, 512], BF16),
        lvlmask=B.inp("lvlmask", [64, 6, 64], BF16),
        gstate=B.inp("gstate", [2, 2, 64, 4, 128]),
        rstate=B.inp("rstate", [2, 2, 64, 8, 64]),
        gst_out=B.outp("gst_out", [2, 2, NSEG, 64, 4, 128]),
        rst_out=B.outp("rst_out", [2, 2, NSEG, 64, 8, 64]),
        XS=B.scratch("XS", [8, 128, T], F32),
        GQ=B.scratch("GQ", [4, 64, T], BF16), GK=B.scratch("GK", [4, 64, T], BF16),
        GV=B.scratch("GV", [4, 128, T], BF16), GG=B.scratch("GG", [4, 128, T], BF16),
        LA=B.scratch("LA", [T, 512], F32), SG=B.scratch("SG", [T, 1024], F32),
        RR=B.scratch("RR", [8, 64, T], BF16), RK=B.scratch("RK", [8, 64, T], BF16), RV=B.scratch("RV", [8, 64, T], BF16),
        RKK=B.scratch("RKK", [8, 64, T], BF16), RBV=B.scratch("RBV", [8, 64, T], BF16), RG=B.scratch("RG", [8, 64, T], BF16),
        RA=B.scratch("RA", [2, 8, 64, T], BF16),
        YFR=B.scratch("YFR", [8, 64, T], F32), YFG=B.scratch("YFG", [4, 128, T], F32),
    )
    B.ab_in = d
    return d


def ab_phase1(B, l):
    i = l // 2
    A = ab_inputs(B)
    X, HB, flg = B.X, B.HB, B.flg
    xtok, htok = B.xtok, B.htok
    gidx = {g[0]: n for n, g in enumerate(AB_GROUPS)}
    with ExitStack() as ph:
        B.es, old = ph, B.es
        for k in range(8):
            B.dma("sp", A["XS"][k], X[:, k, :], reads=[xtok(k, tt) for tt in range(NT)], writes=("XS%d" % k,))
        B.fence()
        mu = B.sb("mu", [128, NG * 2], F32)
        c1 = B.sb("c1", [128, NG], F32)
        nmu = B.sb("nmu", [128, NG * 2], F32)
        rc = B.sb("rc", [64, 40], F32)
        a0 = B.sb("a0", [64, 16], F32)
        U = [B.sb("aU%d" % k, [128, T + 2], F32) for k in range(2)]
        acc = [X[:, k, :] for k in range(3)]
        ob = [X[:, 3 + k, 0:1024].bitcast(BF16) for k in range(3)]
        twl = X[:, 3, 1024:2048].bitcast(BF16)
        alb = X[:, 4, 1024:2048].bitcast(BF16)
        sg0 = X[:, 5, 1024:2048].bitcast(BF16)
        sg1 = B.sb("sg1", [32, T], BF16)
        lrb = B.sb("lrb", [32, T], BF16)
        wps = [B.sb("awps%d" % k, [128, 8, 128], BF16) for k in range(3)]
        w2 = B.sb("w2", [128, 1024], BF16)
        w0r = B.sb("w0r", [1, 1024], F32)
        a2 = B.sb("a2", [128, 512], BF16)
        g2a = B.sb("g2a", [128, 512], BF16)
        g2b = B.sb("g2b", [32, 512], BF16)
        wdb = B.sb("wdb", [32, 512], BF16)
        gbr = B.sb("gbr", [1, 512], F32)
        onesf = B.sb("onesf", [1, 128], F32)
        ones1 = B.sb("ones1", [128, 128], BF16)
        tmpa = [B.sb("tmpa%d" % k, [128, 512], F32) for k in range(2)]
        tmpb = [B.sb("tmpb%d" % k, [128, 512], BF16) for k in range(2)]
        tmpc = [B.sb("tmpc%d" % k, [128, 512], F32) for k in range(2)]
        tokb = [B.sb("tokb%d" % k, [128, 1024], F32) for k in range(2)]
        psI = [B.ps("apsI%d" % k, [128, 512], F32) for k in range(4)]
        psS = [B.ps("apsS%d" % k, [128, 512], F32) for k in range(4)]
        B.dma("sp", mu[:], A["abmu"][i], writes=("mu",))
        B.dma("sp", rc[:], A["rcols"][i], writes=("rc",))
        B.dma("sp", a0[:], A["a0c"][i], writes=("a0",))
        B.dma("pool", w2[:], A["w2bd"][i], writes=("w2",))
        B.dma("sp", w0r[:], A["w0row"][i], writes=("w0r",))
        B.dma("pool", a2[:], A["a2s"][i], writes=("a2",))
        B.dma("pool", g2a[:], A["g2a"][i], writes=("g2a",))
        B.dma("pool", g2b[:], A["g2b"][i], writes=("g2b",))
        B.dma("pool", wdb[:], A["wdbd"][i], writes=("wdb",))
        B.dma("sp", gbr[:], A["gbrow"][i], writes=("gbr",))
        B.memset("dve", onesf[:], 1.0, writes=("onesf",))
        B.memset("dve", ones1[:], 1.0, writes=("ones1",))
        muv = mu[:].rearrange("p (g two) -> p g two", two=2)
        B.tt("dve", c1[:], muv[:, :, 0], muv[:, :, 1], ALU.add, reads=("mu",), writes=("c1",))
        B.ts("dve", c1[:], c1[:], -1.0, 1.0, ALU.mult, ALU.add, reads=("c1",), writes=("c1",))
        B.ts("dve", nmu[:], mu[:], flg[:, 1:2], None, ALU.mult, reads=("mu", "flg"), writes=("nmu",))
        for k in range(2):
            B.memset("dve", U[k][:, 0:1], 0.0, writes=("aU%d" % k,))
            B.memset("dve", U[k][:, T + 1:T + 2], 0.0, writes=("aU%d" % k,))
        cnt = dict(pc=0, u=0, w=0, o=0, s=0, t=0)

        def inproj(gname, shift):
            n = gidx[gname]
            M = AB_GROUPS[n][2]
            s = cnt["w"] % 3
            cnt["w"] += 1
            B.dma("pool", wps[s][:], A["abwin"][i, n], writes=("awps%d" % s,))
            if shift:
                ui = cnt["u"] % 2
                cnt["u"] += 1
                Uk, ut = U[ui], "aU%d" % ui
            ai = cnt["o"] % 3
            cnt["o"] += 1
            ak, at = acc[ai], "aacc%d" % ai
            for tt in range(NT):
                tsl = slice(tt * 512, (tt + 1) * 512)
                q = cnt["pc"] % 4
                cnt["pc"] += 1
                for k in range(8):
                    B.mm(psI[q][0:M, :], wps[s][:, k, 0:M], HB[:, k, tsl], k == 0, k == 7,
                         reads=("awps%d" % s, htok(k, tt)), writes=("apsI%d" % q,))
                if shift:
                    B.cp("act", Uk[0:M, 1 + tt * 512:1 + (tt + 1) * 512], psI[q][0:M, :], reads=("apsI%d" % q,), writes=(ut,))
                else:
                    B.cp("act", ak[0:M, tsl], psI[q][0:M, :], reads=("apsI%d" % q,), writes=(at,))
            if shift:
                B.act(ak[0:M, :], Uk[0:M, 1:T + 1], AF.Identity, scale=c1[0:M, n:n + 1], reads=(ut, "c1"), writes=(at,))
                B.stt("dve", ak[0:M, :], Uk[0:M, 0:T], mu[0:M, 2 * n:2 * n + 1], ak[0:M, :], ALU.mult, ALU.add, reads=(ut, at, "mu"), writes=(at,))
                B.stt("dve", ak[0:M, :], Uk[0:M, 2:T + 2], mu[0:M, 2 * n + 1:2 * n + 2], ak[0:M, :], ALU.mult, ALU.add, reads=(ut, at, "mu"), writes=(at,))
                B.stt("dve", ak[0:M, SEG:T:SEG], Uk[0:M, SEG:T:SEG], nmu[0:M, 2 * n:2 * n + 1], ak[0:M, SEG:T:SEG], ALU.mult, ALU.add,
                      reads=(ut, at, "nmu"), writes=(at,))
                B.stt("dve", ak[0:M, SEG - 1:T - 1:SEG], Uk[0:M, SEG + 1:T + 1:SEG], nmu[0:M, 2 * n + 1:2 * n + 2], ak[0:M, SEG - 1:T - 1:SEG],
                      ALU.mult, ALU.add, reads=(ut, at, "nmu"), writes=(at,))
            return ak, at, M

        def outbuf():
            oi = cnt["s"] % 3
            cnt["s"] += 1
            return ob[oi], "aob%d" % oi

        for h in range(4):
            for nm, dst in (("gq", "GQ"), ("gk", "GK"), ("gv", "GV")):
                ak, at, M = inproj("%s%d" % (nm, h), False)
                o, ot = outbuf()
                B.cp("dve", o[0:M, :], ak[0:M, :], reads=(at,), writes=(ot,))
                B.dma("sp", A[dst][h], o[0:M, :], reads=(ot,), writes=("%s%d" % (dst, h),))
            ak, at, M = inproj("gg%d" % h, False)
            o, ot = outbuf()
            B.act(o[:], ak[:], AF.Silu, reads=(at,), writes=(ot,))
            B.dma("sp", A["GG"][h], o[:], reads=(ot,), writes=("GG%d" % h,))
        ak, at, M = inproj("glr", False)
        B.cp("dve", lrb[:], ak[0:32, :], reads=(at,), writes=("lrb",))
        for t16 in range(16):
            q = cnt["t"] % 2
            cnt["t"] += 1
            B.mm(psS[q][:], lrb[:, t16 * 128:(t16 + 1) * 128], wdb[:], True, False, reads=("lrb", "wdb"), writes=("apsS%d" % q,))
            B.mm(psS[q][:], onesf[:], gbr[:], False, True, reads=("onesf", "gbr"), writes=("apsS%d" % q,))
            tb = tokb[q]
            B.act(tb[:, 0:512], psS[q][:], AF.Sigmoid, reads=("apsS%d" % q,), writes=("tokb%d" % q,))
            B.act(tb[:, 0:512], tb[:, 0:512], AF.Ln, reads=("tokb%d" % q,), writes=("tokb%d" % q,))
            B.ts("dve", tb[:, 0:512], tb[:, 0:512], 1.0 / 16.0, None, ALU.mult, reads=("tokb%d" % q,), writes=("tokb%d" % q,))
            B.dma("sp", A["LA"][t16 * 128:(t16 + 1) * 128, :], tb[:, 0:512], reads=("tokb%d" % q,), writes=("LA%d" % t16,))
        ak, at, M = inproj("rwl", True)
        B.act(twl[:], ak[:], AF.Tanh, reads=(at,), writes=("twl",))
        for t16 in range(16):
            q = cnt["t"] % 2
            cnt["t"] += 1
            for hh in range(2):
                pp = psS[q * 2 + hh]
                B.mm(pp[:], twl[:, t16 * 128:(t16 + 1) * 128], w2[:, hh * 512:(hh + 1) * 512], True, False, reads=("twl", "w2"), writes=("apsS%d" % (q * 2 + hh),))
                B.mm(pp[:], onesf[:], w0r[:, hh * 512:(hh + 1) * 512], False, True, reads=("onesf", "w0r"), writes=("apsS%d" % (q * 2 + hh),))
                B.act(tokb[q][:, hh * 512:(hh + 1) * 512], pp[:], AF.Sigmoid, reads=("apsS%d" % (q * 2 + hh),), writes=("tokb%d" % q,))
            B.dma("sp", A["SG"][t16 * 128:(t16 + 1) * 128, :], tokb[q][:], reads=("tokb%d" % q,), writes=("SG%d" % t16,))
        ak, at, M = inproj("ral", True)
        B.cp("dve", alb[:], ak[:], reads=(at,), writes=("alb",))
        for d in range(2):
            for h in range(8):
                o, ot = outbuf()
                for tt in range(NT):
                    tsl = slice(tt * 512, (tt + 1) * 512)
                    q = cnt["pc"] % 4
                    cnt["pc"] += 1
                    B.mm(psI[q][0:64, :], a2[d * 64:(d + 1) * 64, h * 64:(h + 1) * 64], alb[d * 64:(d + 1) * 64, tsl], True, True,
                         reads=("a2", "alb"), writes=("apsI%d" % q,))
                    B.act(o[0:64, tsl], psI[q][0:64, :], AF.Sigmoid, bias=a0[:, d * 8 + h:d * 8 + h + 1], reads=("apsI%d" % q, "a0"), writes=(ot,))
                B.dma("sp", A["RA"][d, h], o[0:64, :], reads=(ot,), writes=("RA%d_%d" % (d, h),))
        ak, at, M = inproj("rgl0", True)
        B.act(sg0[:], ak[:], AF.Sigmoid, reads=(at,), writes=("sg0",))
        ak, at, M = inproj("rgl1", True)
        B.act(sg1[:], ak[0:32, :], AF.Sigmoid, reads=(at,), writes=("sg1",))
        for h in range(8):
            o, ot = outbuf()
            for tt in range(NT):
                tsl = slice(tt * 512, (tt + 1) * 512)
                q = cnt["pc"] % 4
                cnt["pc"] += 1
                B.mm(psI[q][0:64, :], g2a[:, h * 64:(h + 1) * 64], sg0[:, tsl], True, False, reads=("g2a", "sg0"), writes=("apsI%d" % q,))
                B.mm(psI[q][0:64, :], g2b[:, h * 64:(h + 1) * 64], sg1[:, tsl], False, True, reads=("g2b", "sg1"), writes=("apsI%d" % q,))
                B.cp("act", o[0:64, tsl], psI[q][0:64, :], reads=("apsI%d" % q,), writes=(ot,))
            B.dma("sp", A["RG"][h], o[0:64, :], reads=(ot,), writes=("RG%d" % h,))
        for h in range(8):
            ar, art, _ = inproj("rr%d" % h, True)
            akk, akt, _ = inproj("rk%d" % h, True)
            av, avt, _ = inproj("rv%d" % h, True)
            for src, st, dst in ((ar, art, "RR"), (akk, akt, "RK"), (av, avt, "RV")):
                o, ot = outbuf()
                B.cp("dve", o[0:64, :], src[0:64, :], reads=(st,), writes=(ot,))
                B.dma("sp", A[dst][h], o[0:64, :], reads=(ot,), writes=("%s%d" % (dst, h),))
            okk, okt = outbuf()
            obv, obt = outbuf()
            for tt in range(NT):
                tsl = slice(tt * 512, (tt + 1) * 512)
                ta, tat = tmpa[tt % 2], "tmpa%d" % (tt % 2)
                tb, tbt = tmpb[tt % 2], "tmpb%d" % (tt % 2)
                q = cnt["t"] % 4
                cnt["t"] += 1
                B.ts("dve", ta[0:64, :], akk[0:64, tsl], rc[:, h:h + 1], None, ALU.mult, reads=(akt, "rc"), writes=(tat,))
                B.act(tb[0:64, :], ta[0:64, :], AF.Square, reads=(tat,), writes=(tbt,))
                B.mm(psS[q][0:64, :], ones1[0:64, 0:64], tb[0:64, :], True, True, reads=("ones1", tbt), writes=("apsS%d" % q,))
                tcn, tct = tmpc[tt % 2], "tmpc%d" % (tt % 2)
                B.act(tcn[0:64, :], psS[q][0:64, :], AF.Ln, bias=B.eps_ln[0:64, 1:2], reads=("apsS%d" % q,), writes=(tct,))
                B.act(tcn[0:64, :], tcn[0:64, :], AF.Exp, scale=-0.5, reads=(tct,), writes=(tct,))
                B.tt("dve", okk[0:64, tsl], ta[0:64, :], tcn[0:64, :], ALU.mult, reads=(tat, tct), writes=(okt,))
                B.stt("dve", tb[0:64, :], ar[0:64, tsl], rc[:, 16 + h:17 + h], akk[0:64, tsl], ALU.mult, ALU.mult, reads=(art, akt, "rc"), writes=(tbt,))
                q2 = cnt["t"] % 4
                cnt["t"] += 1
                B.mm(psS[q2][0:64, :], ones1[0:64, 0:64], tb[0:64, :], True, True, reads=("ones1", tbt), writes=("apsS%d" % q2,))
                B.tt("dve", obv[0:64, tsl], psS[q2][0:64, :], av[0:64, tsl], ALU.mult, reads=("apsS%d" % q2, avt), writes=(obt,))
            B.dma("sp", A["RKK"][h], okk[0:64, :], reads=(okt,), writes=("RKK%d" % h,))
            B.dma("sp", A["RBV"][h], obv[0:64, :], reads=(obt,), writes=("RBV%d" % h,))
        B.fence()
    B.es = old


def ab_phase2(B, l):
    i = l // 2
    A = ab_inputs(B)
    HB, flg = B.HB, B.flg
    htok = B.htok
    with ExitStack() as ph:
        B.es, old = ph, B.es
        tri = B.sb("tri", [128, 4, 128], F32)
        msk2 = B.sb("msk", [128, 4, 64], BF16)
        idr2 = B.sb("idr", [128, 512], BF16)
        lvl2 = B.sb("lvl", [128, 6, 64], BF16)
        rc2 = B.sb("rc2", [128, 40], F32)
        omka2 = B.sb("omka", [128, 8], F32)
        msk, idr, lvl, rc, omka = msk2[0:64], idr2[0:64], lvl2[0:64], rc2[0:64], omka2[0:64]
        for hf_ in range(2):
            B.dma("sp", lvl2[hf_ * 64:(hf_ + 1) * 64], A["lvlmask"], writes=("lvl",))
            B.dma("sp", idr2[hf_ * 64:(hf_ + 1) * 64], A["identrep"], writes=("idr",))
            B.dma("sp", rc2[hf_ * 64:(hf_ + 1) * 64], A["rcols"][i], writes=("rc2",))
            for k in range(4):
                B.dma("sp", msk2[hf_ * 64:(hf_ + 1) * 64, k, :], A["cmasks"][k][:, 0:64], writes=("msk",))
        gnw = B.sb("gnw", [128, 1], F32)
        ones1 = B.sb("ones1b", [128, 128], BF16)
        for k in range(4):
            B.dma("sp", tri[:, k, :], A["trimats"][k], writes=("tri",))
        B.dma("sp", gnw[:], A["gnw"][i], writes=("gnw",))
        B.memset("dve", ones1[:], 1.0, writes=("ones1b",))
        B.ts("dve", omka2[:], rc2[:, 8:16], -1.0, 1.0, ALU.mult, ALU.add, reads=("rc2",), writes=("omka",))
        psD = [B.ps("psD%d" % k, [64, 4, 128], F32) for k in range(2)]
        psTr = [B.ps("psTr%d" % k, [64, 512], BF16) for k in range(2)]
        psA = [B.ps("psA%d" % k, [128, 512], F32) for k in range(4)]
        cnt = dict(a=0, t=0, d=0)

        def next_psA():
            q = cnt["a"] % 4
            cnt["a"] += 1
            return psA[q], "psA%d" % q

        for kind in ("g",):
            NH = 4 if kind == "g" else 8
            DV = 128 if kind == "g" else 64
            VP = DV
            W = NH * 64
            WV = NH * DV
            low = kind == "r"
            with ExitStack() as kp:
                B.es = kp
                names = ["q", "k", "v"] + (["kk", "a"] if low else [])
                ld = {}
                xslots = {"q": (6, 0), "k": (6, 1), "kk": (7, 0), "a": (7, 1)}
                for nm in names:
                    if nm == "v":
                        ld[nm] = [B.sb("ld%s%s" % (kind, nm), [VP, NH, SEG], BF16)] * 2
                    else:
                        xk, xh = xslots[nm]
                        xv_ = B.X[0:64, xk, xh * 1024:(xh + 1) * 1024].bitcast(BF16)
                        ld[nm] = [xv_[:, 0:NH * SEG].rearrange("p (h t) -> p h t", h=NH)] * 2
                dec = [B.sb("dec%s" % kind, [128, 2, W], F32)] * 2
                onm = ["RHO", "KT", "KH"] + (["KAP", "BT", "BH"] if low else [])
                opd = {}
                for oi_, nm in enumerate(onm):
                    opd[nm] = []
                    for z in range(2):
                        nbuf = oi_ * 2 + z
                        xv_ = B.X[0:64, nbuf // 2, (nbuf % 2) * 1024:(nbuf % 2 + 1) * 1024].bitcast(BF16)
                        opd[nm].append(xv_[:, 0:NH * SEG].rearrange("p (h t) -> p h t", h=NH))
                pC = [B.sb("pC%s%d" % (kind, z), [64, NH, 4], F32) for z in range(2)]
                pt = {nm: B.sb("pt%s%s" % (kind, nm), [64, NH, 128], F32) for nm in ("inc", "inv", "exc", "end", "b", "kd")}
                Hf = B.sb("Hf" + kind, [64, NH, DV], F32)
                Hb = B.sb("Hb" + kind, [64, NH, DV], BF16)
                if low:
                    YF = [B.HB[0:64, 4 + 2 * z:6 + 2 * z, :].rearrange("p k t -> p (k t)").bitcast(F32).rearrange("p (h t) -> p h t", h=NH) for z in range(2)]
                else:
                    YF = [B.HB[:, 4 + z, :].bitcast(F32).rearrange("p (h t) -> p h t", h=NH) for z in range(2)]
                cb = {}
                for nm, wdt in (("KHt", W), ("BHt", W), ("Vt", WV), ("M0", W), ("N0", W), ("Akk", W), ("Brk", W), ("Brb", W),
                                ("Nn", W), ("Mn", W), ("Tt", W), ("Xb", WV), ("Ub", WV)):
                    if not low and nm in ("BHt", "M0", "N0", "Akk", "Brb", "Nn", "Mn", "Tt", "Xb", "Ub"):
                        continue
                    nb = 4 if nm == "Tt" else 2
                    cb[nm] = [B.sb("cb%s%s%d" % (kind, nm, z), [64, wdt], BF16) for z in range(nb)]
                if low:
                    nrm = {nm: pt[pn][:].rearrange("p h t -> p (h t)")[:, 0:512] for nm, pn in (("a", "inc"), ("b", "inv"), ("c", "exc"))}
                    nrmt = dict(a="ptinc", b="ptinv", c="ptexc")
                else:
                    nrm = {nm: B.sb("nrm%s%s" % (kind, nm), [VP, 512], F32)[:] for nm in ("a", "b", "c")}
                    nrmt = dict(a="nrma", b="nrmb", c="nrmc")
                nrb = {nm: B.sb("nrb%s%s" % (kind, nm), [VP, 512], BF16) for nm in ("a", "b")}
                if low:
                    gl = {"g": ld["kk"], "bv": ld["a"]}
                    glt = {"g": "ldkk", "bv": "lda"}
                else:
                    gl = {"g": [B.sb("glgg", [VP, NH, SEG], BF16)] * 2}
                    glt = {"g": "glg"}
                src = dict(q=A["RR"] if low else A["GQ"], k=A["RK"] if low else A["GK"], v=A["RV"] if low else A["GV"])
                if low:
                    src["kk"] = A["RKK"]
                stin = A["rstate"] if low else A["gstate"]
                stout = A["rst_out"] if low else A["gst_out"]
                YFS = A["YFR"] if low else A["YFG"]
                DEC = A["SG"] if low else A["LA"]
                DW = 512 if low else 256
                sn = 0
                for d in range(2):
                    fwd = d == 0
                    triI, triE, triS = (0, 1, 2) if fwd else (3, 2, 1)
                    mSU, mIU, mSL = (0, 1, 2) if fwd else (2, 3, 0)
                    B.dma("sp", Hf[:], stin[i, d], writes=("Hf",))
                    B.cp("act", Hb[:], Hf[:], reads=("Hf",), writes=("Hb",))
                    segs = list(range(NSEG)) if fwd else list(range(NSEG - 1, -1, -1))
                    for seg in segs:
                        z = sn % 2
                        sn += 1
                        ssl = slice(seg * SEG, (seg + 1) * SEG)
                        L = {nm: ld[nm][z] for nm in names}
                        LT = {nm: "ld%s" % nm for nm in names}
                        for nm in names:
                            sa = A["RA"][d] if nm == "a" else src[nm]
                            B.dma("sp", L[nm][:], sa.rearrange("h p t -> p h t")[:, :, ssl], writes=(LT[nm],))
                        B.dma("sp", dec[z][:], DEC[ssl, d * DW:(d + 1) * DW].rearrange("(q p) c -> p q c", p=128), writes=("dec0",))
                        if not fwd:
                            B.dma("sp", YF[z][:], YFS.rearrange("h p t -> p h t")[:, :, ssl], writes=("YF%d" % z,))
                        O = {nm: opd[nm][z] for nm in onm}
                        OT = {nm: "op%s%d" % (nm, z) for nm in onm}
                        for tq in range(2):
                            qsl = slice(tq * 128, (tq + 1) * 128)
                            esc = -EXPC if low else 1.0
                            for var, trik, outs in ((0, triI, (("inc", esc), ("inv", -esc))), (1, triE, (("exc", esc),)), (2, triS, (("end", esc),))):
                                if not low and var == 1:
                                    continue
                                for hg in range(NH // 4):
                                    pd = psD[cnt["d"] % 2]
                                    pdt = "psD%d" % (cnt["d"] % 2)
                                    cnt["d"] += 1
                                    for h4 in range(4):
                                        h = hg * 4 + h4
                                        B.mm(pd[:, h4, :], dec[z][:, tq, h * 64:(h + 1) * 64], tri[:, trik, :], True, True,
                                             reads=("dec0", "tri"), writes=(pdt,))
                                    for onm_, sc in outs:
                                        B.act(pt[onm_][:, hg * 4:(hg + 1) * 4, :], pd[:], AF.Exp, scale=sc, reads=(pdt,), writes=("pt" + onm_,))
                            if low:
                                B.tt("dve", pt["b"][:], L["a"][:, :, qsl], L["kk"][:, :, qsl], ALU.mult, reads=(LT["a"], LT["kk"]), writes=("ptb",))
                                for h in range(NH):
                                    B.ts("dve", pt["kd"][:, h, :], L["a"][:, h, qsl], rc[:, 8 + h:9 + h], omka[:, h:h + 1], ALU.mult, ALU.add,
                                         reads=(LT["a"], "rc2", "omka"), writes=("ptkd",))
                                B.tt("dve", pt["kd"][:], pt["kd"][:], L["k"][:, :, qsl], ALU.mult, reads=("ptkd", LT["k"]), writes=("ptkd",))
                                kd = pt["kd"][:]
                                kdt = "ptkd"
                                B.tt("dve", O["RHO"][:, :, qsl], L["q"][:, :, qsl], pt["inc"][:], ALU.mult, reads=(LT["q"], "ptinc"), writes=(OT["RHO"],))
                                B.tt("pool", O["KAP"][:, :, qsl], L["kk"][:, :, qsl], pt["exc"][:], ALU.mult, reads=(LT["kk"], "ptexc"), writes=(OT["KAP"],))
                                B.stt("dve", O["BT"][:, :, qsl], pt["b"][:], -1.0, pt["inv"][:], ALU.mult, ALU.mult, reads=("ptb", "ptinv"), writes=(OT["BT"],))
                                B.stt("dve", O["BH"][:, :, qsl], pt["b"][:], -1.0, pt["end"][:], ALU.mult, ALU.mult, reads=("ptb", "ptend"), writes=(OT["BH"],))
                            else:
                                kd = L["k"][:, :, qsl]
                                kdt = LT["k"]
                                B.stt("dve", O["RHO"][:, :, qsl], L["q"][:, :, qsl], 0.125, pt["inc"][:], ALU.mult, ALU.mult,
                                      reads=(LT["q"], "ptinc"), writes=(OT["RHO"],))
                            B.tt("dve", O["KT"][:, :, qsl], kd, pt["inv"][:], ALU.mult, reads=(kdt, "ptinv"), writes=(OT["KT"],))
                            B.tt("pool", O["KH"][:, :, qsl], kd, pt["end"][:], ALU.mult, reads=(kdt, "ptend"), writes=(OT["KH"],))
                            ccol = 63 if fwd else 0
                            B.cp("dve", pC[z][:, :, tq * 2:(tq + 1) * 2], pt["inc"][:, :, ccol:128:64], reads=("ptinc",), writes=("pC%d" % z,))

                        def pre(c):
                            zz = c % 2
                            cs = slice(c * CH, (c + 1) * CH)
                            todo = [("KHt", O["KH"], OT["KH"], 64)] + ([("BHt", O["BH"], OT["BH"], 64)] if low else []) + [("Vt", L["v"], LT["v"], DV)]
                            for nm, sarr, stk, wd in todo:
                                pq = cnt["t"] % 2
                                cnt["t"] += 1
                                for h in range(NH):
                                    B.tr(psTr[pq][:, h * wd:(h + 1) * wd], sarr[:, h, cs], B.idb[0:(VP if nm == "Vt" else 64), 0:(VP if nm == "Vt" else 64)],
                                         reads=(stk, "idb"), writes=("psTr%d" % pq,))
                                B.cp("act", cb[nm][zz][:, 0:NH * wd], psTr[pq][:, 0:NH * wd], reads=("psTr%d" % pq,), writes=("cb%s%d" % (nm, zz),))
                            mats = [("Brk", "KT", "RHO", mIU)]
                            if low:
                                mats += [("M0", "BT", "KAP", mSU), ("N0", "KAP", "BT", mSL), ("Akk", "KT", "KAP", mSU), ("Brb", "BT", "RHO", mIU)]
                            for nm, la_, ra_, mk in mats:
                                pa, pat = next_psA()
                                for h in range(NH):
                                    B.mm(pa[0:64, h * 64:(h + 1) * 64], O[la_][:, h, cs], O[ra_][:, h, cs], True, True,
                                         reads=(OT[la_], OT[ra_]), writes=(pat,))
                                B.tt("dve", cb[nm][zz][:].rearrange("p (h t) -> p h t", h=NH), pa[0:64, 0:W].rearrange("p (h t) -> p h t", h=NH),
                                     msk[:, mk:mk + 1, :].to_broadcast([64, NH, 64]), ALU.mult, reads=(pat, "msk"), writes=("cb%s%d" % (nm, zz),))
                            if low:
                                tb0 = (c % 2) * 2

                                def v3(ap):
                                    return ap.rearrange("p (h t) -> p h t", h=NH)

                                def lm(k):
                                    return lvl[:, k:k + 1, :].to_broadcast([64, NH, 64])
                                M0v, N0v = v3(cb["M0"][zz][:]), v3(cb["N0"][zz][:])
                                m0t, n0t = "cbM0%d" % zz, "cbN0%d" % zz
                                Tm, Tmt = cb["Nn"][0], "cbNn0"
                                Tt, Ttt = cb["Tt"][tb0], "cbTt%d" % tb0
                                B.tt("dve", v3(Tm[:]), N0v, lm(0), ALU.mult, reads=(n0t, "lvl"), writes=(Tmt,))
                                B.tt("dve", Tm[:], Tm[:], idr[:, 0:W], ALU.add, reads=(Tmt, "idr"), writes=(Tmt,))
                                B.tt("dve", v3(Tt[:]), M0v, lm(0), ALU.mult, reads=(m0t, "lvl"), writes=(Ttt,))
                                B.tt("dve", Tt[:], Tt[:], idr[:, 0:W], ALU.add, reads=(Ttt, "idr"), writes=(Ttt,))
                                for lev in range(1, 6):
                                    Ml, Mlt = cb["Mn"][0], "cbMn0"
                                    Pb, Pbt = cb["Mn"][1], "cbMn1"
                                    B.tt("dve", v3(Ml[:]), M0v, lm(lev), ALU.mult, reads=(m0t, "lvl"), writes=(Mlt,))
                                    pa, pat = next_psA()
                                    for h in range(NH):
                                        hs_ = slice(h * 64, (h + 1) * 64)
                                        B.mm(pa[0:64, hs_], Ml[:, hs_], Tm[:, hs_], True, True, reads=(Mlt, Tmt), writes=(pat,))
                                    B.cp("act", Pb[:], pa[0:64, 0:W], reads=(pat,), writes=(Pbt,))
                                    if lev < 5:
                                        pa2, pat2 = next_psA()
                                        for h in range(NH):
                                            hs_ = slice(h * 64, (h + 1) * 64)
                                            B.mm(pa2[0:64, hs_], Tt[:, hs_], Pb[:, hs_], True, True, reads=(Ttt, Pbt), writes=(pat2,))
                                    pa3, pat3 = next_psA()
                                    for h in range(NH):
                                        hs_ = slice(h * 64, (h + 1) * 64)
                                        B.mm(pa3[0:64, hs_], Pb[:, hs_], Tt[:, hs_], True, True, reads=(Pbt, Ttt), writes=(pat3,))
                                    if lev < 5:
                                        Tn, Tnt = cb["Nn"][lev % 2], "cbNn%d" % (lev % 2)
                                        B.tt("dve", Tn[:], pa2[0:64, 0:W], Tm[:], ALU.add, reads=(pat2, Tmt), writes=(Tnt,))
                                    T2, T2t = cb["Tt"][tb0 + (lev % 2)], "cbTt%d" % (tb0 + (lev % 2))
                                    B.tt("dve", T2[:], pa3[0:64, 0:W], Tt[:], ALU.add, reads=(pat3, Ttt), writes=(T2t,))
                                    Tt, Ttt = T2, T2t
                                    if lev < 5:
                                        Tm, Tmt = Tn, Tnt
                                return (Tt, Ttt)
                            return None

                        def seq(c, tinfo):
                            zz = c % 2
                            cs = slice(c * CH, (c + 1) * CH)
                            Vt, Vtt = cb["Vt"][zz], "cbVt%d" % zz
                            KHt, KHtt = cb["KHt"][zz], "cbKHt%d" % zz
                            Brk, Brkt = cb["Brk"][zz], "cbBrk%d" % zz
                            if low:
                                Tt, Ttt = tinfo
                                Akk, Akkt = cb["Akk"][zz], "cbAkk%d" % zz
                                Brb, Brbt = cb["Brb"][zz], "cbBrb%d" % zz
                                BHt, BHtt = cb["BHt"][zz], "cbBHt%d" % zz
                                Xb, Xbt = cb["Xb"][zz], "cbXb%d" % zz
                                Ub, Ubt = cb["Ub"][zz], "cbUb%d" % zz
                                pa, pat = next_psA()
                                for h in range(NH):
                                    hs_ = slice(h * 64, (h + 1) * 64)
                                    B.mm(pa[0:64, hs_], O["KAP"][:, h, cs], Hb[:, h, :], True, False, reads=(OT["KAP"], "Hb"), writes=(pat,))
                                    B.mm(pa[0:64, hs_], Akk[:, hs_], Vt[:, hs_], False, True, reads=(Akkt, Vtt), writes=(pat,))
                                B.cp("act", Xb[:], pa[0:64, 0:WV], reads=(pat,), writes=(Xbt,))
                                pa, pat = next_psA()
                                for h in range(NH):
                                    hs_ = slice(h * 64, (h + 1) * 64)
                                    B.mm(pa[0:64, hs_], Tt[:, hs_], Xb[:, hs_], True, True, reads=(Ttt, Xbt), writes=(pat,))
                                B.cp("act", Ub[:], pa[0:64, 0:WV], reads=(pat,), writes=(Ubt,))
                            pa, pat = next_psA()
                            for h in range(NH):
                                hs_ = slice(h * 64, (h + 1) * 64)
                                vs_ = slice(h * DV, (h + 1) * DV)
                                B.mm(pa[0:VP, hs_], Hb[:, h, :], O["RHO"][:, h, cs], True, False, reads=("Hb", OT["RHO"]), writes=(pat,))
                                B.mm(pa[0:VP, hs_], Vt[:, vs_], Brk[:, hs_], False, not low, reads=(Vtt, Brkt), writes=(pat,))
                                if low:
                                    B.mm(pa[0:VP, hs_], Ub[:, vs_], Brb[:, hs_], False, True, reads=(Ubt, Brbt), writes=(pat,))
                            yv = pa[0:VP, 0:W].rearrange("p (h t) -> p h t", h=NH)
                            if fwd:
                                B.cp("act", YF[z][:, :, cs], yv, reads=(pat,), writes=("YF%d" % z,))
                            else:
                                B.tt("dve", YF[z][:, :, cs], yv, YF[z][:, :, cs], ALU.add, reads=(pat, "YF%d" % z), writes=("YF%d" % z,))
                            pa, pat = next_psA()
                            for h in range(NH):
                                hs_ = slice(h * 64, (h + 1) * 64)
                                vs_ = slice(h * DV, (h + 1) * DV)
                                B.mm(pa[0:64, vs_], KHt[:, hs_], Vt[:, vs_], True, not low, reads=(KHtt, Vtt), writes=(pat,))
                                if low:
                                    B.mm(pa[0:64, vs_], BHt[:, hs_], Ub[:, vs_], False, True, reads=(BHtt, Ubt), writes=(pat,))
                            B.tt("dve", Hf[:], Hf[:], pC[z][:, :, c:c + 1].to_broadcast([64, NH, DV]), ALU.mult, reads=("Hf", "pC%d" % z), writes=("Hf",))
                            B.tt("dve", Hf[:], Hf[:], pa[0:64, 0:WV].rearrange("p (h v) -> p h v", h=NH), ALU.add, reads=("Hf", pat), writes=("Hf",))
                            B.cp("act", Hb[:], Hf[:], reads=("Hf",), writes=("Hb",))

                        order = list(range(4)) if fwd else [3, 2, 1, 0]
                        tinfo = pre(order[0])
                        for ci, c in enumerate(order):
                            nxt = pre(order[ci + 1]) if ci + 1 < 4 else None
                            seq(c, tinfo)
                            tinfo = nxt
                        B.dma("sp", stout[i, d, seg], Hf[:], reads=("Hf",), is_out=True)
                        B.ts("dve", Hf[:], Hf[:], flg[0:64, 2:3], None, ALU.mult, reads=("Hf", "flg"), writes=("Hf",))
                        B.cp("act", Hb[:], Hf[:], reads=("Hf",), writes=("Hb",))
                        if fwd:
                            B.dma("sp", YFS.rearrange("h p t -> p h t")[:, :, ssl], YF[z][:], reads=("YF%d" % z,), writes=("YFS%d" % seg,))
                        else:
                            for nm in gl:
                                sa = A["RG"] if (low and nm == "g") else (A["RBV"] if nm == "bv" else A["GG"])
                                B.dma("sp", gl[nm][z][:], sa.rearrange("h p t -> p h t")[:, :, ssl], writes=(glt[nm],))
                            ab_finish_segment(B, l, kind, low, NH, VP, YF[z], "YF%d" % z, gl, glt, z, nrm, nrmt, nrb, rc, gnw, ones1, next_psA, seg)
                B.fence()
        ab_rwkv_parallel(B, l, dict(tri=tri, msk2=msk2, idr2=idr2, lvl2=lvl2, rc2=rc2, omka2=omka2, gnw=gnw, ones1=ones1,
                                    psD=psD, psTr=psTr, psA=psA))
        B.fence()
    B.es = old


def ab_finish_segment(B, l, kind, low, NH, VP, Y, Yt, gl, glt, z, nrm, nrmt, nrb, rc, gnw, ones1, next_psA, seg):
    ssl = slice(seg * SEG, (seg + 1) * SEG)
    for hp in range(NH // 2):
        yv = Y[:, hp * 2:hp * 2 + 2, :]
        a, b_, c_ = nrm["a"].rearrange("p (h t) -> p h t", h=2), nrm["b"].rearrange("p (h t) -> p h t", h=2), nrm["c"].rearrange("p (h t) -> p h t", h=2)
        ba, bb = nrb["a"][:].rearrange("p (h t) -> p h t", h=2), nrb["b"][:].rearrange("p (h t) -> p h t", h=2)
        B.act(bb, yv, AF.Square, reads=(Yt,), writes=("nrbb",))
        pq, pqt = next_psA()
        B.mm(pq[0:VP, :], ones1[0:VP, 0:VP], nrb["b"][:], True, True, reads=("ones1b", "nrbb"), writes=(pqt,))
        if low:
            B.cp("act", ba, yv, reads=(Yt,), writes=("nrba",))
            pm_, pmt = next_psA()
            B.mm(pm_[0:VP, :], ones1[0:VP, 0:VP], nrb["a"][:], True, True, reads=("ones1b", "nrba"), writes=(pmt,))
            B.act(nrm["a"], pm_[0:VP, :], AF.Identity, scale=1.0 / 64.0, reads=(pmt,), writes=(nrmt["a"],))
            B.tt("dve", nrm["b"], nrm["a"], nrm["a"], ALU.mult, reads=(nrmt["a"],), writes=(nrmt["b"],))
            B.stt("dve", nrm["b"], pq[0:VP, :], 1.0 / 64.0, nrm["b"], ALU.mult, ALU.subtract, reads=(pqt, nrmt["b"]), writes=(nrmt["b"],))
            B.act(nrm["b"], nrm["b"], AF.Ln, bias=B.eps_ln[0:VP, 2:3], reads=(nrmt["b"],), writes=(nrmt["b"],))
            B.act(nrm["b"], nrm["b"], AF.Exp, scale=-0.5, reads=(nrmt["b"],), writes=(nrmt["b"],))
            B.tt("dve", c_, yv, a, ALU.subtract, reads=(Yt, nrmt["a"]), writes=(nrmt["c"],))
            B.tt("dve", c_, c_, b_, ALU.mult, reads=(nrmt["c"], nrmt["b"]), writes=(nrmt["c"],))
            for hh in range(2):
                h = hp * 2 + hh
                B.ts("dve", c_[:, hh, :], c_[:, hh, :], rc[:, 24 + h:25 + h], rc[:, 32 + h:33 + h], ALU.mult, ALU.add, reads=(nrmt["c"], "rc2"), writes=(nrmt["c"],))
            B.tt("pool", c_, c_, gl["bv"][z][:, hp * 2:hp * 2 + 2, :], ALU.add, reads=(nrmt["c"], glt["bv"]), writes=(nrmt["c"],))
            B.tt("pool", B.ORW[:, hp * 2:hp * 2 + 2, ssl], c_, gl["g"][z][:, hp * 2:hp * 2 + 2, :], ALU.mult, reads=(nrmt["c"], glt["g"]), writes=("ORW%d" % seg,))
        else:
            B.act(nrm["b"], pq[0:VP, :], AF.Ln, scale=1.0 / 128.0, bias=B.eps_ln[0:VP, 0:1], reads=(pqt,), writes=(nrmt["b"],))
            B.act(nrm["b"], nrm["b"], AF.Exp, scale=-0.5, reads=(nrmt["b"],), writes=(nrmt["b"],))
            B.stt("dve", c_, yv, gnw[:, 0:1], b_, ALU.mult, ALU.mult, reads=(Yt, "gnw", nrmt["b"]), writes=(nrmt["c"],))
            for hh in range(2):
                h = hp * 2 + hh
                tt = seg // 2
                B.tt("pool", B.HB[:, h, ssl], c_[:, hh, :], gl["g"][z][:, h, :], ALU.mult, reads=(nrmt["c"], glt["g"]), writes=("HBg%d:%d" % (h, seg),))


def ab_phase3(B, l):
    i = l // 2
    A = ab_inputs(B)
    X, HB, modc = B.X, B.HB, B.modc
    xtok, htok = B.xtok, B.htok
    with ExitStack() as ph:
        B.es, old = ph, B.es
        for k in range(8):
            B.dma("sp", X[:, k, :], A["XS"][k], reads=("XS%d" % k,), writes=[xtok(k, tt) for tt in range(NT)])
        lnps = B.alloc_ln()
        wog = [B.sb("wog%d" % k, [128, 4, 128], BF16) for k in range(8)]
        wor = [B.sb("wor%d" % k, [64, 8, 128], BF16) for k in range(8)]
        otm = [B.sb("aotm%d" % k, [128, 512], F32) for k in range(2)]
        psO = [B.ps("apsO%d" % k, [128, 512], F32) for k in range(4)]
        for dch in range(8):
            B.dma("pool", wog[dch][:], A["abwo_g"][i, dch], writes=("wog%d" % dch,))
            B.dma("pool", wor[dch][:], A["abwo_r"][i, dch], writes=("wor%d" % dch,))
        oc = [0]

        def OP(tt):
            tsl = slice(tt * 512, (tt + 1) * 512)
            for dch in range(8):
                q = oc[0] % 4
                oc[0] += 1
                for h in range(4):
                    B.mm(psO[q][:], wog[dch][:, h, :], HB[:, h, tsl], h == 0, False, reads=("wog%d" % dch, htok(h, tt)), writes=("apsO%d" % q,))
                for h in range(8):
                    B.mm(psO[q][:], wor[dch][:, h, :], B.ORW[:, h, tsl], False, h == 7, reads=("wor%d" % dch,), writes=("apsO%d" % q,))
                c = l * 48 + 2 * 8 + dch
                B.act(otm[q % 2][:], psO[q][:], AF.Identity, scale=modc[:, c:c + 1], reads=("apsO%d" % q, "modc"), writes=("aotm%d" % (q % 2),))
                B.stt("dve", X[:, dch, tsl], X[:, dch, tsl], ALPHA, otm[q % 2][:], ALU.mult, ALU.add,
                      reads=(xtok(dch, tt), "aotm%d" % (q % 2)), writes=(xtok(dch, tt),))
                yield

        def LNt(tt):
            yield from B.ln_tile_gen(lnps, l, 0, tt, (l, 3))
        drain(OP(0))
        for tt in range(1, NT):
            drain(OP(tt), LNt(tt - 1))
        drain(LNt(NT - 1))
        B.fence()
    B.es = old


def ab_mixer(B, l):
    ab_phase1(B, l)
    with ExitStack() as mix:
        B.es, old = mix, B.es
        B.ORW = B.sb("ORW", [64, 8, T], BF16)
        ab_phase2(B, l)
        ab_phase3(B, l)
    B.es = old


def ab_rwkv_parallel(B, l, C):
    i = l // 2
    A = ab_inputs(B)
    P = B.P
    flg = B.flg
    NH, DV, W = 8, 64, 512
    tri, msk2, idr2, lvl2, rc2, omka2, ones1 = (C[k] for k in ("tri", "msk2", "idr2", "lvl2", "rc2", "omka2", "ones1"))
    psD, psTr, psA = C["psD"], C["psTr"], C["psA"]
    if "YBR" not in A:
        A["YBR"] = B.scratch("YBR", [8, 64, T], F32)
    with ExitStack() as kp:
        B.es, old = kp, B.es
        ldv = B.sb("ldrv", [128, NH, SEG], BF16)
        dec = [B.sb("decr%d" % d, [128, 2, W], F32) for d in range(2)]
        pC = B.sb("pCr", [128, NH, 4], F32)
        pt = {nm: B.sb("ptr" + nm, [128, NH, 128], F32) for nm in ("inc", "inv", "exc", "end", "b", "kd")}
        Hf = B.sb("Hfr", [128, NH, DV], F32)
        Hb = B.sb("Hbr", [128, NH, DV], BF16)
        cbn = [("KHt", 2), ("BHt", 2), ("Vt", 2), ("M0", 2), ("N0", 2), ("Akk", 2), ("Brk", 2), ("Brb", 2), ("Nn", 2), ("Mn", 2), ("Tt", 4), ("Xb", 2), ("Ub", 2)]
        cb = {nm: [B.sb("cbp%s%d" % (nm, z), [128, W], BF16) for z in range(nb)] for nm, nb in cbn}
        nrb = {nm: B.sb("nrbp" + nm, [64, 512], BF16) for nm in ("a", "b")}
        onm = ["RHO", "KT", "KH", "KAP", "BT", "BH"]
        names = ["q", "k", "v", "kk", "a"]
        src = dict(q=A["RR"], k=A["RK"], v=A["RV"], kk=A["RKK"])
        xslots = {"q": (6, 0), "k": (6, 1), "kk": (7, 0), "a": (7, 1)}

        def xview(po, k, half):
            return B.X[po:po + 64, k, half * 1024:(half + 1) * 1024].bitcast(BF16).rearrange("p (h t) -> p h t", h=NH)

        def stream(d):
            po = 64 * d
            ps_ = slice(po, po + 64)
            fwd = d == 0
            triI, triE, triS = (0, 1, 2) if fwd else (3, 2, 1)
            mSU, mIU, mSL = (0, 1, 2) if fwd else (2, 3, 0)
            msk, idr, lvl, rc, omka = msk2[ps_], idr2[ps_], lvl2[ps_], rc2[ps_], omka2[ps_]
            L = {nm: (ldv[ps_] if nm == "v" else xview(po, *xslots[nm])) for nm in names}
            opd = {nm: [xview(po, (oi_ * 2 + z) // 2, (oi_ * 2 + z) % 2) for z in range(2)] for oi_, nm in enumerate(onm)}
            YF = [B.HB[ps_, 4 + 2 * z:6 + 2 * z, :].rearrange("p k t -> p (k t)").bitcast(F32).rearrange("p (h t) -> p h t", h=NH) for z in range(2)]
            ptd = {nm: pt[nm][ps_] for nm in pt}
            Hfd, Hbd, pCd = Hf[ps_], Hb[ps_], pC[ps_]
            cbd = {nm: [t_[ps_] for t_ in cb[nm]] for nm in cb}
            YS = A["YFR"] if fwd else A["YBR"]
            pa_n = [0]

            def next_psA():
                q = 2 * d + (pa_n[0] % 2)
                pa_n[0] += 1
                return psA[q], "psA%d" % q
            pd, pdt = psD[d], "psD"
            pd2 = psD[d][:].rearrange("p a b -> p (a b)")
            SEQ_EVERY = B.cfg.get("seq_every", 2)
            pT, pTt = psTr[d], "psTr"
            idb = B.idb[ps_, po:po + 64]

            def v3(ap):
                return ap.rearrange("p (h t) -> p h t", h=NH)

            B.dma("sp", Hfd, A["rstate"][i, d], writes=("Hf",))
            B.cp("act", Hbd, Hfd, reads=("Hf",), writes=("Hb",))
            yield
            segs = list(range(NSEG)) if fwd else list(range(NSEG - 1, -1, -1))
            for sn, seg in enumerate(segs):
                z = sn % 2
                ssl = slice(seg * SEG, (seg + 1) * SEG)
                for nm in names:
                    sa = A["RA"][d] if nm == "a" else src[nm]
                    B.dma("sp", L[nm], sa.rearrange("h p t -> p h t")[:, :, ssl], writes=("ld" + nm,))
                B.dma("sp", dec[d][:], A["SG"][ssl, d * 512:(d + 1) * 512].rearrange("(q p) c -> p q c", p=128), writes=("dec",))
                O = {nm: opd[nm][z] for nm in onm}
                OT = {nm: "op%s%d" % (nm, z) for nm in onm}
                yield
                for tq in range(2):
                    qsl = slice(tq * 128, (tq + 1) * 128)
                    for var, trik, outs in ((0, triI, (("inc", -EXPC), ("inv", EXPC))), (1, triE, (("exc", -EXPC),)), (2, triS, (("end", -EXPC),))):
                        for hg in range(2):
                            for h4 in range(4):
                                h = hg * 4 + h4
                                B.mm(pd[:, h4, :], dec[d][:, tq, h * 64:(h + 1) * 64], tri[:, trik, :], True, True, reads=("dec",), writes=(pdt,))
                            for onm_, sc in outs:
                                B.act(ptd[onm_][:, hg * 4:(hg + 1) * 4, :], pd[:], AF.Exp, scale=sc, reads=(pdt,), writes=("pt" + onm_,))
                            yield
                    B.tt("pool", ptd["b"], L["a"][:, :, qsl], L["kk"][:, :, qsl], ALU.mult, reads=("lda", "ldkk"), writes=("ptb",))
                    for h in range(NH):
                        B.act(ptd["kd"][:, h, :], L["a"][:, h, qsl], AF.Identity, bias=omka[:, h:h + 1], scale=rc[:, 8 + h:9 + h],
                              reads=("lda",), writes=("ptkd",))
                    B.tt("dve", ptd["kd"], ptd["kd"], L["k"][:, :, qsl], ALU.mult, reads=("ptkd", "ldk"), writes=("ptkd",))
                    yield
                    B.tt("dve", O["RHO"][:, :, qsl], L["q"][:, :, qsl], ptd["inc"], ALU.mult, reads=("ldq", "ptinc"), writes=(OT["RHO"],))
                    B.tt("pool", O["KAP"][:, :, qsl], L["kk"][:, :, qsl], ptd["exc"], ALU.mult, reads=("ldkk", "ptexc"), writes=(OT["KAP"],))
                    B.stt("dve", O["BT"][:, :, qsl], ptd["b"], -1.0, ptd["inv"], ALU.mult, ALU.mult, reads=("ptb", "ptinv"), writes=(OT["BT"],))
                    yield
                    B.stt("dve", O["BH"][:, :, qsl], ptd["b"], -1.0, ptd["end"], ALU.mult, ALU.mult, reads=("ptb", "ptend"), writes=(OT["BH"],))
                    B.tt("pool", O["KT"][:, :, qsl], ptd["kd"], ptd["inv"], ALU.mult, reads=("ptkd", "ptinv"), writes=(OT["KT"],))
                    B.tt("pool", O["KH"][:, :, qsl], ptd["kd"], ptd["end"], ALU.mult, reads=("ptkd", "ptend"), writes=(OT["KH"],))
                    ccol = 63 if fwd else 0
                    B.cp("act", pCd[:, :, tq * 2:(tq + 1) * 2], ptd["inc"][:, :, ccol:128:64], reads=("ptinc",), writes=("pC",))
                    yield

                def pre(c):
                    zz = c % 2
                    cs = slice(c * CH, (c + 1) * CH)
                    for nm, sarr, stk in (("KHt", O["KH"], OT["KH"]), ("BHt", O["BH"], OT["BH"]), ("Vt", L["v"], "ldv")):
                        for h in range(NH):
                            B.tr(pT[:, h * 64:(h + 1) * 64], sarr[:, h, cs], idb, reads=(stk,), writes=(pTt,))
                        B.cp("act", cbd[nm][zz], pT[:, 0:W], reads=(pTt,), writes=("cb%s%d" % (nm, zz),))
                        yield
                    for nm, la_, ra_, mk in (("Brk", "KT", "RHO", mIU), ("M0", "BT", "KAP", mSU), ("N0", "KAP", "BT", mSL),
                                             ("Akk", "KT", "KAP", mSU), ("Brb", "BT", "RHO", mIU)):
                        pa, pat = next_psA()
                        for h in range(NH):
                            B.mm(pa[0:64, h * 64:(h + 1) * 64], O[la_][:, h, cs], O[ra_][:, h, cs], True, True, reads=(OT[la_], OT[ra_]), writes=(pat,))
                        B.tt("dve", v3(cbd[nm][zz]), v3(pa[0:64, 0:W]), msk[:, mk:mk + 1, :].to_broadcast([64, NH, 64]), ALU.mult,
                             reads=(pat,), writes=("cb%s%d" % (nm, zz),))
                        yield
                    tb0 = (c % 2) * 2

                    def lm(k):
                        return lvl[:, k:k + 1, :].to_broadcast([64, NH, 64])
                    M0v, N0v = v3(cbd["M0"][zz]), v3(cbd["N0"][zz])
                    m0t, n0t = "cbM0%d" % zz, "cbN0%d" % zz
                    Tm, Tmt = cbd["Nn"][0], "cbNn0"
                    Tt, Ttt = cbd["Tt"][tb0], "cbTt%d" % tb0
                    B.tt("dve", v3(Tm), N0v, lm(0), ALU.mult, reads=(n0t,), writes=(Tmt,))
                    B.tt("pool", Tm, Tm, idr[:, 0:W], ALU.add, reads=(Tmt,), writes=(Tmt,))
                    B.tt("dve", v3(Tt), M0v, lm(0), ALU.mult, reads=(m0t,), writes=(Ttt,))
                    B.tt("pool", Tt, Tt, idr[:, 0:W], ALU.add, reads=(Ttt,), writes=(Ttt,))
                    yield
                    for lev in range(1, 6):
                        Ml, Mlt = cbd["Mn"][0], "cbMn0"
                        Pb, Pbt = cbd["Mn"][1], "cbMn1"
                        B.tt("dve", v3(Ml), M0v, lm(lev), ALU.mult, reads=(m0t,), writes=(Mlt,))
                        pa, pat = next_psA()
                        for h in range(NH):
                            hs_ = slice(h * 64, (h + 1) * 64)
                            B.mm(pa[0:64, hs_], Ml[:, hs_], Tm[:, hs_], True, True, reads=(Mlt, Tmt), writes=(pat,))
                        B.cp("act", Pb, pa[0:64, 0:W], reads=(pat,), writes=(Pbt,))
                        yield
                        if lev < 5:
                            pa2, pat2 = next_psA()
                            for h in range(NH):
                                hs_ = slice(h * 64, (h + 1) * 64)
                                B.mm(pa2[0:64, hs_], Tt[:, hs_], Pb[:, hs_], True, True, reads=(Ttt, Pbt), writes=(pat2,))
                            Tn, Tnt = cbd["Nn"][lev % 2], "cbNn%d" % (lev % 2)
                            B.tt("dve", Tn, pa2[0:64, 0:W], Tm, ALU.add, reads=(pat2, Tmt), writes=(Tnt,))
                            yield
                        pa3, pat3 = next_psA()
                        for h in range(NH):
                            hs_ = slice(h * 64, (h + 1) * 64)
                            B.mm(pa3[0:64, hs_], Pb[:, hs_], Tt[:, hs_], True, True, reads=(Pbt, Ttt), writes=(pat3,))
                        T2, T2t = cbd["Tt"][tb0 + (lev % 2)], "cbTt%d" % (tb0 + (lev % 2))
                        B.tt("dve", T2, pa3[0:64, 0:W], Tt, ALU.add, reads=(pat3, Ttt), writes=(T2t,))
                        Tt, Ttt = T2, T2t
                        if lev < 5:
                            Tm, Tmt = Tn, Tnt
                        yield
                    self_t[c] = (Tt, Ttt)

                def seq(c):
                    zz = c % 2
                    cs = slice(c * CH, (c + 1) * CH)
                    Tt, Ttt = self_t[c]
                    Vt, Vtt = cbd["Vt"][zz], "cbVt%d" % zz
                    KHt, KHtt = cbd["KHt"][zz], "cbKHt%d" % zz
                    BHt, BHtt = cbd["BHt"][zz], "cbBHt%d" % zz
                    Brk, Brkt = cbd["Brk"][zz], "cbBrk%d" % zz
                    Brb, Brbt = cbd["Brb"][zz], "cbBrb%d" % zz
                    Akk, Akkt = cbd["Akk"][zz], "cbAkk%d" % zz
                    Xb, Xbt = cbd["Xb"][zz], "cbXb%d" % zz
                    Ub, Ubt = cbd["Ub"][zz], "cbUb%d" % zz
                    pa, pat = pd2, "psD"
                    for h in range(NH):
                        hs_ = slice(h * 64, (h + 1) * 64)
                        B.mm(pa[0:64, hs_], O["KAP"][:, h, cs], Hbd[:, h, :], True, False, reads=(OT["KAP"], "Hb"), writes=(pat,))
                        B.mm(pa[0:64, hs_], Akk[:, hs_], Vt[:, hs_], False, True, reads=(Akkt, Vtt), writes=(pat,))
                    B.cp("act", Xb, pa[0:64, 0:W], reads=(pat,), writes=(Xbt,))
                    yield
                    pa, pat = pd2, "psD"
                    for h in range(NH):
                        hs_ = slice(h * 64, (h + 1) * 64)
                        B.mm(pa[0:64, hs_], Tt[:, hs_], Xb[:, hs_], True, True, reads=(Ttt, Xbt), writes=(pat,))
                    B.cp("act", Ub, pa[0:64, 0:W], reads=(pat,), writes=(Ubt,))
                    yield
                    pa, pat = pd2, "psD"
                    for h in range(NH):
                        hs_ = slice(h * 64, (h + 1) * 64)
                        B.mm(pa[0:64, hs_], Hbd[:, h, :], O["RHO"][:, h, cs], True, False, reads=("Hb", OT["RHO"]), writes=(pat,))
                        B.mm(pa[0:64, hs_], Vt[:, hs_], Brk[:, hs_], False, False, reads=(Vtt, Brkt), writes=(pat,))
                        B.mm(pa[0:64, hs_], Ub[:, hs_], Brb[:, hs_], False, True, reads=(Ubt, Brbt), writes=(pat,))
                    B.cp("act", YF[z][:, :, cs], v3(pa[0:64, 0:W]), reads=(pat,), writes=("YF%d" % z,))
                    yield
                    pa, pat = pd2, "psD"
                    for h in range(NH):
                        hs_ = slice(h * 64, (h + 1) * 64)
                        B.mm(pa[0:64, hs_], KHt[:, hs_], Vt[:, hs_], True, False, reads=(KHtt, Vtt), writes=(pat,))
                        B.mm(pa[0:64, hs_], BHt[:, hs_], Ub[:, hs_], False, True, reads=(BHtt, Ubt), writes=(pat,))
                    B.tt("dve", Hfd, Hfd, pCd[:, :, c:c + 1].to_broadcast([64, NH, DV]), ALU.mult, reads=("Hf", "pC"), writes=("Hf",))
                    B.tt("dve", Hfd, Hfd, v3(pa[0:64, 0:W]), ALU.add, reads=("Hf", pat), writes=("Hf",))
                    B.cp("act", Hbd, Hfd, reads=("Hf",), writes=("Hb",))
                    yield

                self_t = {}
                order = list(range(4)) if fwd else [3, 2, 1, 0]
                yield from pre(order[0])
                for ci, c in enumerate(order):
                    gs = seq(c)
                    if ci + 1 < 4:
                        k_ = 0
                        for _ in pre(order[ci + 1]):
                            yield
                            k_ += 1
                            if k_ % SEQ_EVERY == 0:
                                try:
                                    next(gs)
                                    yield
                                except StopIteration:
                                    pass
                    yield from gs
                B.dma("sp", A["rst_out"][i, d, seg], Hfd, reads=("Hf",), is_out=True)
                B.ts("dve", Hfd, Hfd, flg[ps_, 2:3], None, ALU.mult, reads=("Hf",), writes=("Hf",))
                B.cp("act", Hbd, Hfd, reads=("Hf",), writes=("Hb",))
                B.dma("sp", YS.rearrange("h p t -> p h t")[:, :, ssl], YF[z], reads=("YF%d" % z,), writes=("YS%d" % seg,))
                yield

        gens = [stream(0), stream(1)]
        alive = [0, 1]
        while alive:
            for d in list(alive):
                P.ns = d
                try:
                    next(gens[d])
                except StopIteration:
                    alive.remove(d)
        P.ns = None
        B.fence()
        YA = B.HB[0:64, 4:6, :].rearrange("p k t -> p (k t)").bitcast(F32).rearrange("p (h t) -> p h t", h=NH)
        YBv = B.HB[0:64, 6:8, :].rearrange("p k t -> p (k t)").bitcast(F32).rearrange("p (h t) -> p h t", h=NH)
        glg = xview(0, 7, 0)
        glbv = xview(0, 7, 1)
        nrm = {nm: pt[pn][0:64].rearrange("p h t -> p (h t)")[:, 0:512] for nm, pn in (("a", "inc"), ("b", "inv"), ("c", "exc"))}
        nrmt = dict(a="ptinc", b="ptinv", c="ptexc")
        pa_n = [0]

        def next_psA2():
            q = pa_n[0] % 4
            pa_n[0] += 1
            return psA[q], "psA%d" % q
        for seg in range(NSEG):
            ssl = slice(seg * SEG, (seg + 1) * SEG)
            B.dma("sp", YA, A["YFR"].rearrange("h p t -> p h t")[:, :, ssl], writes=("YA",))
            B.dma("sp", YBv, A["YBR"].rearrange("h p t -> p h t")[:, :, ssl], writes=("YB",))
            B.dma("sp", glg, A["RG"].rearrange("h p t -> p h t")[:, :, ssl], writes=("glg",))
            B.dma("sp", glbv, A["RBV"].rearrange("h p t -> p h t")[:, :, ssl], writes=("glbv",))
            B.tt("dve", YA, YA, YBv, ALU.add, reads=("YA", "YB"), writes=("YA",))
            ab_finish_segment(B, l, "r", True, NH, 64, YA, "YA", {"g": [glg] * 2, "bv": [glbv] * 2}, {"g": "glg", "bv": "glbv"}, 0,
                              nrm, nrmt, nrb, rc2[0:64], C["gnw"], ones1, next_psA2, seg)
        B.fence()
    B.es = old
```

```python
import math
from contextlib import ExitStack
import numpy as np
import ml_dtypes
import concourse.bass as bass
import concourse.mybir as mybir
from concourse.bass_utils import run_bass_kernel_spmd

F32 = mybir.dt.float32
BF16 = mybir.dt.bfloat16
AF = mybir.ActivationFunctionType
ALU = mybir.AluOpType
NPBF = ml_dtypes.bfloat16

D = 1024
T = 2048
DEPTH = 4
DFF = 2816
NM = DFF // 128
ALPHA = (2 * DEPTH) ** 0.25
LN_EPS = 1e-5
NT = T // 512
SEG = 256
NSEG = T // SEG


class Prog:
    ENGS = ("pe", "act", "dve", "pool", "sp")
    NDS = 12

    def __init__(self, nc, same_eng_sync=False):
        self.nc = nc
        self.ops = []
        self.per_eng = {e: [] for e in self.ENGS}
        self.lastw = {}
        self.rd_c = {}
        self.rd_d = {}
        self.dma_rr = {e: 0 for e in self.ENGS}
        self.dma_last = {}
        self.same_eng_sync = same_eng_sync
        self.out_dmas = []
        self.fence_id = None
        self.ns = None

    def op(self, eng, fn, reads=(), writes=(), dma=False, is_out=False):
        i = len(self.ops)
        if self.ns is not None:
            reads = [(self.ns, r) for r in reads]
            writes = [(self.ns, w) for w in writes]
        deps = set()
        for r in reads:
            w = self.lastw.get(r)
            if w is not None:
                deps.add(w)
        for w in writes:
            lw = self.lastw.get(w)
            if lw is not None:
                deps.add(lw)
            deps.update(self.rd_c.get(w, {}).values())
            deps.update(self.rd_d.get(w, ()))
        if self.fence_id is not None:
            deps.add(self.fence_id)
        semslot = None
        if dma:
            k = self.dma_rr[eng]
            self.dma_rr[eng] = (k + 1) % self.NDS
            prev = self.dma_last.get((eng, k))
            if prev is not None:
                deps.add(prev)
            self.dma_last[(eng, k)] = i
            semslot = (eng, k)
        fdeps = set()
        for d in deps:
            od = self.ops[d]
            if od["eng"] == eng and not od["dma"] and not dma and not self.same_eng_sync:
                continue
            if od["eng"] == eng and not od["dma"] and eng == "pe":
                continue
            fdeps.add(d)
            od["flag"] = True
        o = dict(id=i, eng=eng, fn=fn, deps=fdeps, dma=dma, flag=False, semslot=semslot)
        for r in reads:
            if dma:
                self.rd_d.setdefault(r, set()).add(i)
            else:
                self.rd_c.setdefault(r, {})[eng] = i
        for w in writes:
            self.lastw[w] = i
            self.rd_c[w] = {}
            self.rd_d[w] = set()
        self.ops.append(o)
        self.per_eng[eng].append(i)
        if is_out:
            self.out_dmas.append(i)
        return i

    def emit(self):
        nc = self.nc
        ops = self.ops
        fin = dict(id=len(ops), eng="sp", fn=None, deps=set(self.out_dmas), dma=False, flag=False, semslot=None)
        ops.append(fin)
        self.per_eng["sp"].append(fin["id"])
        with ExitStack() as es:
            esem = {e: es.enter_context(nc.semaphore("s_" + e)) for e in ("pe", "act", "dve", "pool")}
            dsem = {}
            for q in ("sp", "pool", "act"):
                for k in range(self.NDS):
                    if (q, k) in self.dma_last:
                        dsem[(q, k)] = es.enter_context(nc.semaphore("d_%s%d" % (q, k)))
            cnt = {e: 0 for e in esem}
            dcnt = {k: 0 for k in dsem}
            semof = {}
            for o in ops:
                if o["dma"]:
                    dcnt[o["semslot"]] += 16
                    semof[o["id"]] = (dsem[o["semslot"]], dcnt[o["semslot"]])
                elif o["flag"]:
                    cnt[o["eng"]] += 1
                    semof[o["id"]] = (esem[o["eng"]], cnt[o["eng"]])
            block = es.enter_context(nc.Block())
            names = dict(pe="tensor", act="scalar", dve="vector", pool="gpsimd", sp="sync")
            for eng in self.ENGS:
                lst = self.per_eng[eng]
                if not lst:
                    continue

                def body(e, lst=lst):
                    waited = {}
                    for i in lst:
                        o = ops[i]
                        need = {}
                        for d in o["deps"]:
                            s, v = semof[d]
                            if need.get(id(s), (None, 0))[1] < v:
                                need[id(s)] = (s, v)
                        for s, v in need.values():
                            if waited.get(id(s), 0) >= v:
                                continue
                            e.wait_ge(s, v)
                            waited[id(s)] = v
                        if o["fn"] is None:
                            continue
                        ins = o["fn"](e)
                        if o["dma"]:
                            ins.then_inc(semof[i][0], 16)
                        elif o["flag"]:
                            ins.then_inc(semof[i][0], 1)

                getattr(block, names[eng])(body)


def drain(*gens):
    alive = list(gens)
    while alive:
        for g in list(alive):
            try:
                next(g)
            except StopIteration:
                alive.remove(g)


def _pk(w, kparts=128):
    K, N = w.shape
    return np.ascontiguousarray(w.reshape(K // kparts, kparts, N).transpose(1, 0, 2))


def _col(v):
    return np.ascontiguousarray(v.reshape(-1, 128).T)


class Builder:
    def __init__(self, cfg):
        self.cfg = cfg
        self.nc = bass.Bass("TRN2", target_bir_lowering=False)
        self.P = Prog(self.nc, same_eng_sync=cfg.get("same_eng_sync", False))
        self.es = ExitStack()
        self.din = {}
        self.dout = {}
        self.uid = 0

    def inp(self, name, shape, dt=F32):
        self.din[name] = self.nc.dram_tensor(name, list(shape), dt, kind="ExternalInput").ap()
        return self.din[name]

    def outp(self, name, shape, dt=F32):
        self.dout[name] = self.nc.dram_tensor(name, list(shape), dt, kind="ExternalOutput").ap()
        return self.dout[name]

    def scratch(self, name, shape, dt=F32):
        return self.nc.dram_tensor(name, list(shape), dt).ap()

    def sb(self, name, shape, dt=F32):
        self.uid += 1
        return self.es.enter_context(self.nc.sbuf_tensor("%s_%d" % (name, self.uid), list(shape), dt))

    def ps(self, name, shape, dt=F32):
        self.uid += 1
        return self.es.enter_context(self.nc.psum_tensor("%s_%d" % (name, self.uid), list(shape), dt))

    def dma(self, q, out, in_, reads=(), writes=(), is_out=False):
        return self.P.op(q, lambda e, out=out, in_=in_: e.dma_start(out=out, in_=in_), reads, writes, dma=True, is_out=is_out)

    def mm(self, out, lhsT, rhs, start, stop, reads=(), writes=()):
        return self.P.op("pe", lambda e, out=out, lhsT=lhsT, rhs=rhs, start=start, stop=stop:
                         e.matmul(out, lhsT, rhs, start=start, stop=stop), reads, writes)

    def tr(self, out, in_, ident, reads=(), writes=()):
        return self.P.op("pe", lambda e, out=out, in_=in_, ident=ident: e.transpose(out, in_, ident), reads, writes)

    def act(self, out, in_, func, bias=0.0, scale=1.0, reads=(), writes=(), eng="act"):
        return self.P.op(eng, lambda e, out=out, in_=in_, func=func, bias=bias, scale=scale:
                         e.activation(out=out, in_=in_, func=func, bias=bias, scale=scale), reads, writes)

    def ts(self, eng, out, in0, s1, s2, op0, op1=None, reads=(), writes=()):
        if op1 is None:
            return self.P.op(eng, lambda e, out=out, in0=in0, s1=s1, op0=op0:
                             e.tensor_scalar(out=out, in0=in0, scalar1=s1, scalar2=None, op0=op0), reads, writes)
        return self.P.op(eng, lambda e, out=out, in0=in0, s1=s1, s2=s2, op0=op0, op1=op1:
                         e.tensor_scalar(out=out, in0=in0, scalar1=s1, scalar2=s2, op0=op0, op1=op1), reads, writes)

    def tt(self, eng, out, in0, in1, op, reads=(), writes=()):
        return self.P.op(eng, lambda e, out=out, in0=in0, in1=in1, op=op:
                         e.tensor_tensor(out=out, in0=in0, in1=in1, op=op), reads, writes)

    def stt(self, eng, out, in0, scalar, in1, op0, op1, reads=(), writes=()):
        return self.P.op(eng, lambda e, out=out, in0=in0, scalar=scalar, in1=in1, op0=op0, op1=op1:
                         e.scalar_tensor_tensor(out=out, in0=in0, scalar=scalar, in1=in1, op0=op0, op1=op1), reads, writes)

    def cp(self, eng, out, in_, reads=(), writes=()):
        if eng == "act":
            return self.P.op(eng, lambda e, out=out, in_=in_: e.copy(out=out, in_=in_), reads, writes)
        return self.P.op(eng, lambda e, out=out, in_=in_: e.tensor_copy(out=out, in_=in_), reads, writes)

    def recip(self, out, in_, reads=(), writes=()):
        return self.P.op("dve", lambda e, out=out, in_=in_: e.reciprocal(out=out, in_=in_), reads, writes)

    def memset(self, eng, ap, val, writes=()):
        return self.P.op(eng, lambda e, ap=ap, val=val: e.memset(ap, val), (), writes)

    def fence(self):
        P = self.P
        deps_tokens = list(P.lastw.keys())
        i = P.op("dve", lambda e, ap=self.dummy[:, 0:1]: e.memset(ap, 0.0), reads=(), writes=tuple(deps_tokens))
        P.lastw = {}
        P.rd_c = {}
        P.rd_d = {}
        P.fence_id = i

    def R(self, *toks):
        return tuple(toks)


def build(cfg):
    B = Builder(cfg)
    nc, P = B.nc, B.P
    mixers = cfg.get("mixers", True)
    nlayers = cfg.get("nlayers", DEPTH)

    xT = B.inp("xT", [D, T])
    posT = B.inp("posT", [D, T])
    condc = B.inp("condc", [128, 8])
    flags = B.inp("flags", [128, 4])
    wmod = B.inp("wmod", [DEPTH, 12, 128, 8, 512])
    bmod = B.inp("bmod", [DEPTH, 128, 48])
    lng = B.inp("lng", [128, DEPTH * 2 * 8])
    lnb = B.inp("lnb", [128, DEPTH * 2 * 8])
    wg = B.inp("wg", [DEPTH, 11, 128, 8, 256])
    wu = B.inp("wu", [DEPTH, 11, 128, 8, 256])
    wd = B.inp("wd", [DEPTH, 8, 128, NM, 128])
    yT = B.outp("yT", [D, T])
    B.io = dict(xT=xT, posT=posT, condc=condc, flags=flags)

    X = B.sb("X", [128, 8, T], F32)
    HB = B.sb("HB", [128, 8, T], BF16)
    modc = B.sb("modc", [128, DEPTH * 48], F32)
    lngs = B.sb("lngs", [128, DEPTH * 16], F32)
    lnbs = B.sb("lnbs", [128, DEPTH * 16], F32)
    flg = B.sb("flg", [128, 4], F32)
    ones_bf = B.sb("ones_bf", [128, 128], BF16)
    B.dummy = B.sb("fdummy", [128, 2], F32)
    B.eps_ln = B.sb("eps_ln", [128, 3], F32)
    B.memset("dve", B.eps_ln[:, 0:1], LN_EPS, writes=("epsln",))
    B.memset("dve", B.eps_ln[:, 1:2], 1e-24, writes=("epsln",))
    B.memset("dve", B.eps_ln[:, 2:3], 64e-5, writes=("epsln",))
    B.X, B.HB, B.modc, B.flg, B.ones_bf = X, HB, modc, flg, ones_bf
    B.lngs, B.lnbs = lngs, lnbs

    def xtok(k, tt):
        return "X%d:%d" % (k, tt)

    def htok(k, tt):
        return "H%d:%d" % (k, tt)
    B.xtok, B.htok = xtok, htok

    B.dma("sp", lngs[:], lng, writes=("lng",))
    B.dma("sp", lnbs[:], lnb, writes=("lnb",))
    B.dma("sp", flg[:], flags, writes=("flg",))
    B.memset("dve", ones_bf[:], 1.0 / 1024.0, writes=("ones",))
    xv = xT.rearrange("(k p) t -> p k t", p=128)
    pv = posT.rearrange("(k p) t -> p k t", p=128)
    with ExitStack() as ph:
        B.es, old_es = ph, B.es
        pst = [B.sb("pst%d" % i, [128, T], F32) for i in range(2)]
        cnd = B.sb("cnd", [128, 8], F32)
        scb = B.sb("scb", [128, 8], BF16)
        bms = B.sb("bms", [128, DEPTH * 48], F32)
        wms = [B.sb("wms%d" % i, [128, 8, 512], BF16) for i in range(2)]
        psM = B.ps("psM", [128, 512], F32)
        for k in range(8):
            B.dma("sp", X[:, k, :], xv[:, k, :], writes=[xtok(k, tt) for tt in range(NT)])
            B.dma("sp", pst[k % 2][:], pv[:, k, :], writes=("pst%d" % (k % 2),))
            B.tt("dve", X[:, k, :], X[:, k, :], pst[k % 2][:], ALU.add,
                 reads=["pst%d" % (k % 2)] + [xtok(k, tt) for tt in range(NT)], writes=[xtok(k, tt) for tt in range(NT)])
        B.dma("sp", cnd[:], condc, writes=("cnd",))
        for l in range(DEPTH):
            B.dma("sp", bms[:, l * 48:(l + 1) * 48], bmod[l], writes=("bms%d" % l,))
        B.act(scb[:], cnd[:], AF.Silu, reads=("cnd",), writes=("scb",))
        n = 0
        for l in range(DEPTH):
            for pn in range(12):
                s = n % 2
                n += 1
                B.dma("pool", wms[s][:], wmod[l, pn], writes=("wms%d" % s,))
                for jj in range(4):
                    col = l * 48 + pn * 4 + jj
                    for k in range(8):
                        B.mm(psM[:, col:col + 1], wms[s][:, k, jj * 128:(jj + 1) * 128], scb[:, k:k + 1], k == 0, k == 7,
                             reads=("wms%d" % s, "scb"), writes=("psM",))
        B.tt("dve", modc[:], psM[:, 0:DEPTH * 48], bms[:], ALU.add,
             reads=["psM"] + ["bms%d" % l for l in range(DEPTH)], writes=("modc",))
        for l in range(DEPTH):
            for g in (1, 4):
                c0 = l * 48 + g * 8
                B.ts("dve", modc[:, c0:c0 + 8], modc[:, c0:c0 + 8], 1.0, None, ALU.add, reads=("modc",), writes=("modc",))
        B.fence()
    B.es = old_es
    make_ident(B)
    if mixers and nlayers > 1:
        hyena_filters(B)
    for k in range(8):
        B.ts("dve", HB[:, k, :], X[:, k, :], modc[:, 8 + k:9 + k], modc[:, k:k + 1], ALU.mult, ALU.add,
             reads=["modc"] + [xtok(k, tt) for tt in range(NT)], writes=[htok(k, tt) for tt in range(NT)])

    def ln_tile(lnps, l, which, tt, nxt):
        for _ in ln_tile_gen(lnps, l, which, tt, nxt):
            pass

    def ln_tile_gen(lnps, l, which, tt, nxt):
        ybf, ysq, mean, msq, rstd, nmr, tmp, ps1, ps2 = lnps
        tsl = slice(tt * 512, (tt + 1) * 512)
        for k in range(8):
            B.cp("act", ybf[:, k, :], X[:, k, tsl], reads=B.R(xtok(k, tt)), writes=("ybf%d" % k,))
            B.act(ysq[:, k, :], X[:, k, tsl], AF.Square, reads=B.R(xtok(k, tt)), writes=("ysq%d" % k,))
            if k % 2 == 1:
                yield
        for k in range(8):
            B.mm(ps1[:], ones_bf[:], ybf[:, k, :], k == 0, k == 7, reads=B.R("ones", "ybf%d" % k), writes=("lnps1",))
        for k in range(8):
            B.mm(ps2[:], ones_bf[:], ysq[:, k, :], k == 0, k == 7, reads=B.R("ones", "ysq%d" % k), writes=("lnps2",))
        yield
        B.cp("act", mean[:], ps1[:], reads=B.R("lnps1"), writes=("mean",))
        B.tt("dve", msq[:], mean[:], mean[:], ALU.mult, reads=B.R("mean"), writes=("msq",))
        B.tt("dve", rstd[:], ps2[:], msq[:], ALU.subtract, reads=B.R("lnps2", "msq"), writes=("rstd",))
        B.act(rstd[:], rstd[:], AF.Ln, bias=B.eps_ln[:, 0:1], reads=B.R("rstd"), writes=("rstd",))
        B.act(rstd[:], rstd[:], AF.Exp, scale=-0.5, reads=B.R("rstd"), writes=("rstd",))
        B.stt("dve", nmr[:], mean[:], -1.0, rstd[:], ALU.mult, ALU.mult, reads=B.R("mean", "rstd"), writes=("nmr",))
        yield
        for k in range(8):
            tk = "lntmp%d" % (k % 2)
            tm = tmp[k % 2]
            B.tt("dve", tm[:], X[:, k, tsl], rstd[:], ALU.mult, reads=B.R(xtok(k, tt), "rstd"), writes=(tk,))
            B.tt("dve", tm[:], tm[:], nmr[:], ALU.add, reads=B.R(tk, "nmr"), writes=(tk,))
            c = l * 16 + which * 8 + k
            B.act(X[:, k, tsl], tm[:], AF.Identity, bias=lnbs[:, c:c + 1], scale=lngs[:, c:c + 1],
                  reads=B.R(tk, "lng", "lnb"), writes=(xtok(k, tt),))
            if nxt is not None:
                nl, g = nxt
                c1 = nl * 48 + (g + 1) * 8 + k
                c0 = nl * 48 + g * 8 + k
                B.ts("dve", HB[:, k, tsl], X[:, k, tsl], modc[:, c1:c1 + 1], modc[:, c0:c0 + 1], ALU.mult, ALU.add,
                     reads=B.R(xtok(k, tt), "modc"), writes=(htok(k, tt),))
            yield

    def alloc_ln():
        ybf = B.sb("ybf", [128, 8, 512], BF16)
        ysq = B.sb("ysq", [128, 8, 512], BF16)
        mean = B.sb("mean", [128, 512], F32)
        msq = B.sb("msq", [128, 512], F32)
        rstd = B.sb("rstd", [128, 512], F32)
        nmr = B.sb("nmr", [128, 512], F32)
        tmp = [B.sb("lntmp%d" % i, [128, 512], F32) for i in range(2)]
        ps1 = B.ps("lnps1", [128, 512], F32)
        ps2 = B.ps("lnps2", [128, 512], F32)
        return (ybf, ysq, mean, msq, rstd, nmr, tmp, ps1, ps2)
    B.ln_tile, B.alloc_ln, B.ln_tile_gen = ln_tile, alloc_ln, ln_tile_gen

    def ffn(l):
        with ExitStack() as ph:
            B.es, old = ph, B.es
            lnps = alloc_ln()
            A = B.sb("A", [128, NM, 1024], BF16)
            wgs = [B.sb("wgs%d" % i, [128, 8, 256], BF16) for i in range(2)]
            wus = [B.sb("wus%d" % i, [128, 8, 256], BF16) for i in range(2)]
            wds = [B.sb("wds%d" % i, [128, NM, 128], BF16) for i in range(2)]
            sg = [B.sb("sg%d" % i, [128, 512], F32) for i in range(2)]
            dtm = [B.sb("dtm%d" % i, [128, 512], F32) for i in range(2)]
            psG = [B.ps("psG%d" % i, [128, 512], F32) for i in range(2)]
            psU = [B.ps("psU%d" % i, [128, 512], F32) for i in range(2)]
            psD = [B.ps("psD%d" % i, [128, 512], F32) for i in range(2)]
            st = dict(wn=0, pn=0, dn=0)

            def G(half):
                tiles = (2 * half, 2 * half + 1)
                for pn in range(11):
                    s = st["wn"] % 2
                    st["wn"] += 1
                    B.dma("pool", wgs[s][:], wg[l, pn], reads=B.R(), writes=("wgs%d" % s,))
                    B.dma("pool", wus[s][:], wu[l, pn], reads=B.R(), writes=("wus%d" % s,))
                    for mi in range(2):
                        m = pn * 2 + mi
                        for ti, tt in enumerate(tiles):
                            tsl = slice(tt * 512, (tt + 1) * 512)
                            q = st["pn"] % 2
                            st["pn"] += 1
                            for k in range(8):
                                B.mm(psG[q][:], wgs[s][:, k, mi * 128:(mi + 1) * 128], HB[:, k, tsl], k == 0, k == 7,
                                     reads=B.R("wgs%d" % s, htok(k, tt)), writes=("psG%d" % q,))
                            for k in range(8):
                                B.mm(psU[q][:], wus[s][:, k, mi * 128:(mi + 1) * 128], HB[:, k, tsl], k == 0, k == 7,
                                     reads=B.R("wus%d" % s, htok(k, tt)), writes=("psU%d" % q,))
                            B.act(sg[q][:], psG[q][:], AF.Silu, reads=B.R("psG%d" % q), writes=("sg%d" % q,))
                            B.tt("dve", A[:, m, ti * 512:(ti + 1) * 512], sg[q][:], psU[q][:], ALU.mult,
                                 reads=B.R("sg%d" % q, "psU%d" % q), writes=("A%d:%d" % (m, ti),))
                            yield

            def Dn(half):
                tiles = (2 * half, 2 * half + 1)
                for dch in range(8):
                    s = st["dn"] % 2
                    st["dn"] += 1
                    B.dma("pool", wds[s][:], wd[l, dch], reads=B.R(), writes=("wds%d" % s,))
                    for ti, tt in enumerate(tiles):
                        tsl = slice(tt * 512, (tt + 1) * 512)
                        q = st["pn"] % 2
                        st["pn"] += 1
                        for m in range(NM):
                            B.mm(psD[q][:], wds[s][:, m, :], A[:, m, ti * 512:(ti + 1) * 512], m == 0, m == NM - 1,
                                 reads=B.R("wds%d" % s, "A%d:%d" % (m, ti)), writes=("psD%d" % q,))
                        c = l * 48 + 5 * 8 + dch
                        B.act(dtm[q][:], psD[q][:], AF.Identity, scale=modc[:, c:c + 1], reads=B.R("psD%d" % q, "modc"), writes=("dtm%d" % q,))
                        B.stt("dve", X[:, dch, tsl], X[:, dch, tsl], ALPHA, dtm[q][:], ALU.mult, ALU.add,
                              reads=B.R(xtok(dch, tt), "dtm%d" % q), writes=(xtok(dch, tt),))
                        yield

            def LNh(half):
                for tt in (2 * half, 2 * half + 1):
                    yield from ln_tile_gen(lnps, l, 1, tt, (l + 1, 0) if l + 1 < DEPTH else None)

            drain(G(0))
            drain(Dn(0))
            drain(G(1), LNh(0))
            drain(Dn(1))
            drain(LNh(1))
            B.fence()
        B.es = old
    B.ffn = ffn

    def null_mixer(l):
        with ExitStack() as ph:
            B.es, old = ph, B.es
            lnps = alloc_ln()
            for tt in range(NT):
                tsl = slice(tt * 512, (tt + 1) * 512)
                for k in range(8):
                    B.ts("dve", X[:, k, tsl], X[:, k, tsl], ALPHA, None, ALU.mult, reads=B.R(xtok(k, tt)), writes=(xtok(k, tt),))
                ln_tile(lnps, l, 0, tt, (l, 3))
            B.fence()
        B.es = old

    for l in range(nlayers):
        if mixers and l % 2 == 0 and "ab_mixer" in globals():
            ab_mixer(B, l)
        elif mixers and l % 2 == 1 and "hyena_mixer" in globals():
            hyena_mixer(B, l)
        else:
            null_mixer(l)
        ffn(l)

    yv = yT.rearrange("(k p) t -> p k t", p=128)
    for k in range(8):
        B.dma("sp", yv[:, k, :], X[:, k, :], reads=B.R(*[xtok(k, tt) for tt in range(NT)]), is_out=True)
    P.emit()
    return B


_CACHE = {}


def _grid_pos_T():
    quarter = D // 4
    omega = (1.0 / (10000.0 ** (np.arange(quarter, dtype=np.float32) / np.float32(quarter)))).astype(np.float32)
    idx = np.arange(T)

    def sincos(pos):
        ang = pos.astype(np.float32)[:, None] * omega[None, :]
        return np.concatenate([np.sin(ang), np.cos(ang)], -1)
    pe = np.concatenate([sincos(idx // 64), sincos(idx % 64)], -1).astype(np.float32)
    return np.ascontiguousarray(pe.T)


def prep_shared(inp, cfg):
    f = lambda a: np.ascontiguousarray(np.asarray(a, dtype=np.float32))
    sh = {}
    sh["wmod"] = f(inp["w_mod"].reshape(DEPTH, 8, 128, 12, 512).transpose(0, 3, 2, 1, 4))
    sh["bmod"] = f(inp["b_mod"].reshape(DEPTH, 48, 128).transpose(0, 2, 1))
    sh["lng"] = f(inp["ln_g"].reshape(DEPTH, 2, 8, 128).transpose(3, 0, 1, 2).reshape(128, DEPTH * 16))
    sh["lnb"] = f(inp["ln_b"].reshape(DEPTH, 2, 8, 128).transpose(3, 0, 1, 2).reshape(128, DEPTH * 16))
    sh["wg"] = f(inp["ffn_w_gate"].reshape(DEPTH, 8, 128, 11, 256).transpose(0, 3, 2, 1, 4))
    sh["wu"] = f(inp["ffn_w_up"].reshape(DEPTH, 8, 128, 11, 256).transpose(0, 3, 2, 1, 4))
    sh["wd"] = f(inp["ffn_w_down"].reshape(DEPTH, NM, 128, 8, 128).transpose(0, 3, 2, 1, 4))
    wi = inp["ab_w_in"]
    abwin = np.zeros((2, NG, 128, 8, 128), np.float32)
    abmu = np.zeros((2, 128, NG * 2), np.float32)
    for n, (nm, c0, wdt) in enumerate(AB_GROUPS):
        abwin[:, n, :, :, 0:wdt] = wi[:, :, c0:c0 + wdt].reshape(2, 8, 128, wdt).transpose(0, 2, 1, 3)
        if nm[0] == "r":
            r0 = c0 - 1568
            abmu[:, 0:wdt, 2 * n] = inp["rwkv_shift_mu"][:, 0, r0:r0 + wdt]
            abmu[:, 0:wdt, 2 * n + 1] = inp["rwkv_shift_mu"][:, 1, r0:r0 + wdt]
    sh["abwin"] = abwin
    sh["abmu"] = abmu
    wo = inp["ab_w_out"]
    sh["abwo_g"] = f(wo[:, 0:512, :].reshape(2, 4, 128, 8, 128).transpose(0, 3, 2, 1, 4))
    sh["abwo_r"] = f(wo[:, 512:1024, :].reshape(2, 8, 64, 8, 128).transpose(0, 3, 2, 1, 4))
    w2bd = np.zeros((2, 128, 1024), np.float32)
    w2bd[:, 0:64, 0:512] = inp["rwkv_w2"][:, 0]
    w2bd[:, 64:128, 512:1024] = inp["rwkv_w2"][:, 1]
    sh["w2bd"] = w2bd
    sh["w0row"] = f(inp["rwkv_w0"].reshape(2, 1, 1024))
    sh["a2s"] = f(inp["rwkv_a2"].reshape(2, 128, 512))
    sh["a0c"] = f(inp["rwkv_a0"].reshape(2, 2, 8, 64).transpose(0, 3, 1, 2).reshape(2, 64, 16))
    sh["g2a"] = f(inp["rwkv_g2"][:, 0:128])
    sh["g2b"] = f(inp["rwkv_g2"][:, 128:160])
    wdbd = np.zeros((2, 32, 512), np.float32)
    wdbd[:, 0:16, 0:256] = inp["gla_w_decay"][:, 0]
    wdbd[:, 16:32, 256:512] = inp["gla_w_decay"][:, 1]
    sh["wdbd"] = wdbd
    sh["gbrow"] = f(inp["gla_b_decay"].reshape(2, 1, 512))
    rcols = np.zeros((2, 64, 40), np.float32)
    rcols[:, :, 0:8] = inp["rwkv_k_k"].reshape(2, 8, 64).transpose(0, 2, 1)
    rcols[:, :, 8:16] = inp["rwkv_k_a"].reshape(2, 8, 64).transpose(0, 2, 1)
    rcols[:, :, 16:24] = inp["rwkv_r_k"].transpose(0, 2, 1)
    rcols[:, :, 24:32] = inp["rwkv_ln_w"].reshape(2, 8, 64).transpose(0, 2, 1)
    rcols[:, :, 32:40] = inp["rwkv_ln_b"].reshape(2, 8, 64).transpose(0, 2, 1)
    sh["rcols"] = rcols
    sh["gnw"] = f(inp["gla_norm_w"].reshape(2, 128, 1))
    ii = np.arange(128)
    same = (ii[:, None] // 64) == (ii[None, :] // 64)
    S_, T_ = ii[:, None], ii[None, :]
    sh["trimats"] = np.stack([(same & (S_ <= T_)), (same & (S_ < T_)), (same & (S_ > T_)), (same & (S_ >= T_))]).astype(np.float32)
    jj = np.arange(64)
    Rw, Cl = jj[:, None], jj[None, :]
    mk = np.stack([(Rw < Cl), (Rw <= Cl), (Rw > Cl), (Rw >= Cl)]).astype(np.float32)
    sh["cmasks"] = np.ascontiguousarray(np.tile(mk, (1, 1, 8))).astype(NPBF)
    sh["identrep"] = np.ascontiguousarray(np.tile(np.eye(64, dtype=np.float32), (1, 8))).astype(NPBF)
    lv = np.stack([((Rw // (2 << k)) == (Cl // (2 << k))) & ((Rw // (1 << k)) != (Cl // (1 << k))) for k in range(6)], 1)
    sh["lvlmask"] = np.ascontiguousarray(lv.astype(np.float32)).astype(NPBF)
    sh["hwin"] = f(inp["hy_w_in"].reshape(2, 8, 128, 24, 128).transpose(0, 3, 2, 1, 4))
    sh["hcw"] = f(inp["hy_conv_w"].reshape(2, 3, 24, 128).transpose(0, 3, 2, 1).reshape(2, 128, 72))
    sh["hcb"] = f(inp["hy_conv_b"].reshape(2, 24, 128).transpose(0, 2, 1))
    sh["hwout"] = f(inp["hy_w_out"].reshape(2, 8, 128, 8, 128).transpose(0, 3, 2, 1, 4))
    sh["hfw1"] = f(inp["hy_f_w1"])
    sh["hfw2"] = f(inp["hy_f_w2"])
    sh["hfw3"] = f(inp["hy_f_w3"])
    sh["hfw4"] = f(inp["hy_f_w4"])
    sh["hffq"] = f(inp["hy_f_freq"].transpose(0, 2, 1))
    sh["hffb"] = f(np.stack([inp["hy_f_b1"], inp["hy_f_b2"], inp["hy_f_b3"]], -1))
    sh["hbias"] = f(np.broadcast_to(inp["hy_bias"][:, None, :], (2, 128, D)))
    return sh


def _hy_consts(kind):
    key = "hyc" + kind
    if key in _CACHE:
        return _CACHE[key]
    L = T if kind == "s" else SEG
    nseg = T // L
    N = 2 * L
    tl = np.linspace(0.0, 1.0, L, dtype=np.float32)
    w = (2.0 * np.pi * np.arange(L, dtype=np.float32) / np.float32(L)).astype(np.float32)
    f = np.linspace(1e-4, 15.0, 16, dtype=np.float32)[None, :]
    z = np.concatenate([tl[:, None], np.cos(f * w[:, None]), -np.sin(f * w[:, None])], -1).astype(np.float32)
    z = np.tile(z, (nseg, 1))
    tfull = np.tile(tl, nseg)
    first = (np.arange(T) % L == 0)
    c = {}
    c["ztab"] = np.ascontiguousarray(z.T)
    c["tcol"] = np.ascontiguousarray(tfull.reshape(16, 128).T)
    c["m0col"] = np.ascontiguousarray((~first).astype(np.float32).reshape(16, 128).T)
    c["d0col"] = np.ascontiguousarray(first.astype(np.float32).reshape(16, 128).T)
    mx = math.log(1e-2) / 0.3
    mn = math.log(1e-2) / 1.5
    deltas = np.abs(np.linspace(mn, mx, D, dtype=np.float32))
    c["ndelta"] = np.ascontiguousarray(np.tile(-deltas[None, :], (128, 1)).astype(np.float32))
    tt = np.arange(L, dtype=np.int64)[:, None]
    ff = np.arange(L, dtype=np.int64)[None, :]
    ang = (((2 * ff + 1) * tt) % (2 * N)).astype(np.float64) * (2.0 * np.pi / (2 * N))
    Fc = np.zeros((T, T), np.float32)
    Fsn = np.zeros((T, T), np.float32)
    for sgi in range(nseg):
        sl = slice(sgi * L, (sgi + 1) * L)
        Fc[sl, sl] = np.cos(ang)
        Fsn[sl, sl] = -np.sin(ang)
    F2 = np.stack([Fc, Fsn], 0)
    Fh = F2.reshape(2, 16, 128, 16, 128).transpose(3, 2, 0, 1, 4)
    c["Fh"] = np.ascontiguousarray(Fh).astype(NPBF)
    G = np.concatenate([Fc.T, Fsn.T], 0) * np.float32(2.0 / N)
    c["Gh"] = np.ascontiguousarray(G.reshape(32, 128, T)).astype(NPBF)
    _CACHE[key] = c
    return c


def core_assignment():
    return [("s", 0), ("s", 1), ("s", 2), ("s", 3), ("p", 0), ("p", 1), ("p", 0), ("p", 1)]


def prep_core(inp, kind, idx, cfg):
    f = lambda a: np.ascontiguousarray(np.asarray(a, dtype=np.float32))
    m = {}
    if kind == "s":
        m["xT"] = f(inp["x_sample"][idx].T)
        m["posT"] = _CACHE.setdefault("pos", _grid_pos_T())
        m["condc"] = _col(f(inp["c"][idx]))
        m["gstate"] = f(inp["state_gla"][idx].transpose(0, 1, 3, 2, 4))
        m["rstate"] = f(inp["state_rwkv"][idx].transpose(0, 1, 4, 2, 3))
        pm = 0.0
    else:
        xs = inp["x_prompt"][idx * 8:(idx + 1) * 8].reshape(T, D)
        m["xT"] = f(xs.T)
        m["posT"] = _CACHE.setdefault("zpos", np.zeros((D, T), np.float32))
        m["condc"] = _col(f(inp["c_ctx"]))
        m["gstate"] = _CACHE.setdefault("zg", np.zeros((2, 2, 64, 4, 128), np.float32))
        m["rstate"] = _CACHE.setdefault("zr", np.zeros((2, 2, 64, 8, 64), np.float32))
        pm = 1.0
    m.update(_hy_consts(kind))
    fl = np.zeros((128, 4), np.float32)
    fl[:, 0] = pm
    fl[:, 1] = -pm
    fl[:, 2] = 1.0 - pm
    m["flags"] = fl
    return m


def run(inputs, cfg):
    key = repr(sorted(cfg.items()))
    if key not in _CACHE:
        _CACHE[key] = build(cfg)
    B = _CACHE[key]
    sh = prep_shared(inputs, cfg)
    in_maps = []
    for kind, idx in core_assignment():
        m = dict(sh)
        m.update(prep_core(inputs, kind, idx, cfg))
        in_maps.append({k: v for k, v in m.items() if k in B.din})
    res = run_bass_kernel_spmd(B.nc, in_maps, core_ids=list(range(8)))
    return res.results


def assemble(results, inputs):
    y_sample = np.stack([results[b]["yT"].T for b in range(4)], 0).astype(np.float32)
    yp = [results[4 + c]["yT"].T.reshape(8, SEG, D) for c in range(2)]
    y_prompt = np.concatenate(yp, 0).astype(np.float32)
    if "gst_out" not in results[4]:
        return y_prompt, y_sample
    gs = np.concatenate([results[4 + c]["gst_out"].transpose(2, 0, 1, 4, 3, 5) for c in range(2)], 0).astype(np.float32)
    rs = np.concatenate([results[4 + c]["rst_out"].transpose(2, 0, 1, 4, 5, 3) for c in range(2)], 0).astype(np.float32)
    return y_prompt, y_sample, np.ascontiguousarray(gs), np.ascontiguousarray(rs)


def kernel(**inputs):
    inputs = {k: np.asarray(v) for k, v in inputs.items()}
    cfg = dict(mixers=True)
    results = run(inputs, cfg)
    return assemble(results, inputs)


PI = float(np.pi)


def make_ident(B):
    idf = B.sb("idf", [128, 128], F32)
    idb = B.sb("idb", [128, 128], BF16)
    B.memset("pool", idf[:], 0.0, writes=("idf",))
    B.P.op("pool", lambda e: e.affine_select(out=idf[:], in_=idf[:], pattern=[[-1, 128]], compare_op=ALU.not_equal,
                                             fill=1.0, base=0, channel_multiplier=1), reads=("idf",), writes=("idf",))
    B.cp("pool", idb[:], idf[:], reads=("idf",), writes=("idb",))
    B.idf, B.idb = idf, idb


def hyena_filters(B):
    nc = B.nc
    ztab = B.inp("ztab", [33, T])
    tcol = B.inp("tcol", [128, 16])
    m0col = B.inp("m0col", [128, 16])
    d0col = B.inp("d0col", [128, 16])
    ndelta = B.inp("ndelta", [128, D])
    hfw1 = B.inp("hfw1", [2, 33, 64])
    hfw2 = B.inp("hfw2", [2, 64, 64])
    hfw3 = B.inp("hfw3", [2, 64, 64])
    hfw4 = B.inp("hfw4", [2, 64, 2 * D])
    hffq = B.inp("hffq", [2, 64, 3])
    hffb = B.inp("hffb", [2, 64, 3])
    hbias = B.inp("hbias", [2, 128, D])
    Fh = B.inp("Fh", [16, 128, 2, 16, 128], BF16)
    B.Fh = Fh
    if B.cfg.get("debug"):
        HF = B.outp("HF", [2, 16, 128, 2, D], BF16)
    else:
        HF = B.scratch("HF", [2, 16, 128, 2, D], BF16)
    B.HF = HF
    with ExitStack() as ph:
        B.es, old = ph, B.es
        zt = B.sb("zt", [33, T], F32)
        tc = B.sb("tc", [128, 16], F32)
        m0 = B.sb("m0", [128, 16], F32)
        d0 = B.sb("d0", [128, 16], F32)
        ndl = B.sb("ndl", [128, D], F32)
        hA = B.sb("hA", [64, T], F32)
        hBf = B.sb("hBf", [64, T], F32)
        w1 = B.sb("w1", [33, 64], F32)
        w2 = B.sb("w2", [64, 64], F32)
        w3 = B.sb("w3", [64, 64], F32)
        w4 = B.sb("w4", [64, 2 * D], F32)
        fq = B.sb("fq", [64, 3], F32)
        fb = B.sb("fb", [64, 3], F32)
        hbb = B.sb("hbb", [128, D], F32)
        dec = B.sb("dec", [128, D], F32)
        hf = B.sb("hf", [128, D], F32)
        hb = B.sb("hb", [128, D], F32)
        arg = [B.sb("arg0", [64, 512], F32)] * 2
        wr = [B.sb("wr0", [64, 512], F32)] * 2
        HS = B.HB[:].rearrange("p k (h c) -> p (k h) c", c=D)
        HD = B.sb("HD", [128, 16, D], BF16)
        Fs = [B.sb("Fs%d" % i, [128, 2, 16, 128], BF16) for i in range(2)]
        hst = [B.sb("hst0", [128, 2, D], BF16)] * 2
        psm = [B.ps("psm%d" % i, [128, 512], F32) for i in range(2)]
        ps4 = [B.ps("ps4%d" % i, [128, 512], F32) for i in range(4)]
        B.dma("sp", zt[:], ztab, writes=("zt",))
        B.dma("sp", tc[:], tcol, writes=("tc",))
        B.dma("sp", m0[:], m0col, writes=("m0",))
        B.dma("sp", d0[:], d0col, writes=("d0",))
        B.dma("sp", ndl[:], ndelta, writes=("ndl",))
        fcnt = 0
        for i in range(2):
            B.dma("sp", w1[:], hfw1[i], writes=("w1",))
            B.dma("sp", w2[:], hfw2[i], writes=("w2",))
            B.dma("sp", w3[:], hfw3[i], writes=("w3",))
            B.dma("sp", w4[:], hfw4[i], writes=("w4",))
            B.dma("sp", fq[:], hffq[i], writes=("fq",))
            B.dma("sp", fb[:], hffb[i], writes=("fb",))
            B.dma("sp", hbb[:], hbias[i], writes=("hbb",))
            B.tt("dve", fb[:], fb[:], fq[:], ALU.mult, reads=("fb", "fq"), writes=("fb",))
            src = zt
            stok = "zt"
            wts = [(w1, "w1", 33), (w2, "w2", 64), (w3, "w3", 64)]
            dsts = [(hA, "hA"), (hBf, "hBf"), (hA, "hA")]
            mcnt = 0
            for li in range(3):
                wt, wtok, kk = wts[li]
                dst, dtok = dsts[li]
                for tt in range(NT):
                    tsl = slice(tt * 512, (tt + 1) * 512)
                    q = mcnt % 2
                    mcnt += 1
                    B.mm(psm[q][0:64, :], wt[0:kk, :], src[0:kk, tsl], True, True, reads=(wtok, stok + ":%d" % tt if stok != "zt" else "zt"),
                         writes=("psm%d" % q,))
                    a, w_ = arg[0], wr[0]
                    B.ts("dve", a[:], psm[q][0:64, :], fq[:, li:li + 1], fb[:, li:li + 1], ALU.mult, ALU.add,
                         reads=("psm%d" % q, "fq", "fb"), writes=("arg0",))
                    q = 0
                    for rep in range(2):
                        B.ts("dve", w_[:], a[:], PI, -2.0 * PI, ALU.is_gt, ALU.mult, reads=("arg%d" % q,), writes=("wr%d" % q,))
                        B.tt("dve", a[:], a[:], w_[:], ALU.add, reads=("arg%d" % q, "wr%d" % q), writes=("arg%d" % q,))
                        B.ts("dve", w_[:], a[:], -PI, 2.0 * PI, ALU.is_lt, ALU.mult, reads=("arg%d" % q,), writes=("wr%d" % q,))
                        B.tt("dve", a[:], a[:], w_[:], ALU.add, reads=("arg%d" % q, "wr%d" % q), writes=("arg%d" % q,))
                    B.act(dst[:, tsl], a[:], AF.Sin, reads=("arg%d" % q,), writes=(dtok + ":%d" % tt,))
                src, stok = dst, dtok
            for t16 in range(16):
                tt = t16 // 4
                B.act(dec[:], ndl[:], AF.Exp, scale=tc[:, t16:t16 + 1], reads=("ndl", "tc"), writes=("dec",))
                for cg in range(4):
                    B.mm(ps4[cg][:], hA[:, t16 * 128:(t16 + 1) * 128], w4[:, cg * 512:(cg + 1) * 512], True, True,
                         reads=("hA:%d" % tt, "w4"), writes=("ps4%d" % cg,))
                for cg in range(2):
                    csl = slice(cg * 512, (cg + 1) * 512)
                    B.tt("dve", hf[:, csl], ps4[cg][:], dec[:, csl], ALU.mult, reads=("ps4%d" % cg, "dec"), writes=("hf%d" % cg,))
                    B.stt("dve", hb[:, csl], ps4[2 + cg][:], m0[:, t16:t16 + 1], dec[:, csl], ALU.mult, ALU.mult,
                          reads=("ps4%d" % (2 + cg), "dec", "m0"), writes=("hb%d" % cg,))
                B.stt("dve", hf[:], hbb[:], d0[:, t16:t16 + 1], hf[:], ALU.mult, ALU.add, reads=("hbb", "d0", "hf0", "hf1"), writes=("hf0", "hf1"))
                B.tt("dve", HS[:, t16, :], hf[:], hb[:], ALU.add, reads=("hf0", "hf1", "hb0", "hb1"), writes=("HS:%d" % t16,))
                B.tt("dve", HD[:, t16, :], hf[:], hb[:], ALU.subtract, reads=("hf0", "hf1", "hb0", "hb1"), writes=("HD:%d" % t16,))
            for g in range(16):
                s = fcnt % 2
                fcnt += 1
                B.dma("sp", Fs[s][:], Fh[g], writes=("Fs%d" % s,))
                for part, (src_, stk) in enumerate(((HS, "HS"), (HD, "HD"))):
                    for hh in range(2):
                        pp = ps4[part * 2 + hh]
                        for t16 in range(16):
                            B.mm(pp[:], Fs[s][:, part, t16, :], src_[:, t16, hh * 512:(hh + 1) * 512], t16 == 0, t16 == 15,
                                 reads=("Fs%d" % s, stk + ":%d" % t16), writes=("ps4%d" % (part * 2 + hh),))
                        B.cp("act", hst[0][:, part, hh * 512:(hh + 1) * 512], pp[:], reads=("ps4%d" % (part * 2 + hh),), writes=("hst0",))
                B.dma("sp", HF[i, g], hst[0][:], reads=("hst0",), writes=("HF%d:%d" % (i, g),))
        B.fence()
    B.es = old


def hyena_mixer(B, l):
    i = l // 2
    X, HB, modc, flg = B.X, B.HB, B.modc, B.flg
    xtok, htok = B.xtok, B.htok
    if not hasattr(B, "hy_in"):
        B.hy_in = dict(
            hwin=B.inp("hwin", [2, 24, 128, 8, 128]),
            hcw=B.inp("hcw", [2, 128, 72]),
            hcb=B.inp("hcb", [2, 128, 24]),
            hwout=B.inp("hwout", [2, 8, 128, 8, 128]),
            Gh=B.inp("Gh", [32, 128, T], BF16),
            X0S=B.scratch("X0S", [8, 128, T], BF16),
        )
    hwin, hcw, hcb, hwout, Gh, X0S = (B.hy_in[k] for k in ("hwin", "hcw", "hcb", "hwout", "Gh", "X0S"))
    Fh, HF = B.Fh, B.HF
    Z = HB
    with ExitStack() as mix:
        B.es, old_mix = mix, B.es
        VX = B.sb("VX", [128, 16, D], BF16)
        with ExitStack() as ph:
            B.es = ph
            cw = B.sb("cw", [128, 72], F32)
            cb = B.sb("cb", [128, 24], F32)
            ncw = B.sb("ncw", [128, 72], F32)
            U = [B.sb("U%d" % k, [128, T + 2], F32) for k in range(3)]
            acc = [B.sb("acc%d" % k, [128, T], F32) for k in range(3)]
            vxb = [B.sb("vxb%d" % k, [128, T], BF16) for k in range(2)]
            x0b = [B.sb("x0b%d" % k, [128, T], BF16) for k in range(2)]
            wps = [B.sb("wps%d" % k, [128, 8, 128], BF16) for k in range(3)]
            psI = [B.ps("psI%d" % k, [128, 512], F32) for k in range(4)]
            psT = [B.ps("psT%d" % k, [128, 4, 128], BF16) for k in range(2)]
            B.dma("sp", cw[:], hcw[i], writes=("cw",))
            B.dma("sp", cb[:], hcb[i], writes=("cb",))
            B.ts("dve", ncw[:], cw[:], flg[:, 1:2], None, ALU.mult, reads=("cw", "flg"), writes=("ncw",))
            for k in range(3):
                B.memset("dve", U[k][:, 0:1], 0.0, writes=("U%d" % k,))
                B.memset("dve", U[k][:, T + 1:T + 2], 0.0, writes=("U%d" % k,))
            pc = 0
            tc_ = 0
            for j in range(8):
                for kind, c in ((0, 8 + j), (1, 16 + j), (2, j)):
                    Uk, ak = U[kind], acc[kind]
                    ut, at = "U%d" % kind, "acc%d" % kind
                    B.dma("pool", wps[kind][:], hwin[i, c], writes=("wps%d" % kind,))
                    for tt in range(NT):
                        tsl = slice(tt * 512, (tt + 1) * 512)
                        q = pc % 4
                        pc += 1
                        for k in range(8):
                            B.mm(psI[q][:], wps[kind][:, k, :], HB[:, k, tsl], k == 0, k == 7,
                                 reads=("wps%d" % kind, htok(k, tt)), writes=("psI%d" % q,))
                        B.cp("act", Uk[:, 1 + tt * 512:1 + (tt + 1) * 512], psI[q][:], reads=("psI%d" % q,), writes=(ut,))
                    B.act(ak[:], Uk[:, 1:T + 1], AF.Identity, bias=cb[:, c:c + 1], scale=cw[:, c * 3 + 1:c * 3 + 2],
                          reads=(ut, "cw", "cb"), writes=(at,))
                    B.stt("dve", ak[:], Uk[:, 0:T], cw[:, c * 3:c * 3 + 1], ak[:], ALU.mult, ALU.add, reads=(ut, at, "cw"), writes=(at,))
                    B.stt("dve", ak[:], Uk[:, 2:T + 2], cw[:, c * 3 + 2:c * 3 + 3], ak[:], ALU.mult, ALU.add, reads=(ut, at, "cw"), writes=(at,))
                    B.stt("dve", ak[:, SEG:T:SEG], Uk[:, SEG:T:SEG], ncw[:, c * 3:c * 3 + 1], ak[:, SEG:T:SEG], ALU.mult, ALU.add,
                          reads=(ut, at, "ncw"), writes=(at,))
                    B.stt("dve", ak[:, SEG - 1:T - 1:SEG], Uk[:, SEG + 1:T + 1:SEG], ncw[:, c * 3 + 2:c * 3 + 3], ak[:, SEG - 1:T - 1:SEG],
                          ALU.mult, ALU.add, reads=(ut, at, "ncw"), writes=(at,))
                    if kind == 1:
                        vb = vxb[j % 2]
                        vt = "vxb%d" % (j % 2)
                        B.tt("dve", vb[:], acc[1][:], acc[0][:], ALU.mult, reads=("acc0", "acc1"), writes=(vt,))
                        for t4 in range(4):
                            pq = tc_ % 2
                            tc_ += 1
                            for qq in range(4):
                                t16 = t4 * 4 + qq
                                B.tr(psT[pq][:, qq, :], vb[:, t16 * 128:(t16 + 1) * 128], B.idb[:], reads=(vt, "idb"), writes=("psT%d" % pq,))
                            B.cp("act", VX[:, t4 * 4:(t4 + 1) * 4, j * 128:(j + 1) * 128], psT[pq][:], reads=("psT%d" % pq,),
                                 writes=["VX:%d:%d" % (t4 * 4 + qq, j // 4) for qq in range(4)])
                    if kind == 2:
                        xb_ = x0b[j % 2]
                        xt_ = "x0b%d" % (j % 2)
                        B.cp("act", xb_[:], acc[2][:], reads=("acc2",), writes=(xt_,))
                        B.dma("sp", X0S[j], xb_[:], reads=(xt_,), writes=("X0S%d" % j,))
            B.fence()
        for half in range(2):
            hsl = slice(half * 512, (half + 1) * 512)
            with ExitStack() as ph2:
                B.es = ph2
                YS = B.sb("YS", [128, 32, 512], BF16)
                with ExitStack() as ph:
                    B.es = ph
                    Fs = [B.sb("Fsd%d" % k, [128, 2, 16, 128], BF16) for k in range(2)]
                    Hs = [B.sb("Hs%d" % k, [128, 2, 512], BF16) for k in range(2)]
                    tp = [B.sb("tp%d" % k, [128, 512], F32) for k in range(4)]
                    psV = [B.ps("psV%d" % k, [128, 512], F32) for k in range(4)]
                    for g in range(16):
                        s = g % 2
                        B.dma("sp", Fs[s][:], Fh[g], writes=("Fs%d" % s,))
                        B.dma("sp", Hs[s][:], HF[i, g][:, :, hsl], reads=("HF%d:%d" % (i, g),), writes=("Hs%d" % s,))
                        for part in range(2):
                            pp = psV[s * 2 + part]
                            for t16 in range(16):
                                B.mm(pp[:], Fs[s][:, part, t16, :], VX[:, t16, hsl], t16 == 0, t16 == 15,
                                     reads=("Fs%d" % s, "VX:%d:%d" % (t16, half)), writes=("psV%d" % (s * 2 + part),))
                        vre, vim = psV[s * 2], psV[s * 2 + 1]
                        rt, it = "psV%d" % (s * 2), "psV%d" % (s * 2 + 1)
                        B.tt("dve", tp[0][:], vre[:], Hs[s][:, 0, :], ALU.mult, reads=(rt, "Hs%d" % s), writes=("tp0",))
                        B.tt("dve", tp[1][:], vim[:], Hs[s][:, 1, :], ALU.mult, reads=(it, "Hs%d" % s), writes=("tp1",))
                        B.tt("dve", tp[2][:], vre[:], Hs[s][:, 1, :], ALU.mult, reads=(rt, "Hs%d" % s), writes=("tp2",))
                        B.tt("dve", tp[3][:], vim[:], Hs[s][:, 0, :], ALU.mult, reads=(it, "Hs%d" % s), writes=("tp3",))
                        B.tt("pool", YS[:, g, :], tp[0][:], tp[1][:], ALU.subtract, reads=("tp0", "tp1"), writes=("YS:%d" % g,))
                        B.tt("pool", YS[:, 16 + g, :], tp[2][:], tp[3][:], ALU.add, reads=("tp2", "tp3"), writes=("YS:%d" % (16 + g),))
                    B.fence()
                with ExitStack() as ph:
                    B.es = ph
                    Gs = [B.sb("Gs%d" % k, [128, 1024], BF16) for k in range(3)]
                    x0s = [B.sb("x0s%d" % k, [128, T], BF16) for k in range(4)]
                    psY = [B.ps("psY%d" % k, [128, 512], F32) for k in range(8)]
                    for cc in range(4):
                        B.dma("sp", x0s[cc][:], X0S[half * 4 + cc], reads=("X0S%d" % (half * 4 + cc),), writes=("x0s%d" % cc,))
                    gc = 0
                    for tp2 in range(2):
                        for fc in range(32):
                            s = gc % 3
                            gc += 1
                            B.dma("sp", Gs[s][:], Gh[fc][:, tp2 * 1024:(tp2 + 1) * 1024], writes=("Gs%d" % s,))
                            for t2 in range(2):
                                for cc in range(4):
                                    B.mm(psY[t2 * 4 + cc][:], YS[:, fc, cc * 128:(cc + 1) * 128], Gs[s][:, t2 * 512:(t2 + 1) * 512],
                                         fc == 0, fc == 31, reads=("YS:%d" % fc, "Gs%d" % s), writes=("psY%d" % (t2 * 4 + cc),))
                        for t2 in range(2):
                            tt = tp2 * 2 + t2
                            tsl = slice(tt * 512, (tt + 1) * 512)
                            for cc in range(4):
                                B.tt("dve", Z[:, half * 4 + cc, tsl], psY[t2 * 4 + cc][:], x0s[cc][:, tsl], ALU.mult,
                                     reads=("psY%d" % (t2 * 4 + cc), "x0s%d" % cc), writes=(htok(half * 4 + cc, tt),))
                    B.fence()
        if B.cfg.get("debug") and l == 1:
            dz = B.outp("dbgZ", [8, 128, T], BF16)
            for cc in range(8):
                B.dma("sp", dz[cc], Z[:, cc, :], reads=[htok(cc, tt) for tt in range(NT)], is_out=True)
            B.fence()
        with ExitStack() as ph:
            B.es = ph
            lnps = B.alloc_ln()
            wos = [B.sb("wos%d" % k, [128, 8, 128], BF16) for k in range(8)]
            otm = [B.sb("otm%d" % k, [128, 512], F32) for k in range(2)]
            psO = [B.ps("psO%d" % k, [128, 512], F32) for k in range(4)]
            for dch in range(8):
                B.dma("pool", wos[dch][:], hwout[i, dch], writes=("wos%d" % dch,))
            oc = [0]

            def OP(tt):
                tsl = slice(tt * 512, (tt + 1) * 512)
                for dch in range(8):
                    q = oc[0] % 4
                    oc[0] += 1
                    for cc in range(8):
                        B.mm(psO[q][:], wos[dch][:, cc, :], Z[:, cc, tsl], cc == 0, cc == 7, reads=("wos%d" % dch, htok(cc, tt)), writes=("psO%d" % q,))
                    c = l * 48 + 2 * 8 + dch
                    B.act(otm[q % 2][:], psO[q][:], AF.Identity, scale=modc[:, c:c + 1], reads=("psO%d" % q, "modc"), writes=("otm%d" % (q % 2),))
                    B.stt("dve", X[:, dch, tsl], X[:, dch, tsl], ALPHA, otm[q % 2][:], ALU.mult, ALU.add,
                          reads=(xtok(dch, tt), "otm%d" % (q % 2)), writes=(xtok(dch, tt),))
                    yield

            def LNt(tt):
                yield from B.ln_tile_gen(lnps, l, 0, tt, (l, 3))
            drain(OP(0))
            for tt in range(1, NT):
                drain(OP(tt), LNt(tt - 1))
            drain(LNt(NT - 1))
            B.fence()
    B.es = old_mix


CH = 64
NG = 45
EXPC = float(math.exp(-0.5))
GN_EPS = 64e-5
def _ab_groups():
    g = []
    for h in range(4):
        g.append(("gq%d" % h, 64 * h, 64))
    for h in range(4):
        g.append(("gk%d" % h, 256 + 64 * h, 64))
    for h in range(4):
        g.append(("gv%d" % h, 512 + 128 * h, 128))
    for h in range(4):
        g.append(("gg%d" % h, 1024 + 128 * h, 128))
    g.append(("glr", 1536, 32))
    r0 = 1568
    g.append(("rwl", r0 + 1536, 128))
    g.append(("ral", r0 + 1664, 128))
    g.append(("rgl0", r0 + 1792, 128))
    g.append(("rgl1", r0 + 1920, 32))
    for h in range(8):
        g.append(("rr%d" % h, r0 + 64 * h, 64))
        g.append(("rk%d" % h, r0 + 512 + 64 * h, 64))
        g.append(("rv%d" % h, r0 + 1024 + 64 * h, 64))
    assert len(g) == NG
    return g


AB_GROUPS = _ab_groups()


def ab_inputs(B):
    if hasattr(B, "ab_in"):
        return B.ab_in
    d = dict(
        abwin=B.inp("abwin", [2, NG, 128, 8, 128]),
        abmu=B.inp("abmu", [2, 128, NG * 2]),
        abwo_g=B.inp("abwo_g", [2, 8, 128, 4, 128]),
        abwo_r=B.inp("abwo_r", [2, 8, 64, 8, 128]),
        w2bd=B.inp("w2bd", [2, 128, 1024]),
        w0row=B.inp("w0row", [2, 1, 1024]),
        a2s=B.inp("a2s", [2, 128, 512]),
        a0c=B.inp("a0c", [2, 64, 16]),
        g2a=B.inp("g2a", [2, 128, 512]),
        g2b=B.inp("g2b", [2, 32, 512]),
        wdbd=B.inp("wdbd", [2, 32, 512]),
        gbrow=B.inp("gbrow", [2, 1, 512]),
        rcols=B.inp("rcols", [2, 64, 40]),
        gnw=B.inp("gnw", [2, 128, 1]),
        trimats=B.inp("trimats", [4, 128, 128]),
        cmasks=B.inp("cmasks", [4, 64, 512], BF16),
        identrep=B.inp("identrep", [64, 512], BF16),
        lvlmask=B.inp("lvlmask", [64, 6, 64], BF16),
        gstate=B.inp("gstate", [2, 2, 64, 4, 128]),
        rstate=B.inp("rstate", [2, 2, 64, 8, 64]),
        gst_out=B.outp("gst_out", [2, 2, NSEG, 64, 4, 128]),
        rst_out=B.outp("rst_out", [2, 2, NSEG, 64, 8, 64]),
        XS=B.scratch("XS", [8, 128, T], F32),
        GQ=B.scratch("GQ", [4, 64, T], BF16), GK=B.scratch("GK", [4, 64, T], BF16),
        GV=B.scratch("GV", [4, 128, T], BF16), GG=B.scratch("GG", [4, 128, T], BF16),
        LA=B.scratch("LA", [T, 512], F32), SG=B.scratch("SG", [T, 1024], F32),
        RR=B.scratch("RR", [8, 64, T], BF16), RK=B.scratch("RK", [8, 64, T], BF16), RV=B.scratch("RV", [8, 64, T], BF16),
        RKK=B.scratch("RKK", [8, 64, T], BF16), RBV=B.scratch("RBV", [8, 64, T], BF16), RG=B.scratch("RG", [8, 64, T], BF16),
        RA=B.scratch("RA", [2, 8, 64, T], BF16),
        YFR=B.scratch("YFR", [8, 64, T], F32), YFG=B.scratch("YFG", [4, 128, T], F32),
    )
    B.ab_in = d
    return d


def ab_phase1(B, l):
    i = l // 2
    A = ab_inputs(B)
    X, HB, flg = B.X, B.HB, B.flg
    xtok, htok = B.xtok, B.htok
    gidx = {g[0]: n for n, g in enumerate(AB_GROUPS)}
    with ExitStack() as ph:
        B.es, old = ph, B.es
        for k in range(8):
            B.dma("sp", A["XS"][k], X[:, k, :], reads=[xtok(k, tt) for tt in range(NT)], writes=("XS%d" % k,))
        B.fence()
        mu = B.sb("mu", [128, NG * 2], F32)
        c1 = B.sb("c1", [128, NG], F32)
        nmu = B.sb("nmu", [128, NG * 2], F32)
        rc = B.sb("rc", [64, 40], F32)
        a0 = B.sb("a0", [64, 16], F32)
        U = [B.sb("aU%d" % k, [128, T + 2], F32) for k in range(2)]
        acc = [X[:, k, :] for k in range(3)]
        ob = [X[:, 3 + k, 0:1024].bitcast(BF16) for k in range(3)]
        twl = X[:, 3, 1024:2048].bitcast(BF16)
        alb = X[:, 4, 1024:2048].bitcast(BF16)
        sg0 = X[:, 5, 1024:2048].bitcast(BF16)
        sg1 = B.sb("sg1", [32, T], BF16)
        lrb = B.sb("lrb", [32, T], BF16)
        wps = [B.sb("awps%d" % k, [128, 8, 128], BF16) for k in range(3)]
        w2 = B.sb("w2", [128, 1024], BF16)
        w0r = B.sb("w0r", [1, 1024], F32)
        a2 = B.sb("a2", [128, 512], BF16)
        g2a = B.sb("g2a", [128, 512], BF16)
        g2b = B.sb("g2b", [32, 512], BF16)
        wdb = B.sb("wdb", [32, 512], BF16)
        gbr = B.sb("gbr", [1, 512], F32)
        onesf = B.sb("onesf", [1, 128], F32)
        ones1 = B.sb("ones1", [128, 128], BF16)
        tmpa = [B.sb("tmpa%d" % k, [128, 512], F32) for k in range(2)]
        tmpb = [B.sb("tmpb%d" % k, [128, 512], BF16) for k in range(2)]
        tmpc = [B.sb("tmpc%d" % k, [128, 512], F32) for k in range(2)]
        tokb = [B.sb("tokb%d" % k, [128, 1024], F32) for k in range(2)]
        psI = [B.ps("apsI%d" % k, [128, 512], F32) for k in range(4)]
        psS = [B.ps("apsS%d" % k, [128, 512], F32) for k in range(4)]
        B.dma("sp", mu[:], A["abmu"][i], writes=("mu",))
        B.dma("sp", rc[:], A["rcols"][i], writes=("rc",))
        B.dma("sp", a0[:], A["a0c"][i], writes=("a0",))
        B.dma("pool", w2[:], A["w2bd"][i], writes=("w2",))
        B.dma("sp", w0r[:], A["w0row"][i], writes=("w0r",))
        B.dma("pool", a2[:], A["a2s"][i], writes=("a2",))
        B.dma("pool", g2a[:], A["g2a"][i], writes=("g2a",))
        B.dma("pool", g2b[:], A["g2b"][i], writes=("g2b",))
        B.dma("pool", wdb[:], A["wdbd"][i], writes=("wdb",))
        B.dma("sp", gbr[:], A["gbrow"][i], writes=("gbr",))
        B.memset("dve", onesf[:], 1.0, writes=("onesf",))
        B.memset("dve", ones1[:], 1.0, writes=("ones1",))
        muv = mu[:].rearrange("p (g two) -> p g two", two=2)
        B.tt("dve", c1[:], muv[:, :, 0], muv[:, :, 1], ALU.add, reads=("mu",), writes=("c1",))
        B.ts("dve", c1[:], c1[:], -1.0, 1.0, ALU.mult, ALU.add, reads=("c1",), writes=("c1",))
        B.ts("dve", nmu[:], mu[:], flg[:, 1:2], None, ALU.mult, reads=("mu", "flg"), writes=("nmu",))
        for k in range(2):
            B.memset("dve", U[k][:, 0:1], 0.0, writes=("aU%d" % k,))
            B.memset("dve", U[k][:, T + 1:T + 2], 0.0, writes=("aU%d" % k,))
        cnt = dict(pc=0, u=0, w=0, o=0, s=0, t=0)

        def inproj(gname, shift):
            n = gidx[gname]
            M = AB_GROUPS[n][2]
            s = cnt["w"] % 3
            cnt["w"] += 1
            B.dma("pool", wps[s][:], A["abwin"][i, n], writes=("awps%d" % s,))
            if shift:
                ui = cnt["u"] % 2
                cnt["u"] += 1
                Uk, ut = U[ui], "aU%d" % ui
            ai = cnt["o"] % 3
            cnt["o"] += 1
            ak, at = acc[ai], "aacc%d" % ai
            for tt in range(NT):
                tsl = slice(tt * 512, (tt + 1) * 512)
                q = cnt["pc"] % 4
                cnt["pc"] += 1
                for k in range(8):
                    B.mm(psI[q][0:M, :], wps[s][:, k, 0:M], HB[:, k, tsl], k == 0, k == 7,
                         reads=("awps%d" % s, htok(k, tt)), writes=("apsI%d" % q,))
                if shift:
                    B.cp("act", Uk[0:M, 1 + tt * 512:1 + (tt + 1) * 512], psI[q][0:M, :], reads=("apsI%d" % q,), writes=(ut,))
                else:
                    B.cp("act", ak[0:M, tsl], psI[q][0:M, :], reads=("apsI%d" % q,), writes=(at,))
            if shift:
                B.act(ak[0:M, :], Uk[0:M, 1:T + 1], AF.Identity, scale=c1[0:M, n:n + 1], reads=(ut, "c1"), writes=(at,))
                B.stt("dve", ak[0:M, :], Uk[0:M, 0:T], mu[0:M, 2 * n:2 * n + 1], ak[0:M, :], ALU.mult, ALU.add, reads=(ut, at, "mu"), writes=(at,))
                B.stt("dve", ak[0:M, :], Uk[0:M, 2:T + 2], mu[0:M, 2 * n + 1:2 * n + 2], ak[0:M, :], ALU.mult, ALU.add, reads=(ut, at, "mu"), writes=(at,))
                B.stt("dve", ak[0:M, SEG:T:SEG], Uk[0:M, SEG:T:SEG], nmu[0:M, 2 * n:2 * n + 1], ak[0:M, SEG:T:SEG], ALU.mult, ALU.add,
                      reads=(ut, at, "nmu"), writes=(at,))
                B.stt("dve", ak[0:M, SEG - 1:T - 1:SEG], Uk[0:M, SEG + 1:T + 1:SEG], nmu[0:M, 2 * n + 1:2 * n + 2], ak[0:M, SEG - 1:T - 1:SEG],
                      ALU.mult, ALU.add, reads=(ut, at, "nmu"), writes=(at,))
            return ak, at, M

        def outbuf():
            oi = cnt["s"] % 3
            cnt["s"] += 1
            return ob[oi], "aob%d" % oi

        for h in range(4):
            for nm, dst in (("gq", "GQ"), ("gk", "GK"), ("gv", "GV")):
                ak, at, M = inproj("%s%d" % (nm, h), False)
                o, ot = outbuf()
                B.cp("dve", o[0:M, :], ak[0:M, :], reads=(at,), writes=(ot,))
                B.dma("sp", A[dst][h], o[0:M, :], reads=(ot,), writes=("%s%d" % (dst, h),))
            ak, at, M = inproj("gg%d" % h, False)
            o, ot = outbuf()
            B.act(o[:], ak[:], AF.Silu, reads=(at,), writes=(ot,))
            B.dma("sp", A["GG"][h], o[:], reads=(ot,), writes=("GG%d" % h,))
        ak, at, M = inproj("glr", False)
        B.cp("dve", lrb[:], ak[0:32, :], reads=(at,), writes=("lrb",))
        for t16 in range(16):
            q = cnt["t"] % 2
            cnt["t"] += 1
            B.mm(psS[q][:], lrb[:, t16 * 128:(t16 + 1) * 128], wdb[:], True, False, reads=("lrb", "wdb"), writes=("apsS%d" % q,))
            B.mm(psS[q][:], onesf[:], gbr[:], False, True, reads=("onesf", "gbr"), writes=("apsS%d" % q,))
            tb = tokb[q]
            B.act(tb[:, 0:512], psS[q][:], AF.Sigmoid, reads=("apsS%d" % q,), writes=("tokb%d" % q,))
            B.act(tb[:, 0:512], tb[:, 0:512], AF.Ln, reads=("tokb%d" % q,), writes=("tokb%d" % q,))
            B.ts("dve", tb[:, 0:512], tb[:, 0:512], 1.0 / 16.0, None, ALU.mult, reads=("tokb%d" % q,), writes=("tokb%d" % q,))
            B.dma("sp", A["LA"][t16 * 128:(t16 + 1) * 128, :], tb[:, 0:512], reads=("tokb%d" % q,), writes=("LA%d" % t16,))
        ak, at, M = inproj("rwl", True)
        B.act(twl[:], ak[:], AF.Tanh, reads=(at,), writes=("twl",))
        for t16 in range(16):
            q = cnt["t"] % 2
            cnt["t"] += 1
            for hh in range(2):
                pp = psS[q * 2 + hh]
                B.mm(pp[:], twl[:, t16 * 128:(t16 + 1) * 128], w2[:, hh * 512:(hh + 1) * 512], True, False, reads=("twl", "w2"), writes=("apsS%d" % (q * 2 + hh),))
                B.mm(pp[:], onesf[:], w0r[:, hh * 512:(hh + 1) * 512], False, True, reads=("onesf", "w0r"), writes=("apsS%d" % (q * 2 + hh),))
                B.act(tokb[q][:, hh * 512:(hh + 1) * 512], pp[:], AF.Sigmoid, reads=("apsS%d" % (q * 2 + hh),), writes=("tokb%d" % q,))
            B.dma("sp", A["SG"][t16 * 128:(t16 + 1) * 128, :], tokb[q][:], reads=("tokb%d" % q,), writes=("SG%d" % t16,))
        ak, at, M = inproj("ral", True)
        B.cp("dve", alb[:], ak[:], reads=(at,), writes=("alb",))
        for d in range(2):
            for h in range(8):
                o, ot = outbuf()
                for tt in range(NT):
                    tsl = slice(tt * 512, (tt + 1) * 512)
                    q = cnt["pc"] % 4
                    cnt["pc"] += 1
                    B.mm(psI[q][0:64, :], a2[d * 64:(d + 1) * 64, h * 64:(h + 1) * 64], alb[d * 64:(d + 1) * 64, tsl], True, True,
                         reads=("a2", "alb"), writes=("apsI%d" % q,))
                    B.act(o[0:64, tsl], psI[q][0:64, :], AF.Sigmoid, bias=a0[:, d * 8 + h:d * 8 + h + 1], reads=("apsI%d" % q, "a0"), writes=(ot,))
                B.dma("sp", A["RA"][d, h], o[0:64, :], reads=(ot,), writes=("RA%d_%d" % (d, h),))
        ak, at, M = inproj("rgl0", True)
        B.act(sg0[:], ak[:], AF.Sigmoid, reads=(at,), writes=("sg0",))
        ak, at, M = inproj("rgl1", True)
        B.act(sg1[:], ak[0:32, :], AF.Sigmoid, reads=(at,), writes=("sg1",))
        for h in range(8):
            o, ot = outbuf()
            for tt in range(NT):
                tsl = slice(tt * 512, (tt + 1) * 512)
                q = cnt["pc"] % 4
                cnt["pc"] += 1
                B.mm(psI[q][0:64, :], g2a[:, h * 64:(h + 1) * 64], sg0[:, tsl], True, False, reads=("g2a", "sg0"), writes=("apsI%d" % q,))
                B.mm(psI[q][0:64, :], g2b[:, h * 64:(h + 1) * 64], sg1[:, tsl], False, True, reads=("g2b", "sg1"), writes=("apsI%d" % q,))
                B.cp("act", o[0:64, tsl], psI[q][0:64, :], reads=("apsI%d" % q,), writes=(ot,))
            B.dma("sp", A["RG"][h], o[0:64, :], reads=(ot,), writes=("RG%d" % h,))
        for h in range(8):
            ar, art, _ = inproj("rr%d" % h, True)
            akk, akt, _ = inproj("rk%d" % h, True)
            av, avt, _ = inproj("rv%d" % h, True)
            for src, st, dst in ((ar, art, "RR"), (akk, akt, "RK"), (av, avt, "RV")):
                o, ot = outbuf()
                B.cp("dve", o[0:64, :], src[0:64, :], reads=(st,), writes=(ot,))
                B.dma("sp", A[dst][h], o[0:64, :], reads=(ot,), writes=("%s%d" % (dst, h),))
            okk, okt = outbuf()
            obv, obt = outbuf()
            for tt in range(NT):
                tsl = slice(tt * 512, (tt + 1) * 512)
                ta, tat = tmpa[tt % 2], "tmpa%d" % (tt % 2)
                tb, tbt = tmpb[tt % 2], "tmpb%d" % (tt % 2)
                q = cnt["t"] % 4
                cnt["t"] += 1
                B.ts("dve", ta[0:64, :], akk[0:64, tsl], rc[:, h:h + 1], None, ALU.mult, reads=(akt, "rc"), writes=(tat,))
                B.act(tb[0:64, :], ta[0:64, :], AF.Square, reads=(tat,), writes=(tbt,))
                B.mm(psS[q][0:64, :], ones1[0:64, 0:64], tb[0:64, :], True, True, reads=("ones1", tbt), writes=("apsS%d" % q,))
                tcn, tct = tmpc[tt % 2], "tmpc%d" % (tt % 2)
                B.act(tcn[0:64, :], psS[q][0:64, :], AF.Ln, bias=B.eps_ln[0:64, 1:2], reads=("apsS%d" % q,), writes=(tct,))
                B.act(tcn[0:64, :], tcn[0:64, :], AF.Exp, scale=-0.5, reads=(tct,), writes=(tct,))
                B.tt("dve", okk[0:64, tsl], ta[0:64, :], tcn[0:64, :], ALU.mult, reads=(tat, tct), writes=(okt,))
                B.stt("dve", tb[0:64, :], ar[0:64, tsl], rc[:, 16 + h:17 + h], akk[0:64, tsl], ALU.mult, ALU.mult, reads=(art, akt, "rc"), writes=(tbt,))
                q2 = cnt["t"] % 4
                cnt["t"] += 1
                B.mm(psS[q2][0:64, :], ones1[0:64, 0:64], tb[0:64, :], True, True, reads=("ones1", tbt), writes=("apsS%d" % q2,))
                B.tt("dve", obv[0:64, tsl], psS[q2][0:64, :], av[0:64, tsl], ALU.mult, reads=("apsS%d" % q2, avt), writes=(obt,))
            B.dma("sp", A["RKK"][h], okk[0:64, :], reads=(okt,), writes=("RKK%d" % h,))
            B.dma("sp", A["RBV"][h], obv[0:64, :], reads=(obt,), writes=("RBV%d" % h,))
        B.fence()
    B.es = old


def ab_phase2(B, l):
    i = l // 2
    A = ab_inputs(B)
    HB, flg = B.HB, B.flg
    htok = B.htok
    with ExitStack() as ph:
        B.es, old = ph, B.es
        tri = B.sb("tri", [128, 4, 128], F32)
        msk2 = B.sb("msk", [128, 4, 64], BF16)
        idr2 = B.sb("idr", [128, 512], BF16)
        lvl2 = B.sb("lvl", [128, 6, 64], BF16)
        rc2 = B.sb("rc2", [128, 40], F32)
        omka2 = B.sb("omka", [128, 8], F32)
        msk, idr, lvl, rc, omka = msk2[0:64], idr2[0:64], lvl2[0:64], rc2[0:64], omka2[0:64]
        for hf_ in range(2):
            B.dma("sp", lvl2[hf_ * 64:(hf_ + 1) * 64], A["lvlmask"], writes=("lvl",))
            B.dma("sp", idr2[hf_ * 64:(hf_ + 1) * 64], A["identrep"], writes=("idr",))
            B.dma("sp", rc2[hf_ * 64:(hf_ + 1) * 64], A["rcols"][i], writes=("rc2",))
            for k in range(4):
                B.dma("sp", msk2[hf_ * 64:(hf_ + 1) * 64, k, :], A["cmasks"][k][:, 0:64], writes=("msk",))
        gnw = B.sb("gnw", [128, 1], F32)
        ones1 = B.sb("ones1b", [128, 128], BF16)
        for k in range(4):
            B.dma("sp", tri[:, k, :], A["trimats"][k], writes=("tri",))
        B.dma("sp", gnw[:], A["gnw"][i], writes=("gnw",))
        B.memset("dve", ones1[:], 1.0, writes=("ones1b",))
        B.ts("dve", omka2[:], rc2[:, 8:16], -1.0, 1.0, ALU.mult, ALU.add, reads=("rc2",), writes=("omka",))
        psD = [B.ps("psD%d" % k, [64, 4, 128], F32) for k in range(2)]
        psTr = [B.ps("psTr%d" % k, [64, 512], BF16) for k in range(2)]
        psA = [B.ps("psA%d" % k, [128, 512], F32) for k in range(4)]
        cnt = dict(a=0, t=0, d=0)

        def next_psA():
            q = cnt["a"] % 4
            cnt["a"] += 1
            return psA[q], "psA%d" % q

        for kind in ("g",):
            NH = 4 if kind == "g" else 8
            DV = 128 if kind == "g" else 64
            VP = DV
            W = NH * 64
            WV = NH * DV
            low = kind == "r"
            with ExitStack() as kp:
                B.es = kp
                names = ["q", "k", "v"] + (["kk", "a"] if low else [])
                ld = {}
                xslots = {"q": (6, 0), "k": (6, 1), "kk": (7, 0), "a": (7, 1)}
                for nm in names:
                    if nm == "v":
                        ld[nm] = [B.sb("ld%s%s" % (kind, nm), [VP, NH, SEG], BF16)] * 2
                    else:
                        xk, xh = xslots[nm]
                        xv_ = B.X[0:64, xk, xh * 1024:(xh + 1) * 1024].bitcast(BF16)
                        ld[nm] = [xv_[:, 0:NH * SEG].rearrange("p (h t) -> p h t", h=NH)] * 2
                dec = [B.sb("dec%s" % kind, [128, 2, W], F32)] * 2
                onm = ["RHO", "KT", "KH"] + (["KAP", "BT", "BH"] if low else [])
                opd = {}
                for oi_, nm in enumerate(onm):
                    opd[nm] = []
                    for z in range(2):
                        nbuf = oi_ * 2 + z
                        xv_ = B.X[0:64, nbuf // 2, (nbuf % 2) * 1024:(nbuf % 2 + 1) * 1024].bitcast(BF16)
                        opd[nm].append(xv_[:, 0:NH * SEG].rearrange("p (h t) -> p h t", h=NH))
                pC = [B.sb("pC%s%d" % (kind, z), [64, NH, 4], F32) for z in range(2)]
                pt = {nm: B.sb("pt%s%s" % (kind, nm), [64, NH, 128], F32) for nm in ("inc", "inv", "exc", "end", "b", "kd")}
                Hf = B.sb("Hf" + kind, [64, NH, DV], F32)
                Hb = B.sb("Hb" + kind, [64, NH, DV], BF16)
                if low:
                    YF = [B.HB[0:64, 4 + 2 * z:6 + 2 * z, :].rearrange("p k t -> p (k t)").bitcast(F32).rearrange("p (h t) -> p h t", h=NH) for z in range(2)]
                else:
                    YF = [B.HB[:, 4 + z, :].bitcast(F32).rearrange("p (h t) -> p h t", h=NH) for z in range(2)]
                cb = {}
                for nm, wdt in (("KHt", W), ("BHt", W), ("Vt", WV), ("M0", W), ("N0", W), ("Akk", W), ("Brk", W), ("Brb", W),
                                ("Nn", W), ("Mn", W), ("Tt", W), ("Xb", WV), ("Ub", WV)):
                    if not low and nm in ("BHt", "M0", "N0", "Akk", "Brb", "Nn", "Mn", "Tt", "Xb", "Ub"):
                        continue
                    nb = 4 if nm == "Tt" else 2
                    cb[nm] = [B.sb("cb%s%s%d" % (kind, nm, z), [64, wdt], BF16) for z in range(nb)]
                if low:
                    nrm = {nm: pt[pn][:].rearrange("p h t -> p (h t)")[:, 0:512] for nm, pn in (("a", "inc"), ("b", "inv"), ("c", "exc"))}
                    nrmt = dict(a="ptinc", b="ptinv", c="ptexc")
                else:
                    nrm = {nm: B.sb("nrm%s%s" % (kind, nm), [VP, 512], F32)[:] for nm in ("a", "b", "c")}
                    nrmt = dict(a="nrma", b="nrmb", c="nrmc")
                nrb = {nm: B.sb("nrb%s%s" % (kind, nm), [VP, 512], BF16) for nm in ("a", "b")}
                if low:
                    gl = {"g": ld["kk"], "bv": ld["a"]}
                    glt = {"g": "ldkk", "bv": "lda"}
                else:
                    gl = {"g": [B.sb("glgg", [VP, NH, SEG], BF16)] * 2}
                    glt = {"g": "glg"}
                src = dict(q=A["RR"] if low else A["GQ"], k=A["RK"] if low else A["GK"], v=A["RV"] if low else A["GV"])
                if low:
                    src["kk"] = A["RKK"]
                stin = A["rstate"] if low else A["gstate"]
                stout = A["rst_out"] if low else A["gst_out"]
                YFS = A["YFR"] if low else A["YFG"]
                DEC = A["SG"] if low else A["LA"]
                DW = 512 if low else 256
                sn = 0
                for d in range(2):
                    fwd = d == 0
                    triI, triE, triS = (0, 1, 2) if fwd else (3, 2, 1)
                    mSU, mIU, mSL = (0, 1, 2) if fwd else (2, 3, 0)
                    B.dma("sp", Hf[:], stin[i, d], writes=("Hf",))
                    B.cp("act", Hb[:], Hf[:], reads=("Hf",), writes=("Hb",))
                    segs = list(range(NSEG)) if fwd else list(range(NSEG - 1, -1, -1))
                    for seg in segs:
                        z = sn % 2
                        sn += 1
                        ssl = slice(seg * SEG, (seg + 1) * SEG)
                        L = {nm: ld[nm][z] for nm in names}
                        LT = {nm: "ld%s" % nm for nm in names}
                        for nm in names:
                            sa = A["RA"][d] if nm == "a" else src[nm]
                            B.dma("sp", L[nm][:], sa.rearrange("h p t -> p h t")[:, :, ssl], writes=(LT[nm],))
                        B.dma("sp", dec[z][:], DEC[ssl, d * DW:(d + 1) * DW].rearrange("(q p) c -> p q c", p=128), writes=("dec0",))
                        if not fwd:
                            B.dma("sp", YF[z][:], YFS.rearrange("h p t -> p h t")[:, :, ssl], writes=("YF%d" % z,))
                        O = {nm: opd[nm][z] for nm in onm}
                        OT = {nm: "op%s%d" % (nm, z) for nm in onm}
                        for tq in range(2):
                            qsl = slice(tq * 128, (tq + 1) * 128)
                            esc = -EXPC if low else 1.0
                            for var, trik, outs in ((0, triI, (("inc", esc), ("inv", -esc))), (1, triE, (("exc", esc),)), (2, triS, (("end", esc),))):
                                if not low and var == 1:
                                    continue
                                for hg in range(NH // 4):
                                    pd = psD[cnt["d"] % 2]
                                    pdt = "psD%d" % (cnt["d"] % 2)
                                    cnt["d"] += 1
                                    for h4 in range(4):
                                        h = hg * 4 + h4
                                        B.mm(pd[:, h4, :], dec[z][:, tq, h * 64:(h + 1) * 64], tri[:, trik, :], True, True,
                                             reads=("dec0", "tri"), writes=(pdt,))
                                    for onm_, sc in outs:
                                        B.act(pt[onm_][:, hg * 4:(hg + 1) * 4, :], pd[:], AF.Exp, scale=sc, reads=(pdt,), writes=("pt" + onm_,))
                            if low:
                                B.tt("dve", pt["b"][:], L["a"][:, :, qsl], L["kk"][:, :, qsl], ALU.mult, reads=(LT["a"], LT["kk"]), writes=("ptb",))
                                for h in range(NH):
                                    B.ts("dve", pt["kd"][:, h, :], L["a"][:, h, qsl], rc[:, 8 + h:9 + h], omka[:, h:h + 1], ALU.mult, ALU.add,
                                         reads=(LT["a"], "rc2", "omka"), writes=("ptkd",))
                                B.tt("dve", pt["kd"][:], pt["kd"][:], L["k"][:, :, qsl], ALU.mult, reads=("ptkd", LT["k"]), writes=("ptkd",))
                                kd = pt["kd"][:]
                                kdt = "ptkd"
                                B.tt("dve", O["RHO"][:, :, qsl], L["q"][:, :, qsl], pt["inc"][:], ALU.mult, reads=(LT["q"], "ptinc"), writes=(OT["RHO"],))
                                B.tt("pool", O["KAP"][:, :, qsl], L["kk"][:, :, qsl], pt["exc"][:], ALU.mult, reads=(LT["kk"], "ptexc"), writes=(OT["KAP"],))
                                B.stt("dve", O["BT"][:, :, qsl], pt["b"][:], -1.0, pt["inv"][:], ALU.mult, ALU.mult, reads=("ptb", "ptinv"), writes=(OT["BT"],))
                                B.stt("dve", O["BH"][:, :, qsl], pt["b"][:], -1.0, pt["end"][:], ALU.mult, ALU.mult, reads=("ptb", "ptend"), writes=(OT["BH"],))
                            else:
                                kd = L["k"][:, :, qsl]
                                kdt = LT["k"]
                                B.stt("dve", O["RHO"][:, :, qsl], L["q"][:, :, qsl], 0.125, pt["inc"][:], ALU.mult, ALU.mult,
                                      reads=(LT["q"], "ptinc"), writes=(OT["RHO"],))
                            B.tt("dve", O["KT"][:, :, qsl], kd, pt["inv"][:], ALU.mult, reads=(kdt, "ptinv"), writes=(OT["KT"],))
                            B.tt("pool", O["KH"][:, :, qsl], kd, pt["end"][:], ALU.mult, reads=(kdt, "ptend"), writes=(OT["KH"],))
                            ccol = 63 if fwd else 0
                            B.cp("dve", pC[z][:, :, tq * 2:(tq + 1) * 2], pt["inc"][:, :, ccol:128:64], reads=("ptinc",), writes=("pC%d" % z,))

                        def pre(c):
                            zz = c % 2
                            cs = slice(c * CH, (c + 1) * CH)
                            todo = [("KHt", O["KH"], OT["KH"], 64)] + ([("BHt", O["BH"], OT["BH"], 64)] if low else []) + [("Vt", L["v"], LT["v"], DV)]
                            for nm, sarr, stk, wd in todo:
                                pq = cnt["t"] % 2
                                cnt["t"] += 1
                                for h in range(NH):
                                    B.tr(psTr[pq][:, h * wd:(h + 1) * wd], sarr[:, h, cs], B.idb[0:(VP if nm == "Vt" else 64), 0:(VP if nm == "Vt" else 64)],
                                         reads=(stk, "idb"), writes=("psTr%d" % pq,))
                                B.cp("act", cb[nm][zz][:, 0:NH * wd], psTr[pq][:, 0:NH * wd], reads=("psTr%d" % pq,), writes=("cb%s%d" % (nm, zz),))
                            mats = [("Brk", "KT", "RHO", mIU)]
                            if low:
                                mats += [("M0", "BT", "KAP", mSU), ("N0", "KAP", "BT", mSL), ("Akk", "KT", "KAP", mSU), ("Brb", "BT", "RHO", mIU)]
                            for nm, la_, ra_, mk in mats:
                                pa, pat = next_psA()
                                for h in range(NH):
                                    B.mm(pa[0:64, h * 64:(h + 1) * 64], O[la_][:, h, cs], O[ra_][:, h, cs], True, True,
                                         reads=(OT[la_], OT[ra_]), writes=(pat,))
                                B.tt("dve", cb[nm][zz][:].rearrange("p (h t) -> p h t", h=NH), pa[0:64, 0:W].rearrange("p (h t) -> p h t", h=NH),
                                     msk[:, mk:mk + 1, :].to_broadcast([64, NH, 64]), ALU.mult, reads=(pat, "msk"), writes=("cb%s%d" % (nm, zz),))
                            if low:
                                tb0 = (c % 2) * 2

                                def v3(ap):
                                    return ap.rearrange("p (h t) -> p h t", h=NH)

                                def lm(k):
                                    return lvl[:, k:k + 1, :].to_broadcast([64, NH, 64])
                                M0v, N0v = v3(cb["M0"][zz][:]), v3(cb["N0"][zz][:])
                                m0t, n0t = "cbM0%d" % zz, "cbN0%d" % zz
                                Tm, Tmt = cb["Nn"][0], "cbNn0"
                                Tt, Ttt = cb["Tt"][tb0], "cbTt%d" % tb0
                                B.tt("dve", v3(Tm[:]), N0v, lm(0), ALU.mult, reads=(n0t, "lvl"), writes=(Tmt,))
                                B.tt("dve", Tm[:], Tm[:], idr[:, 0:W], ALU.add, reads=(Tmt, "idr"), writes=(Tmt,))
                                B.tt("dve", v3(Tt[:]), M0v, lm(0), ALU.mult, reads=(m0t, "lvl"), writes=(Ttt,))
                                B.tt("dve", Tt[:], Tt[:], idr[:, 0:W], ALU.add, reads=(Ttt, "idr"), writes=(Ttt,))
                                for lev in range(1, 6):
                                    Ml, Mlt = cb["Mn"][0], "cbMn0"
                                    Pb, Pbt = cb["Mn"][1], "cbMn1"
                                    B.tt("dve", v3(Ml[:]), M0v, lm(lev), ALU.mult, reads=(m0t, "lvl"), writes=(Mlt,))
                                    pa, pat = next_psA()
                                    for h in range(NH):
                                        hs_ = slice(h * 64, (h + 1) * 64)
                                        B.mm(pa[0:64, hs_], Ml[:, hs_], Tm[:, hs_], True, True, reads=(Mlt, Tmt), writes=(pat,))
                                    B.cp("act", Pb[:], pa[0:64, 0:W], reads=(pat,), writes=(Pbt,))
                                    if lev < 5:
                                        pa2, pat2 = next_psA()
                                        for h in range(NH):
                                            hs_ = slice(h * 64, (h + 1) * 64)
                                            B.mm(pa2[0:64, hs_], Tt[:, hs_], Pb[:, hs_], True, True, reads=(Ttt, Pbt), writes=(pat2,))
                                    pa3, pat3 = next_psA()
                                    for h in range(NH):
                                        hs_ = slice(h * 64, (h + 1) * 64)
                                        B.mm(pa3[0:64, hs_], Pb[:, hs_], Tt[:, hs_], True, True, reads=(Pbt, Ttt), writes=(pat3,))
                                    if lev < 5:
                                        Tn, Tnt = cb["Nn"][lev % 2], "cbNn%d" % (lev % 2)
                                        B.tt("dve", Tn[:], pa2[0:64, 0:W], Tm[:], ALU.add, reads=(pat2, Tmt), writes=(Tnt,))
                                    T2, T2t = cb["Tt"][tb0 + (lev % 2)], "cbTt%d" % (tb0 + (lev % 2))
                                    B.tt("dve", T2[:], pa3[0:64, 0:W], Tt[:], ALU.add, reads=(pat3, Ttt), writes=(T2t,))
                                    Tt, Ttt = T2, T2t
                                    if lev < 5:
                                        Tm, Tmt = Tn, Tnt
                                return (Tt, Ttt)
                            return None

                        def seq(c, tinfo):
                            zz = c % 2
                            cs = slice(c * CH, (c + 1) * CH)
                            Vt, Vtt = cb["Vt"][zz], "cbVt%d" % zz
                            KHt, KHtt = cb["KHt"][zz], "cbKHt%d" % zz
                            Brk, Brkt = cb["Brk"][zz], "cbBrk%d" % zz
                            if low:
                                Tt, Ttt = tinfo
                                Akk, Akkt = cb["Akk"][zz], "cbAkk%d" % zz
                                Brb, Brbt = cb["Brb"][zz], "cbBrb%d" % zz
                                BHt, BHtt = cb["BHt"][zz], "cbBHt%d" % zz
                                Xb, Xbt = cb["Xb"][zz], "cbXb%d" % zz
                                Ub, Ubt = cb["Ub"][zz], "cbUb%d" % zz
                                pa, pat = next_psA()
                                for h in range(NH):
                                    hs_ = slice(h * 64, (h + 1) * 64)
                                    B.mm(pa[0:64, hs_], O["KAP"][:, h, cs], Hb[:, h, :], True, False, reads=(OT["KAP"], "Hb"), writes=(pat,))
                                    B.mm(pa[0:64, hs_], Akk[:, hs_], Vt[:, hs_], False, True, reads=(Akkt, Vtt), writes=(pat,))
                                B.cp("act", Xb[:], pa[0:64, 0:WV], reads=(pat,), writes=(Xbt,))
                                pa, pat = next_psA()
                                for h in range(NH):
                                    hs_ = slice(h * 64, (h + 1) * 64)
                                    B.mm(pa[0:64, hs_], Tt[:, hs_], Xb[:, hs_], True, True, reads=(Ttt, Xbt), writes=(pat,))
                                B.cp("act", Ub[:], pa[0:64, 0:WV], reads=(pat,), writes=(Ubt,))
                            pa, pat = next_psA()
                            for h in range(NH):
                                hs_ = slice(h * 64, (h + 1) * 64)
                                vs_ = slice(h * DV, (h + 1) * DV)
                                B.mm(pa[0:VP, hs_], Hb[:, h, :], O["RHO"][:, h, cs], True, False, reads=("Hb", OT["RHO"]), writes=(pat,))
                                B.mm(pa[0:VP, hs_], Vt[:, vs_], Brk[:, hs_], False, not low, reads=(Vtt, Brkt), writes=(pat,))
                                if low:
                                    B.mm(pa[0:VP, hs_], Ub[:, vs_], Brb[:, hs_], False, True, reads=(Ubt, Brbt), writes=(pat,))
                            yv = pa[0:VP, 0:W].rearrange("p (h t) -> p h t", h=NH)
                            if fwd:
                                B.cp("act", YF[z][:, :, cs], yv, reads=(pat,), writes=("YF%d" % z,))
                            else:
                                B.tt("dve", YF[z][:, :, cs], yv, YF[z][:, :, cs], ALU.add, reads=(pat, "YF%d" % z), writes=("YF%d" % z,))
                            pa, pat = next_psA()
                            for h in range(NH):
                                hs_ = slice(h * 64, (h + 1) * 64)
                                vs_ = slice(h * DV, (h + 1) * DV)
                                B.mm(pa[0:64, vs_], KHt[:, hs_], Vt[:, vs_], True, not low, reads=(KHtt, Vtt), writes=(pat,))
                                if low:
                                    B.mm(pa[0:64, vs_], BHt[:, hs_], Ub[:, vs_], False, True, reads=(BHtt, Ubt), writes=(pat,))
                            B.tt("dve", Hf[:], Hf[:], pC[z][:, :, c:c + 1].to_broadcast([64, NH, DV]), ALU.mult, reads=("Hf", "pC%d" % z), writes=("Hf",))
                            B.tt("dve", Hf[:], Hf[:], pa[0:64, 0:WV].rearrange("p (h v) -> p h v", h=NH), ALU.add, reads=("Hf", pat), writes=("Hf",))
                            B.cp("act", Hb[:], Hf[:], reads=("Hf",), writes=("Hb",))

                        order = list(range(4)) if fwd else [3, 2, 1, 0]
                        tinfo = pre(order[0])
                        for ci, c in enumerate(order):
                            nxt = pre(order[ci + 1]) if ci + 1 < 4 else None
                            seq(c, tinfo)
                            tinfo = nxt
                        B.dma("sp", stout[i, d, seg], Hf[:], reads=("Hf",), is_out=True)
                        B.ts("dve", Hf[:], Hf[:], flg[0:64, 2:3], None, ALU.mult, reads=("Hf", "flg"), writes=("Hf",))
                        B.cp("act", Hb[:], Hf[:], reads=("Hf",), writes=("Hb",))
                        if fwd:
                            B.dma("sp", YFS.rearrange("h p t -> p h t")[:, :, ssl], YF[z][:], reads=("YF%d" % z,), writes=("YFS%d" % seg,))
                        else:
                            for nm in gl:
                                sa = A["RG"] if (low and nm == "g") else (A["RBV"] if nm == "bv" else A["GG"])
                                B.dma("sp", gl[nm][z][:], sa.rearrange("h p t -> p h t")[:, :, ssl], writes=(glt[nm],))
                            ab_finish_segment(B, l, kind, low, NH, VP, YF[z], "YF%d" % z, gl, glt, z, nrm, nrmt, nrb, rc, gnw, ones1, next_psA, seg)
                B.fence()
        ab_rwkv_parallel(B, l, dict(tri=tri, msk2=msk2, idr2=idr2, lvl2=lvl2, rc2=rc2, omka2=omka2, gnw=gnw, ones1=ones1,
                                    psD=psD, psTr=psTr, psA=psA))
        B.fence()
    B.es = old


def ab_finish_segment(B, l, kind, low, NH, VP, Y, Yt, gl, glt, z, nrm, nrmt, nrb, rc, gnw, ones1, next_psA, seg):
    ssl = slice(seg * SEG, (seg + 1) * SEG)
    if not isinstance(nrm, list):
        nrm, nrmt, nrb = [nrm], [nrmt], [nrb]
    for hp in range(NH // 2):
        kk_ = hp % len(nrm)
        n_, nt_, nb_ = nrm[kk_], nrmt[kk_], nrb[kk_]
        ta_, tb_ = "nrba%d" % kk_, "nrbb%d" % kk_
        yv = Y[:, hp * 2:hp * 2 + 2, :]
        a, b_, c_ = n_["a"].rearrange("p (h t) -> p h t", h=2), n_["b"].rearrange("p (h t) -> p h t", h=2), n_["c"].rearrange("p (h t) -> p h t", h=2)
        ba, bb = nb_["a"][:].rearrange("p (h t) -> p h t", h=2), nb_["b"][:].rearrange("p (h t) -> p h t", h=2)
        B.act(bb, yv, AF.Square, reads=(Yt,), writes=(tb_,))
        pq, pqt = next_psA()
        B.mm(pq[0:VP, :], ones1[0:VP, 0:VP], nb_["b"][:], True, True, reads=("ones1b", tb_), writes=(pqt,))
        if low:
            B.cp("act", ba, yv, reads=(Yt,), writes=(ta_,))
            pm_, pmt = next_psA()
            B.mm(pm_[0:VP, :], ones1[0:VP, 0:VP], nb_["a"][:], True, True, reads=("ones1b", ta_), writes=(pmt,))
            B.act(n_["a"], pm_[0:VP, :], AF.Identity, scale=1.0 / 64.0, reads=(pmt,), writes=(nt_["a"],))
            B.tt("dve", n_["b"], n_["a"], n_["a"], ALU.mult, reads=(nt_["a"],), writes=(nt_["b"],))
            B.stt("dve", n_["b"], pq[0:VP, :], 1.0 / 64.0, n_["b"], ALU.mult, ALU.subtract, reads=(pqt, nt_["b"]), writes=(nt_["b"],))
            B.act(n_["b"], n_["b"], AF.Ln, bias=B.eps_ln[0:VP, 2:3], reads=(nt_["b"],), writes=(nt_["b"],))
            B.act(n_["b"], n_["b"], AF.Exp, scale=-0.5, reads=(nt_["b"],), writes=(nt_["b"],))
            B.tt("dve", c_, yv, a, ALU.subtract, reads=(Yt, nt_["a"]), writes=(nt_["c"],))
            B.tt("dve", c_, c_, b_, ALU.mult, reads=(nt_["c"], nt_["b"]), writes=(nt_["c"],))
            for hh in range(2):
                h = hp * 2 + hh
                B.ts("dve", c_[:, hh, :], c_[:, hh, :], rc[:, 24 + h:25 + h], rc[:, 32 + h:33 + h], ALU.mult, ALU.add, reads=(nt_["c"], "rc2"), writes=(nt_["c"],))
            B.tt("pool", c_, c_, gl["bv"][z][:, hp * 2:hp * 2 + 2, :], ALU.add, reads=(nt_["c"], glt["bv"]), writes=(nt_["c"],))
            B.tt("pool", B.ORW[:, hp * 2:hp * 2 + 2, ssl], c_, gl["g"][z][:, hp * 2:hp * 2 + 2, :], ALU.mult, reads=(nt_["c"], glt["g"]), writes=("ORW%d" % seg,))
        else:
            B.act(n_["b"], pq[0:VP, :], AF.Ln, scale=1.0 / 128.0, bias=B.eps_ln[0:VP, 0:1], reads=(pqt,), writes=(nt_["b"],))
            B.act(n_["b"], n_["b"], AF.Exp, scale=-0.5, reads=(nt_["b"],), writes=(nt_["b"],))
            B.stt("dve", c_, yv, gnw[:, 0:1], b_, ALU.mult, ALU.mult, reads=(Yt, "gnw", nt_["b"]), writes=(nt_["c"],))
            for hh in range(2):
                h = hp * 2 + hh
                tt = seg // 2
                B.tt("pool", B.HB[:, h, ssl], c_[:, hh, :], gl["g"][z][:, h, :], ALU.mult, reads=(nt_["c"], glt["g"]), writes=("HBg%d:%d" % (h, seg),))


def ab_phase3(B, l):
    i = l // 2
    A = ab_inputs(B)
    X, HB, modc = B.X, B.HB, B.modc
    xtok, htok = B.xtok, B.htok
    with ExitStack() as ph:
        B.es, old = ph, B.es
        for k in range(8):
            B.dma("sp", X[:, k, :], A["XS"][k], reads=("XS%d" % k,), writes=[xtok(k, tt) for tt in range(NT)])
        lnps = B.alloc_ln()
        wog = [B.sb("wog%d" % k, [128, 4, 128], BF16) for k in range(8)]
        wor = [B.sb("wor%d" % k, [64, 8, 128], BF16) for k in range(8)]
        otm = [B.sb("aotm%d" % k, [128, 512], F32) for k in range(2)]
        psO = [B.ps("apsO%d" % k, [128, 512], F32) for k in range(4)]
        for dch in range(8):
            B.dma("pool", wog[dch][:], A["abwo_g"][i, dch], writes=("wog%d" % dch,))
            B.dma("pool", wor[dch][:], A["abwo_r"][i, dch], writes=("wor%d" % dch,))
        oc = [0]

        def OP(tt):
            tsl = slice(tt * 512, (tt + 1) * 512)
            for dch in range(8):
                q = oc[0] % 4
                oc[0] += 1
                for h in range(4):
                    B.mm(psO[q][:], wog[dch][:, h, :], HB[:, h, tsl], h == 0, False, reads=("wog%d" % dch, htok(h, tt)), writes=("apsO%d" % q,))
                for h in range(8):
                    B.mm(psO[q][:], wor[dch][:, h, :], B.ORW[:, h, tsl], False, h == 7, reads=("wor%d" % dch,), writes=("apsO%d" % q,))
                c = l * 48 + 2 * 8 + dch
                B.act(otm[q % 2][:], psO[q][:], AF.Identity, scale=modc[:, c:c + 1], reads=("apsO%d" % q, "modc"), writes=("aotm%d" % (q % 2),))
                B.stt("dve", X[:, dch, tsl], X[:, dch, tsl], ALPHA, otm[q % 2][:], ALU.mult, ALU.add,
                      reads=(xtok(dch, tt), "aotm%d" % (q % 2)), writes=(xtok(dch, tt),))
                yield

        def LNt(tt):
            yield from B.ln_tile_gen(lnps, l, 0, tt, (l, 3))
        drain(OP(0))
        for tt in range(1, NT):
            drain(OP(tt), LNt(tt - 1))
        drain(LNt(NT - 1))
        B.fence()
    B.es = old


def ab_mixer(B, l):
    ab_phase1(B, l)
    with ExitStack() as mix:
        B.es, old = mix, B.es
        B.ORW = B.sb("ORW", [64, 8, T], BF16)
        ab_phase2(B, l)
        ab_phase3(B, l)
    B.es = old


def ab_rwkv_parallel(B, l, C):
    i = l // 2
    A = ab_inputs(B)
    P = B.P
    flg = B.flg
    NH, DV, W = 8, 64, 512
    tri, msk2, idr2, lvl2, rc2, omka2, ones1 = (C[k] for k in ("tri", "msk2", "idr2", "lvl2", "rc2", "omka2", "ones1"))
    psD, psTr, psA = C["psD"], C["psTr"], C["psA"]
    if "YBR" not in A:
        A["YBR"] = B.scratch("YBR", [8, 64, T], F32)
    with ExitStack() as kp:
        B.es, old = kp, B.es
        ldv = B.sb("ldrv", [128, NH, SEG], BF16)
        dec = [B.sb("decr%d" % d, [128, 2, W], F32) for d in range(2)]
        pC = B.sb("pCr", [128, NH, 4], F32)
        pt = {nm: B.sb("ptr" + nm, [128, NH, 128], F32) for nm in ("inc", "inv", "exc", "end", "b", "kd")}
        Hf = B.sb("Hfr", [128, NH, DV], F32)
        Hb = B.sb("Hbr", [128, NH, DV], BF16)
        cbn = [("KHt", 2), ("BHt", 2), ("Vt", 2), ("M0", 2), ("N0", 2), ("Akk", 2), ("Brk", 2), ("Brb", 2), ("Nn", 2), ("Mn", 2), ("Tt", 4), ("Xb", 2), ("Ub", 2)]
        cb = {nm: [B.sb("cbp%s%d" % (nm, z), [128, W], BF16) for z in range(nb)] for nm, nb in cbn}
        nrb = {nm: B.sb("nrbp" + nm, [64, 512], BF16) for nm in ("a", "b")}
        onm = ["RHO", "KT", "KH", "KAP", "BT", "BH"]
        names = ["q", "k", "v", "kk", "a"]
        src = dict(q=A["RR"], k=A["RK"], v=A["RV"], kk=A["RKK"])
        xslots = {"q": (6, 0), "k": (6, 1), "kk": (7, 0), "a": (7, 1)}

        def xview(po, k, half):
            return B.X[po:po + 64, k, half * 1024:(half + 1) * 1024].bitcast(BF16).rearrange("p (h t) -> p h t", h=NH)

        def stream(d):
            po = 64 * d
            ps_ = slice(po, po + 64)
            fwd = d == 0
            triI, triE, triS = (0, 1, 2) if fwd else (3, 2, 1)
            mSU, mIU, mSL = (0, 1, 2) if fwd else (2, 3, 0)
            msk, idr, lvl, rc, omka = msk2[ps_], idr2[ps_], lvl2[ps_], rc2[ps_], omka2[ps_]
            L = {nm: (ldv[ps_] if nm == "v" else xview(po, *xslots[nm])) for nm in names}
            opd = {nm: [xview(po, (oi_ * 2 + z) // 2, (oi_ * 2 + z) % 2) for z in range(2)] for oi_, nm in enumerate(onm)}
            YF = [B.HB[ps_, 4 + 2 * z:6 + 2 * z, :].rearrange("p k t -> p (k t)").bitcast(F32).rearrange("p (h t) -> p h t", h=NH) for z in range(2)]
            ptd = {nm: pt[nm][ps_] for nm in pt}
            Hfd, Hbd, pCd = Hf[ps_], Hb[ps_], pC[ps_]
            cbd = {nm: [t_[ps_] for t_ in cb[nm]] for nm in cb}
            YS = A["YFR"] if fwd else A["YBR"]
            pa_n = [0]

            def next_psA():
                q = 2 * d + (pa_n[0] % 2)
                pa_n[0] += 1
                return psA[q], "psA%d" % q
            pd, pdt = psD[d], "psD"
            pd2 = psD[d][:].rearrange("p a b -> p (a b)")
            SEQ_EVERY = B.cfg.get("seq_every", 2)
            pT, pTt = psTr[d], "psTr"
            idb = B.idb[ps_, po:po + 64]

            def v3(ap):
                return ap.rearrange("p (h t) -> p h t", h=NH)

            B.dma("sp", Hfd, A["rstate"][i, d], writes=("Hf",))
            B.cp("act", Hbd, Hfd, reads=("Hf",), writes=("Hb",))
            yield
            segs = list(range(NSEG)) if fwd else list(range(NSEG - 1, -1, -1))
            for sn, seg in enumerate(segs):
                z = sn % 2
                ssl = slice(seg * SEG, (seg + 1) * SEG)
                for nm in names:
                    sa = A["RA"][d] if nm == "a" else src[nm]
                    B.dma("sp", L[nm], sa.rearrange("h p t -> p h t")[:, :, ssl], writes=("ld" + nm,))
                B.dma("sp", dec[d][:], A["SG"][ssl, d * 512:(d + 1) * 512].rearrange("(q p) c -> p q c", p=128), writes=("dec",))
                O = {nm: opd[nm][z] for nm in onm}
                OT = {nm: "op%s%d" % (nm, z) for nm in onm}
                yield
                for tq in range(2):
                    qsl = slice(tq * 128, (tq + 1) * 128)
                    for var, trik, outs in ((0, triI, (("inc", -EXPC), ("inv", EXPC))), (1, triE, (("exc", -EXPC),)), (2, triS, (("end", -EXPC),))):
                        for hg in range(2):
                            for h4 in range(4):
                                h = hg * 4 + h4
                                B.mm(pd[:, h4, :], dec[d][:, tq, h * 64:(h + 1) * 64], tri[:, trik, :], True, True, reads=("dec",), writes=(pdt,))
                            for onm_, sc in outs:
                                B.act(ptd[onm_][:, hg * 4:(hg + 1) * 4, :], pd[:], AF.Exp, scale=sc, reads=(pdt,), writes=("pt" + onm_,))
                            yield
                    B.tt("dve", ptd["b"], L["a"][:, :, qsl], L["kk"][:, :, qsl], ALU.mult, reads=("lda", "ldkk"), writes=("ptb",))
                    for h in range(NH):
                        B.ts("dve", ptd["kd"][:, h, :], L["a"][:, h, qsl], rc[:, 8 + h:9 + h], omka[:, h:h + 1], ALU.mult, ALU.add,
                             reads=("lda",), writes=("ptkd",))
                    B.tt("dve", ptd["kd"], ptd["kd"], L["k"][:, :, qsl], ALU.mult, reads=("ptkd", "ldk"), writes=("ptkd",))
                    yield
                    B.tt("dve", O["RHO"][:, :, qsl], L["q"][:, :, qsl], ptd["inc"], ALU.mult, reads=("ldq", "ptinc"), writes=(OT["RHO"],))
                    B.tt("pool", O["KAP"][:, :, qsl], L["kk"][:, :, qsl], ptd["exc"], ALU.mult, reads=("ldkk", "ptexc"), writes=(OT["KAP"],))
                    B.stt("dve", O["BT"][:, :, qsl], ptd["b"], -1.0, ptd["inv"], ALU.mult, ALU.mult, reads=("ptb", "ptinv"), writes=(OT["BT"],))
                    yield
                    B.stt("dve", O["BH"][:, :, qsl], ptd["b"], -1.0, ptd["end"], ALU.mult, ALU.mult, reads=("ptb", "ptend"), writes=(OT["BH"],))
                    B.tt("pool", O["KT"][:, :, qsl], ptd["kd"], ptd["inv"], ALU.mult, reads=("ptkd", "ptinv"), writes=(OT["KT"],))
                    B.tt("pool", O["KH"][:, :, qsl], ptd["kd"], ptd["end"], ALU.mult, reads=("ptkd", "ptend"), writes=(OT["KH"],))
                    ccol = 63 if fwd else 0
                    B.cp("dve", pCd[:, :, tq * 2:(tq + 1) * 2], ptd["inc"][:, :, ccol:128:64], reads=("ptinc",), writes=("pC",))
                    yield

                def pre(c):
                    zz = c % 2
                    cs = slice(c * CH, (c + 1) * CH)
                    for nm, sarr, stk in (("KHt", O["KH"], OT["KH"]), ("BHt", O["BH"], OT["BH"]), ("Vt", L["v"], "ldv")):
                        for h in range(NH):
                            B.tr(pT[:, h * 64:(h + 1) * 64], sarr[:, h, cs], idb, reads=(stk,), writes=(pTt,))
                        B.cp("act", cbd[nm][zz], pT[:, 0:W], reads=(pTt,), writes=("cb%s%d" % (nm, zz),))
                        yield
                    for nm, la_, ra_, mk in (("Brk", "KT", "RHO", mIU), ("M0", "BT", "KAP", mSU), ("N0", "KAP", "BT", mSL),
                                             ("Akk", "KT", "KAP", mSU), ("Brb", "BT", "RHO", mIU)):
                        pa, pat = next_psA()
                        for h in range(NH):
                            B.mm(pa[0:64, h * 64:(h + 1) * 64], O[la_][:, h, cs], O[ra_][:, h, cs], True, True, reads=(OT[la_], OT[ra_]), writes=(pat,))
                        B.tt("dve", v3(cbd[nm][zz]), v3(pa[0:64, 0:W]), msk[:, mk:mk + 1, :].to_broadcast([64, NH, 64]), ALU.mult,
                             reads=(pat,), writes=("cb%s%d" % (nm, zz),))
                        yield
                    tb0 = (c % 2) * 2

                    def lm(k):
                        return lvl[:, k:k + 1, :].to_broadcast([64, NH, 64])
                    M0v, N0v = v3(cbd["M0"][zz]), v3(cbd["N0"][zz])
                    m0t, n0t = "cbM0%d" % zz, "cbN0%d" % zz
                    Tm, Tmt = cbd["Nn"][0], "cbNn0"
                    Tt, Ttt = cbd["Tt"][tb0], "cbTt%d" % tb0
                    B.tt("dve", v3(Tm), N0v, lm(0), ALU.mult, reads=(n0t,), writes=(Tmt,))
                    B.tt("pool", Tm, Tm, idr[:, 0:W], ALU.add, reads=(Tmt,), writes=(Tmt,))
                    B.tt("dve", v3(Tt), M0v, lm(0), ALU.mult, reads=(m0t,), writes=(Ttt,))
                    B.tt("pool", Tt, Tt, idr[:, 0:W], ALU.add, reads=(Ttt,), writes=(Ttt,))
                    yield
                    for lev in range(1, 6):
                        Ml, Mlt = cbd["Mn"][0], "cbMn0"
                        Pb, Pbt = cbd["Mn"][1], "cbMn1"
                        B.tt("dve", v3(Ml), M0v, lm(lev), ALU.mult, reads=(m0t,), writes=(Mlt,))
                        pa, pat = next_psA()
                        for h in range(NH):
                            hs_ = slice(h * 64, (h + 1) * 64)
                            B.mm(pa[0:64, hs_], Ml[:, hs_], Tm[:, hs_], True, True, reads=(Mlt, Tmt), writes=(pat,))
                        B.cp("act", Pb, pa[0:64, 0:W], reads=(pat,), writes=(Pbt,))
                        yield
                        if lev < 5:
                            pa2, pat2 = next_psA()
                            for h in range(NH):
                                hs_ = slice(h * 64, (h + 1) * 64)
                                B.mm(pa2[0:64, hs_], Tt[:, hs_], Pb[:, hs_], True, True, reads=(Ttt, Pbt), writes=(pat2,))
                            Tn, Tnt = cbd["Nn"][lev % 2], "cbNn%d" % (lev % 2)
                            B.tt("dve", Tn, pa2[0:64, 0:W], Tm, ALU.add, reads=(pat2, Tmt), writes=(Tnt,))
                            yield
                        pa3, pat3 = next_psA()
                        for h in range(NH):
                            hs_ = slice(h * 64, (h + 1) * 64)
                            B.mm(pa3[0:64, hs_], Pb[:, hs_], Tt[:, hs_], True, True, reads=(Pbt, Ttt), writes=(pat3,))
                        T2, T2t = cbd["Tt"][tb0 + (lev % 2)], "cbTt%d" % (tb0 + (lev % 2))
                        B.tt("dve", T2, pa3[0:64, 0:W], Tt, ALU.add, reads=(pat3, Ttt), writes=(T2t,))
                        Tt, Ttt = T2, T2t
                        if lev < 5:
                            Tm, Tmt = Tn, Tnt
                        yield
                    self_t[c] = (Tt, Ttt)

                def seq(c):
                    zz = c % 2
                    cs = slice(c * CH, (c + 1) * CH)
                    Tt, Ttt = self_t[c]
                    Vt, Vtt = cbd["Vt"][zz], "cbVt%d" % zz
                    KHt, KHtt = cbd["KHt"][zz], "cbKHt%d" % zz
                    BHt, BHtt = cbd["BHt"][zz], "cbBHt%d" % zz
                    Brk, Brkt = cbd["Brk"][zz], "cbBrk%d" % zz
                    Brb, Brbt = cbd["Brb"][zz], "cbBrb%d" % zz
                    Akk, Akkt = cbd["Akk"][zz], "cbAkk%d" % zz
                    Xb, Xbt = cbd["Xb"][zz], "cbXb%d" % zz
                    Ub, Ubt = cbd["Ub"][zz], "cbUb%d" % zz
                    pa, pat = pd2, "psD"
                    for h in range(NH):
                        hs_ = slice(h * 64, (h + 1) * 64)
                        B.mm(pa[0:64, hs_], O["KAP"][:, h, cs], Hbd[:, h, :], True, False, reads=(OT["KAP"], "Hb"), writes=(pat,))
                        B.mm(pa[0:64, hs_], Akk[:, hs_], Vt[:, hs_], False, True, reads=(Akkt, Vtt), writes=(pat,))
                    B.cp("act", Xb, pa[0:64, 0:W], reads=(pat,), writes=(Xbt,))
                    yield
                    pa, pat = pd2, "psD"
                    for h in range(NH):
                        hs_ = slice(h * 64, (h + 1) * 64)
                        B.mm(pa[0:64, hs_], Tt[:, hs_], Xb[:, hs_], True, True, reads=(Ttt, Xbt), writes=(pat,))
                    B.cp("act", Ub, pa[0:64, 0:W], reads=(pat,), writes=(Ubt,))
                    yield
                    pa, pat = pd2, "psD"
                    for h in range(NH):
                        hs_ = slice(h * 64, (h + 1) * 64)
                        B.mm(pa[0:64, hs_], Hbd[:, h, :], O["RHO"][:, h, cs], True, False, reads=("Hb", OT["RHO"]), writes=(pat,))
                        B.mm(pa[0:64, hs_], Vt[:, hs_], Brk[:, hs_], False, False, reads=(Vtt, Brkt), writes=(pat,))
                        B.mm(pa[0:64, hs_], Ub[:, hs_], Brb[:, hs_], False, True, reads=(Ubt, Brbt), writes=(pat,))
                    B.cp("act", YF[z][:, :, cs], v3(pa[0:64, 0:W]), reads=(pat,), writes=("YF%d" % z,))
                    yield
                    pa, pat = pd2, "psD"
                    for h in range(NH):
                        hs_ = slice(h * 64, (h + 1) * 64)
                        B.mm(pa[0:64, hs_], KHt[:, hs_], Vt[:, hs_], True, False, reads=(KHtt, Vtt), writes=(pat,))
                        B.mm(pa[0:64, hs_], BHt[:, hs_], Ub[:, hs_], False, True, reads=(BHtt, Ubt), writes=(pat,))
                    B.tt("dve", Hfd, Hfd, pCd[:, :, c:c + 1].to_broadcast([64, NH, DV]), ALU.mult, reads=("Hf", "pC"), writes=("Hf",))
                    B.tt("dve", Hfd, Hfd, v3(pa[0:64, 0:W]), ALU.add, reads=("Hf", pat), writes=("Hf",))
                    B.cp("act", Hbd, Hfd, reads=("Hf",), writes=("Hb",))
                    yield

                self_t = {}
                order = list(range(4)) if fwd else [3, 2, 1, 0]
                yield from pre(order[0])
                for ci, c in enumerate(order):
                    gs = seq(c)
                    if ci + 1 < 4:
                        k_ = 0
                        for _ in pre(order[ci + 1]):
                            yield
                            k_ += 1
                            if k_ % SEQ_EVERY == 0:
                                try:
                                    next(gs)
                                    yield
                                except StopIteration:
                                    pass
                    yield from gs
                B.dma("sp", A["rst_out"][i, d, seg], Hfd, reads=("Hf",), is_out=True)
                B.ts("dve", Hfd, Hfd, flg[ps_, 2:3], None, ALU.mult, reads=("Hf",), writes=("Hf",))
                B.cp("act", Hbd, Hfd, reads=("Hf",), writes=("Hb",))
                B.dma("sp", YS.rearrange("h p t -> p h t")[:, :, ssl], YF[z], reads=("YF%d" % z,), writes=("YS%d" % seg,))
                yield

        gens = [stream(0), stream(1)]
        alive = [0, 1]
        while alive:
            for d in list(alive):
                P.ns = d
                try:
                    next(gens[d])
                except StopIteration:
                    alive.remove(d)
        P.ns = None
        B.fence()
        YA = B.HB[0:64, 4:6, :].rearrange("p k t -> p (k t)").bitcast(F32).rearrange("p (h t) -> p h t", h=NH)
        YBv = B.HB[0:64, 6:8, :].rearrange("p k t -> p (k t)").bitcast(F32).rearrange("p (h t) -> p h t", h=NH)
        glg = xview(0, 7, 0)
        glbv = xview(0, 7, 1)
        nrm = [{nm: pt[pn][0:64].rearrange("p h t -> p (h t)")[:, k_ * 512:(k_ + 1) * 512] for nm, pn in (("a", "inc"), ("b", "inv"), ("c", "exc"))}
               for k_ in range(2)]
        nrmt = [dict(a="fna%d" % k_, b="fnb%d" % k_, c="fnc%d" % k_) for k_ in range(2)]
        nrb = [nrb, {nm: pt[pn][0:64].rearrange("p h t -> p (h t)")[:, 0:256].bitcast(BF16) for nm, pn in (("a", "end"), ("b", "b"))}]
        pa_n = [0]

        def next_psA2():
            q = pa_n[0] % 4
            pa_n[0] += 1
            return psA[q], "psA%d" % q
        for seg in range(NSEG):
            ssl = slice(seg * SEG, (seg + 1) * SEG)
            B.dma("sp", YA, A["YFR"].rearrange("h p t -> p h t")[:, :, ssl], writes=("YA",))
            B.dma("sp", YBv, A["YBR"].rearrange("h p t -> p h t")[:, :, ssl], writes=("YB",))
            B.dma("sp", glg, A["RG"].rearrange("h p t -> p h t")[:, :, ssl], writes=("glg",))
            B.dma("sp", glbv, A["RBV"].rearrange("h p t -> p h t")[:, :, ssl], writes=("glbv",))
            B.tt("dve", YA, YA, YBv, ALU.add, reads=("YA", "YB"), writes=("YA",))
            ab_finish_segment(B, l, "r", True, NH, 64, YA, "YA", {"g": [glg] * 2, "bv": [glbv] * 2}, {"g": "glg", "bv": "glbv"}, 0,
                              nrm, nrmt, nrb, rc2[0:64], C["gnw"], ones1, next_psA2, seg)
        B.fence()
    B.es = old
```

```python
import math
from contextlib import ExitStack
import numpy as np
import ml_dtypes
import concourse.bass as bass
import concourse.mybir as mybir
from concourse.bass_utils import run_bass_kernel_spmd

F32 = mybir.dt.float32
BF16 = mybir.dt.bfloat16
AF = mybir.ActivationFunctionType
ALU = mybir.AluOpType
NPBF = ml_dtypes.bfloat16

D = 1024
T = 2048
DEPTH = 4
DFF = 2816
NM = DFF // 128
ALPHA = (2 * DEPTH) ** 0.25
LN_EPS = 1e-5
NT = T // 512
SEG = 256
NSEG = T // SEG


class Prog:
    ENGS = ("pe", "act", "dve", "pool", "sp")
    NDS = 12

    def __init__(self, nc, same_eng_sync=False):
        self.nc = nc
        self.ops = []
        self.per_eng = {e: [] for e in self.ENGS}
        self.lastw = {}
        self.rd_c = {}
        self.rd_d = {}
        self.dma_rr = {e: 0 for e in self.ENGS}
        self.dma_last = {}
        self.same_eng_sync = same_eng_sync
        self.out_dmas = []
        self.fence_id = None
        self.ns = None

    def op(self, eng, fn, reads=(), writes=(), dma=False, is_out=False):
        i = len(self.ops)
        if self.ns is not None:
            reads = [(self.ns, r) for r in reads]
            writes = [(self.ns, w) for w in writes]
        deps = set()
        for r in reads:
            w = self.lastw.get(r)
            if w is not None:
                deps.add(w)
        for w in writes:
            lw = self.lastw.get(w)
            if lw is not None:
                deps.add(lw)
            deps.update(self.rd_c.get(w, {}).values())
            deps.update(self.rd_d.get(w, ()))
        if self.fence_id is not None:
            deps.add(self.fence_id)
        semslot = None
        if dma:
            k = self.dma_rr[eng]
            self.dma_rr[eng] = (k + 1) % self.NDS
            prev = self.dma_last.get((eng, k))
            if prev is not None:
                deps.add(prev)
            self.dma_last[(eng, k)] = i
            semslot = (eng, k)
        fdeps = set()
        for d in deps:
            od = self.ops[d]
            if od["eng"] == eng and not od["dma"] and not dma and not self.same_eng_sync:
                continue
            if od["eng"] == eng and not od["dma"] and eng == "pe":
                continue
            fdeps.add(d)
            od["flag"] = True
        o = dict(id=i, eng=eng, fn=fn, deps=fdeps, dma=dma, flag=False, semslot=semslot)
        for r in reads:
            if dma:
                self.rd_d.setdefault(r, set()).add(i)
            else:
                self.rd_c.setdefault(r, {})[eng] = i
        for w in writes:
            self.lastw[w] = i
            self.rd_c[w] = {}
            self.rd_d[w] = set()
        self.ops.append(o)
        self.per_eng[eng].append(i)
        if is_out:
            self.out_dmas.append(i)
        return i

    def emit(self):
        nc = self.nc
        ops = self.ops
        fin = dict(id=len(ops), eng="sp", fn=None, deps=set(self.out_dmas), dma=False, flag=False, semslot=None)
        ops.append(fin)
        self.per_eng["sp"].append(fin["id"])
        with ExitStack() as es:
            esem = {e: es.enter_context(nc.semaphore("s_" + e)) for e in ("pe", "act", "dve", "pool")}
            dsem = {}
            for q in ("sp", "pool", "act"):
                for k in range(self.NDS):
                    if (q, k) in self.dma_last:
                        dsem[(q, k)] = es.enter_context(nc.semaphore("d_%s%d" % (q, k)))
            cnt = {e: 0 for e in esem}
            dcnt = {k: 0 for k in dsem}
            semof = {}
            for o in ops:
                if o["dma"]:
                    dcnt[o["semslot"]] += 16
                    semof[o["id"]] = (dsem[o["semslot"]], dcnt[o["semslot"]])
                elif o["flag"]:
                    cnt[o["eng"]] += 1
                    semof[o["id"]] = (esem[o["eng"]], cnt[o["eng"]])
            block = es.enter_context(nc.Block())
            names = dict(pe="tensor", act="scalar", dve="vector", pool="gpsimd", sp="sync")
            for eng in self.ENGS:
                lst = self.per_eng[eng]
                if not lst:
                    continue

                def body(e, lst=lst):
                    waited = {}
                    for i in lst:
                        o = ops[i]
                        need = {}
                        for d in o["deps"]:
                            s, v = semof[d]
                            if need.get(id(s), (None, 0))[1] < v:
                                need[id(s)] = (s, v)
                        for s, v in need.values():
                            if waited.get(id(s), 0) >= v:
                                continue
                            e.wait_ge(s, v)
                            waited[id(s)] = v
                        if o["fn"] is None:
                            continue
                        ins = o["fn"](e)
                        if o["dma"]:
                            ins.then_inc(semof[i][0], 16)
                        elif o["flag"]:
                            ins.then_inc(semof[i][0], 1)

                getattr(block, names[eng])(body)


def drain(*gens):
    alive = list(gens)
    while alive:
        for g in list(alive):
            try:
                next(g)
            except StopIteration:
                alive.remove(g)


def _pk(w, kparts=128):
    K, N = w.shape
    return np.ascontiguousarray(w.reshape(K // kparts, kparts, N).transpose(1, 0, 2))


def _col(v):
    return np.ascontiguousarray(v.reshape(-1, 128).T)


class Builder:
    def __init__(self, cfg):
        self.cfg = cfg
        self.nc = bass.Bass("TRN2", target_bir_lowering=False)
        self.P = Prog(self.nc, same_eng_sync=cfg.get("same_eng_sync", False))
        self.es = ExitStack()
        self.din = {}
        self.dout = {}
        self.uid = 0

    def inp(self, name, shape, dt=F32):
        self.din[name] = self.nc.dram_tensor(name, list(shape), dt, kind="ExternalInput").ap()
        return self.din[name]

    def outp(self, name, shape, dt=F32):
        self.dout[name] = self.nc.dram_tensor(name, list(shape), dt, kind="ExternalOutput").ap()
        return self.dout[name]

    def scratch(self, name, shape, dt=F32):
        return self.nc.dram_tensor(name, list(shape), dt).ap()

    def sb(self, name, shape, dt=F32):
        self.uid += 1
        return self.es.enter_context(self.nc.sbuf_tensor("%s_%d" % (name, self.uid), list(shape), dt))

    def ps(self, name, shape, dt=F32):
        self.uid += 1
        return self.es.enter_context(self.nc.psum_tensor("%s_%d" % (name, self.uid), list(shape), dt))

    def dma(self, q, out, in_, reads=(), writes=(), is_out=False):
        return self.P.op(q, lambda e, out=out, in_=in_: e.dma_start(out=out, in_=in_), reads, writes, dma=True, is_out=is_out)

    def mm(self, out, lhsT, rhs, start, stop, reads=(), writes=()):
        return self.P.op("pe", lambda e, out=out, lhsT=lhsT, rhs=rhs, start=start, stop=stop:
                         e.matmul(out, lhsT, rhs, start=start, stop=stop), reads, writes)

    def tr(self, out, in_, ident, reads=(), writes=()):
        return self.P.op("pe", lambda e, out=out, in_=in_, ident=ident: e.transpose(out, in_, ident), reads, writes)

    def act(self, out, in_, func, bias=0.0, scale=1.0, reads=(), writes=(), eng="act"):
        return self.P.op(eng, lambda e, out=out, in_=in_, func=func, bias=bias, scale=scale:
                         e.activation(out=out, in_=in_, func=func, bias=bias, scale=scale), reads, writes)

    def ts(self, eng, out, in0, s1, s2, op0, op1=None, reads=(), writes=()):
        if op1 is None:
            return self.P.op(eng, lambda e, out=out, in0=in0, s1=s1, op0=op0:
                             e.tensor_scalar(out=out, in0=in0, scalar1=s1, scalar2=None, op0=op0), reads, writes)
        return self.P.op(eng, lambda e, out=out, in0=in0, s1=s1, s2=s2, op0=op0, op1=op1:
                         e.tensor_scalar(out=out, in0=in0, scalar1=s1, scalar2=s2, op0=op0, op1=op1), reads, writes)

    def tt(self, eng, out, in0, in1, op, reads=(), writes=()):
        return self.P.op(eng, lambda e, out=out, in0=in0, in1=in1, op=op:
                         e.tensor_tensor(out=out, in0=in0, in1=in1, op=op), reads, writes)

    def stt(self, eng, out, in0, scalar, in1, op0, op1, reads=(), writes=()):
        return self.P.op(eng, lambda e, out=out, in0=in0, scalar=scalar, in1=in1, op0=op0, op1=op1:
                         e.scalar_tensor_tensor(out=out, in0=in0, scalar=scalar, in1=in1, op0=op0, op1=op1), reads, writes)

    def cp(self, eng, out, in_, reads=(), writes=()):
        if eng == "act":
            return self.P.op(eng, lambda e, out=out, in_=in_: e.copy(out=out, in_=in_), reads, writes)
        return self.P.op(eng, lambda e, out=out, in_=in_: e.tensor_copy(out=out, in_=in_), reads, writes)

    def recip(self, out, in_, reads=(), writes=()):
        return self.P.op("dve", lambda e, out=out, in_=in_: e.reciprocal(out=out, in_=in_), reads, writes)

    def memset(self, eng, ap, val, writes=()):
        return self.P.op(eng, lambda e, ap=ap, val=val: e.memset(ap, val), (), writes)

    def fence(self):
        P = self.P
        deps_tokens = list(P.lastw.keys())
        i = P.op("dve", lambda e, ap=self.dummy[:, 0:1]: e.memset(ap, 0.0), reads=(), writes=tuple(deps_tokens))
        P.lastw = {}
        P.rd_c = {}
        P.rd_d = {}
        P.fence_id = i

    def R(self, *toks):
        return tuple(toks)


def build(cfg):
    B = Builder(cfg)
    nc, P = B.nc, B.P
    mixers = cfg.get("mixers", True)
    nlayers = cfg.get("nlayers", DEPTH)

    xT = B.inp("xT", [D, T])
    posT = B.inp("posT", [D, T])
    condc = B.inp("condc", [128, 8])
    flags = B.inp("flags", [128, 4])
    wmod = B.inp("wmod", [DEPTH, 12, 128, 8, 512])
    bmod = B.inp("bmod", [DEPTH, 128, 48])
    lng = B.inp("lng", [128, DEPTH * 2 * 8])
    lnb = B.inp("lnb", [128, DEPTH * 2 * 8])
    wg = B.inp("wg", [DEPTH, 11, 128, 8, 256])
    wu = B.inp("wu", [DEPTH, 11, 128, 8, 256])
    wd = B.inp("wd", [DEPTH, 8, 128, NM, 128])
    yT = B.outp("yT", [D, T])
    B.io = dict(xT=xT, posT=posT, condc=condc, flags=flags)

    X = B.sb("X", [128, 8, T], F32)
    HB = B.sb("HB", [128, 8, T], BF16)
    modc = B.sb("modc", [128, DEPTH * 48], F32)
    lngs = B.sb("lngs", [128, DEPTH * 16], F32)
    lnbs = B.sb("lnbs", [128, DEPTH * 16], F32)
    flg = B.sb("flg", [128, 4], F32)
    ones_bf = B.sb("ones_bf", [128, 128], BF16)
    B.dummy = B.sb("fdummy", [128, 2], F32)
    B.eps_ln = B.sb("eps_ln", [128, 3], F32)
    B.memset("dve", B.eps_ln[:, 0:1], LN_EPS, writes=("epsln",))
    B.memset("dve", B.eps_ln[:, 1:2], 1e-24, writes=("epsln",))
    B.memset("dve", B.eps_ln[:, 2:3], 64e-5, writes=("epsln",))
    B.X, B.HB, B.modc, B.flg, B.ones_bf = X, HB, modc, flg, ones_bf
    B.lngs, B.lnbs = lngs, lnbs

    def xtok(k, tt):
        return "X%d:%d" % (k, tt)

    def htok(k, tt):
        return "H%d:%d" % (k, tt)
    B.xtok, B.htok = xtok, htok

    B.dma("sp", lngs[:], lng, writes=("lng",))
    B.dma("sp", lnbs[:], lnb, writes=("lnb",))
    B.dma("sp", flg[:], flags, writes=("flg",))
    B.memset("dve", ones_bf[:], 1.0 / 1024.0, writes=("ones",))
    xv = xT.rearrange("(k p) t -> p k t", p=128)
    pv = posT.rearrange("(k p) t -> p k t", p=128)
    with ExitStack() as ph:
        B.es, old_es = ph, B.es
        pst = [B.sb("pst%d" % i, [128, T], F32) for i in range(2)]
        cnd = B.sb("cnd", [128, 8], F32)
        scb = B.sb("scb", [128, 8], BF16)
        bms = B.sb("bms", [128, DEPTH * 48], F32)
        wms = [B.sb("wms%d" % i, [128, 8, 512], BF16) for i in range(2)]
        psM = B.ps("psM", [128, 512], F32)
        for k in range(8):
            B.dma("sp", X[:, k, :], xv[:, k, :], writes=[xtok(k, tt) for tt in range(NT)])
            B.dma("sp", pst[k % 2][:], pv[:, k, :], writes=("pst%d" % (k % 2),))
            B.tt("dve", X[:, k, :], X[:, k, :], pst[k % 2][:], ALU.add,
                 reads=["pst%d" % (k % 2)] + [xtok(k, tt) for tt in range(NT)], writes=[xtok(k, tt) for tt in range(NT)])
        B.dma("sp", cnd[:], condc, writes=("cnd",))
        for l in range(DEPTH):
            B.dma("sp", bms[:, l * 48:(l + 1) * 48], bmod[l], writes=("bms%d" % l,))
        B.act(scb[:], cnd[:], AF.Silu, reads=("cnd",), writes=("scb",))
        n = 0
        for l in range(DEPTH):
            for pn in range(12):
                s = n % 2
                n += 1
                B.dma("pool", wms[s][:], wmod[l, pn], writes=("wms%d" % s,))
                for jj in range(4):
                    col = l * 48 + pn * 4 + jj
                    for k in range(8):
                        B.mm(psM[:, col:col + 1], wms[s][:, k, jj * 128:(jj + 1) * 128], scb[:, k:k + 1], k == 0, k == 7,
                             reads=("wms%d" % s, "scb"), writes=("psM",))
        B.tt("dve", modc[:], psM[:, 0:DEPTH * 48], bms[:], ALU.add,
             reads=["psM"] + ["bms%d" % l for l in range(DEPTH)], writes=("modc",))
        for l in range(DEPTH):
            for g in (1, 4):
                c0 = l * 48 + g * 8
                B.ts("dve", modc[:, c0:c0 + 8], modc[:, c0:c0 + 8], 1.0, None, ALU.add, reads=("modc",), writes=("modc",))
        B.fence()
    B.es = old_es
    make_ident(B)
    if mixers and nlayers > 1:
        hyena_filters(B)
    for k in range(8):
        B.ts("dve", HB[:, k, :], X[:, k, :], modc[:, 8 + k:9 + k], modc[:, k:k + 1], ALU.mult, ALU.add,
             reads=["modc"] + [xtok(k, tt) for tt in range(NT)], writes=[htok(k, tt) for tt in range(NT)])

    def ln_tile(lnps, l, which, tt, nxt):
        for _ in ln_tile_gen(lnps, l, which, tt, nxt):
            pass

    def ln_tile_gen(lnps, l, which, tt, nxt):
        ybf, ysq, mean, msq, rstd, nmr, tmp, ps1, ps2 = lnps
        tsl = slice(tt * 512, (tt + 1) * 512)
        for k in range(8):
            B.cp("act", ybf[:, k, :], X[:, k, tsl], reads=B.R(xtok(k, tt)), writes=("ybf%d" % k,))
            B.act(ysq[:, k, :], X[:, k, tsl], AF.Square, reads=B.R(xtok(k, tt)), writes=("ysq%d" % k,))
            if k % 2 == 1:
                yield
        for k in range(8):
            B.mm(ps1[:], ones_bf[:], ybf[:, k, :], k == 0, k == 7, reads=B.R("ones", "ybf%d" % k), writes=("lnps1",))
        for k in range(8):
            B.mm(ps2[:], ones_bf[:], ysq[:, k, :], k == 0, k == 7, reads=B.R("ones", "ysq%d" % k), writes=("lnps2",))
        yield
        B.cp("act", mean[:], ps1[:], reads=B.R("lnps1"), writes=("mean",))
        B.tt("dve", msq[:], mean[:], mean[:], ALU.mult, reads=B.R("mean"), writes=("msq",))
        B.tt("dve", rstd[:], ps2[:], msq[:], ALU.subtract, reads=B.R("lnps2", "msq"), writes=("rstd",))
        B.act(rstd[:], rstd[:], AF.Ln, bias=B.eps_ln[:, 0:1], reads=B.R("rstd"), writes=("rstd",))
        B.act(rstd[:], rstd[:], AF.Exp, scale=-0.5, reads=B.R("rstd"), writes=("rstd",))
        B.stt("dve", nmr[:], mean[:], -1.0, rstd[:], ALU.mult, ALU.mult, reads=B.R("mean", "rstd"), writes=("nmr",))
        yield
        for k in range(8):
            tk = "lntmp%d" % (k % 2)
            tm = tmp[k % 2]
            B.tt("dve", tm[:], X[:, k, tsl], rstd[:], ALU.mult, reads=B.R(xtok(k, tt), "rstd"), writes=(tk,))
            B.tt("dve", tm[:], tm[:], nmr[:], ALU.add, reads=B.R(tk, "nmr"), writes=(tk,))
            c = l * 16 + which * 8 + k
            B.act(X[:, k, tsl], tm[:], AF.Identity, bias=lnbs[:, c:c + 1], scale=lngs[:, c:c + 1],
                  reads=B.R(tk, "lng", "lnb"), writes=(xtok(k, tt),))
            if nxt is not None:
                nl, g = nxt
                c1 = nl * 48 + (g + 1) * 8 + k
                c0 = nl * 48 + g * 8 + k
                B.ts("dve", HB[:, k, tsl], X[:, k, tsl], modc[:, c1:c1 + 1], modc[:, c0:c0 + 1], ALU.mult, ALU.add,
                     reads=B.R(xtok(k, tt), "modc"), writes=(htok(k, tt),))
            yield

    def alloc_ln():
        ybf = B.sb("ybf", [128, 8, 512], BF16)
        ysq = B.sb("ysq", [128, 8, 512], BF16)
        mean = B.sb("mean", [128, 512], F32)
        msq = B.sb("msq", [128, 512], F32)
        rstd = B.sb("rstd", [128, 512], F32)
        nmr = B.sb("nmr", [128, 512], F32)
        tmp = [B.sb("lntmp%d" % i, [128, 512], F32) for i in range(2)]
        ps1 = B.ps("lnps1", [128, 512], F32)
        ps2 = B.ps("lnps2", [128, 512], F32)
        return (ybf, ysq, mean, msq, rstd, nmr, tmp, ps1, ps2)
    B.ln_tile, B.alloc_ln, B.ln_tile_gen = ln_tile, alloc_ln, ln_tile_gen

    def ffn(l):
        with ExitStack() as ph:
            B.es, old = ph, B.es
            lnps = alloc_ln()
            A = B.sb("A", [128, NM, 1024], BF16)
            wgs = [B.sb("wgs%d" % i, [128, 8, 256], BF16) for i in range(2)]
            wus = [B.sb("wus%d" % i, [128, 8, 256], BF16) for i in range(2)]
            wds = [B.sb("wds%d" % i, [128, NM, 128], BF16) for i in range(2)]
            sg = [B.sb("sg%d" % i, [128, 512], F32) for i in range(2)]
            dtm = [B.sb("dtm%d" % i, [128, 512], F32) for i in range(2)]
            psG = [B.ps("psG%d" % i, [128, 512], F32) for i in range(2)]
            psU = [B.ps("psU%d" % i, [128, 512], F32) for i in range(2)]
            psD = [B.ps("psD%d" % i, [128, 512], F32) for i in range(2)]
            st = dict(wn=0, pn=0, dn=0)

            def G(half):
                tiles = (2 * half, 2 * half + 1)
                for pn in range(11):
                    s = st["wn"] % 2
                    st["wn"] += 1
                    B.dma("pool", wgs[s][:], wg[l, pn], reads=B.R(), writes=("wgs%d" % s,))
                    B.dma("pool", wus[s][:], wu[l, pn], reads=B.R(), writes=("wus%d" % s,))
                    for mi in range(2):
                        m = pn * 2 + mi
                        for ti, tt in enumerate(tiles):
                            tsl = slice(tt * 512, (tt + 1) * 512)
                            q = st["pn"] % 2
                            st["pn"] += 1
                            for k in range(8):
                                B.mm(psG[q][:], wgs[s][:, k, mi * 128:(mi + 1) * 128], HB[:, k, tsl], k == 0, k == 7,
                                     reads=B.R("wgs%d" % s, htok(k, tt)), writes=("psG%d" % q,))
                            for k in range(8):
                                B.mm(psU[q][:], wus[s][:, k, mi * 128:(mi + 1) * 128], HB[:, k, tsl], k == 0, k == 7,
                                     reads=B.R("wus%d" % s, htok(k, tt)), writes=("psU%d" % q,))
                            B.act(sg[q][:], psG[q][:], AF.Silu, reads=B.R("psG%d" % q), writes=("sg%d" % q,))
                            B.tt("dve", A[:, m, ti * 512:(ti + 1) * 512], sg[q][:], psU[q][:], ALU.mult,
                                 reads=B.R("sg%d" % q, "psU%d" % q), writes=("A%d:%d" % (m, ti),))
                            yield

            def Dn(half):
                tiles = (2 * half, 2 * half + 1)
                for dch in range(8):
                    s = st["dn"] % 2
                    st["dn"] += 1
                    B.dma("pool", wds[s][:], wd[l, dch], reads=B.R(), writes=("wds%d" % s,))
                    for ti, tt in enumerate(tiles):
                        tsl = slice(tt * 512, (tt + 1) * 512)
                        q = st["pn"] % 2
                        st["pn"] += 1
                        for m in range(NM):
                            B.mm(psD[q][:], wds[s][:, m, :], A[:, m, ti * 512:(ti + 1) * 512], m == 0, m == NM - 1,
                                 reads=B.R("wds%d" % s, "A%d:%d" % (m, ti)), writes=("psD%d" % q,))
                        c = l * 48 + 5 * 8 + dch
                        B.act(dtm[q][:], psD[q][:], AF.Identity, scale=modc[:, c:c + 1], reads=B.R("psD%d" % q, "modc"), writes=("dtm%d" % q,))
                        B.stt("dve", X[:, dch, tsl], X[:, dch, tsl], ALPHA, dtm[q][:], ALU.mult, ALU.add,
                              reads=B.R(xtok(dch, tt), "dtm%d" % q), writes=(xtok(dch, tt),))
                        yield

            def LNh(half):
                for tt in (2 * half, 2 * half + 1):
                    yield from ln_tile_gen(lnps, l, 1, tt, (l + 1, 0) if l + 1 < DEPTH else None)

            drain(G(0))
            drain(Dn(0))
            drain(G(1), LNh(0))
            drain(Dn(1))
            drain(LNh(1))
            B.fence()
        B.es = old
    B.ffn = ffn

    def null_mixer(l):
        with ExitStack() as ph:
            B.es, old = ph, B.es
            lnps = alloc_ln()
            for tt in range(NT):
                tsl = slice(tt * 512, (tt + 1) * 512)
                for k in range(8):
                    B.ts("dve", X[:, k, tsl], X[:, k, tsl], ALPHA, None, ALU.mult, reads=B.R(xtok(k, tt)), writes=(xtok(k, tt),))
                ln_tile(lnps, l, 0, tt, (l, 3))
            B.fence()
        B.es = old

    for l in range(nlayers):
        if mixers and l % 2 == 0 and "ab_mixer" in globals():
            ab_mixer(B, l)
        elif mixers and l % 2 == 1 and "hyena_mixer" in globals():
            hyena_mixer(B, l)
        else:
            null_mixer(l)
        ffn(l)

    yv = yT.rearrange("(k p) t -> p k t", p=128)
    for k in range(8):
        B.dma("sp", yv[:, k, :], X[:, k, :], reads=B.R(*[xtok(k, tt) for tt in range(NT)]), is_out=True)
    P.emit()
    return B


_CACHE = {}


def _grid_pos_T():
    quarter = D // 4
    omega = (1.0 / (10000.0 ** (np.arange(quarter, dtype=np.float32) / np.float32(quarter)))).astype(np.float32)
    idx = np.arange(T)

    def sincos(pos):
        ang = pos.astype(np.float32)[:, None] * omega[None, :]
        return np.concatenate([np.sin(ang), np.cos(ang)], -1)
    pe = np.concatenate([sincos(idx // 64), sincos(idx % 64)], -1).astype(np.float32)
    return np.ascontiguousarray(pe.T)


def prep_shared(inp, cfg):
    f = lambda a: np.ascontiguousarray(np.asarray(a, dtype=np.float32))
    sh = {}
    sh["wmod"] = f(inp["w_mod"].reshape(DEPTH, 8, 128, 12, 512).transpose(0, 3, 2, 1, 4))
    sh["bmod"] = f(inp["b_mod"].reshape(DEPTH, 48, 128).transpose(0, 2, 1))
    sh["lng"] = f(inp["ln_g"].reshape(DEPTH, 2, 8, 128).transpose(3, 0, 1, 2).reshape(128, DEPTH * 16))
    sh["lnb"] = f(inp["ln_b"].reshape(DEPTH, 2, 8, 128).transpose(3, 0, 1, 2).reshape(128, DEPTH * 16))
    sh["wg"] = f(inp["ffn_w_gate"].reshape(DEPTH, 8, 128, 11, 256).transpose(0, 3, 2, 1, 4))
    sh["wu"] = f(inp["ffn_w_up"].reshape(DEPTH, 8, 128, 11, 256).transpose(0, 3, 2, 1, 4))
    sh["wd"] = f(inp["ffn_w_down"].reshape(DEPTH, NM, 128, 8, 128).transpose(0, 3, 2, 1, 4))
    wi = inp["ab_w_in"]
    abwin = np.zeros((2, NG, 128, 8, 128), np.float32)
    abmu = np.zeros((2, 128, NG * 2), np.float32)
    for n, (nm, c0, wdt) in enumerate(AB_GROUPS):
        abwin[:, n, :, :, 0:wdt] = wi[:, :, c0:c0 + wdt].reshape(2, 8, 128, wdt).transpose(0, 2, 1, 3)
        if nm[0] == "r":
            r0 = c0 - 1568
            abmu[:, 0:wdt, 2 * n] = inp["rwkv_shift_mu"][:, 0, r0:r0 + wdt]
            abmu[:, 0:wdt, 2 * n + 1] = inp["rwkv_shift_mu"][:, 1, r0:r0 + wdt]
    sh["abwin"] = abwin
    sh["abmu"] = abmu
    wo = inp["ab_w_out"]
    sh["abwo_g"] = f(wo[:, 0:512, :].reshape(2, 4, 128, 8, 128).transpose(0, 3, 2, 1, 4))
    sh["abwo_r"] = f(wo[:, 512:1024, :].reshape(2, 8, 64, 8, 128).transpose(0, 3, 2, 1, 4))
    w2bd = np.zeros((2, 128, 1024), np.float32)
    w2bd[:, 0:64, 0:512] = inp["rwkv_w2"][:, 0]
    w2bd[:, 64:128, 512:1024] = inp["rwkv_w2"][:, 1]
    sh["w2bd"] = w2bd
    sh["w0row"] = f(inp["rwkv_w0"].reshape(2, 1, 1024))
    sh["a2s"] = f(inp["rwkv_a2"].reshape(2, 128, 512))
    sh["a0c"] = f(inp["rwkv_a0"].reshape(2, 2, 8, 64).transpose(0, 3, 1, 2).reshape(2, 64, 16))
    sh["g2a"] = f(inp["rwkv_g2"][:, 0:128])
    sh["g2b"] = f(inp["rwkv_g2"][:, 128:160])
    wdbd = np.zeros((2, 32, 512), np.float32)
    wdbd[:, 0:16, 0:256] = inp["gla_w_decay"][:, 0]
    wdbd[:, 16:32, 256:512] = inp["gla_w_decay"][:, 1]
    sh["wdbd"] = wdbd
    sh["gbrow"] = f(inp["gla_b_decay"].reshape(2, 1, 512))
    rcols = np.zeros((2, 64, 40), np.float32)
    rcols[:, :, 0:8] = inp["rwkv_k_k"].reshape(2, 8, 64).transpose(0, 2, 1)
    rcols[:, :, 8:16] = inp["rwkv_k_a"].reshape(2, 8, 64).transpose(0, 2, 1)
    rcols[:, :, 16:24] = inp["rwkv_r_k"].transpose(0, 2, 1)
    rcols[:, :, 24:32] = inp["rwkv_ln_w"].reshape(2, 8, 64).transpose(0, 2, 1)
    rcols[:, :, 32:40] = inp["rwkv_ln_b"].reshape(2, 8, 64).transpose(0, 2, 1)
    sh["rcols"] = rcols
    sh["gnw"] = f(inp["gla_norm_w"].reshape(2, 128, 1))
    ii = np.arange(128)
    same = (ii[:, None] // 64) == (ii[None, :] // 64)
    S_, T_ = ii[:, None], ii[None, :]
    sh["trimats"] = np.stack([(same & (S_ <= T_)), (same & (S_ < T_)), (same & (S_ > T_)), (same & (S_ >= T_))]).astype(np.float32)
    jj = np.arange(64)
    Rw, Cl = jj[:, None], jj[None, :]
    mk = np.stack([(Rw < Cl), (Rw <= Cl), (Rw > Cl), (Rw >= Cl)]).astype(np.float32)
    sh["cmasks"] = np.ascontiguousarray(np.tile(mk, (1, 1, 8))).astype(NPBF)
    sh["identrep"] = np.ascontiguousarray(np.tile(np.eye(64, dtype=np.float32), (1, 8))).astype(NPBF)
    lv = np.stack([((Rw // (2 << k)) == (Cl // (2 << k))) & ((Rw // (1 << k)) != (Cl // (1 << k))) for k in range(6)], 1)
    sh["lvlmask"] = np.ascontiguousarray(lv.astype(np.float32)).astype(NPBF)
    sh["hwin"] = f(inp["hy_w_in"].reshape(2, 8, 128, 24, 128).transpose(0, 3, 2, 1, 4))
    sh["hcw"] = f(inp["hy_conv_w"].reshape(2, 3, 24, 128).transpose(0, 3, 2, 1).reshape(2, 128, 72))
    sh["hcb"] = f(inp["hy_conv_b"].reshape(2, 24, 128).transpose(0, 2, 1))
    sh["hwout"] = f(inp["hy_w_out"].reshape(2, 8, 128, 8, 128).transpose(0, 3, 2, 1, 4))
    sh["hfw1"] = f(inp["hy_f_w1"])
    sh["hfw2"] = f(inp["hy_f_w2"])
    sh["hfw3"] = f(inp["hy_f_w3"])
    sh["hfw4"] = f(inp["hy_f_w4"])
    sh["hffq"] = f(inp["hy_f_freq"].transpose(0, 2, 1))
    sh["hffb"] = f(np.stack([inp["hy_f_b1"], inp["hy_f_b2"], inp["hy_f_b3"]], -1))
    sh["hbias"] = f(np.broadcast_to(inp["hy_bias"][:, None, :], (2, 128, D)))
    return sh


def _hy_consts(kind):
    key = "hyc" + kind
    if key in _CACHE:
        return _CACHE[key]
    L = T if kind == "s" else SEG
    nseg = T // L
    N = 2 * L
    tl = np.linspace(0.0, 1.0, L, dtype=np.float32)
    w = (2.0 * np.pi * np.arange(L, dtype=np.float32) / np.float32(L)).astype(np.float32)
    f = np.linspace(1e-4, 15.0, 16, dtype=np.float32)[None, :]
    z = np.concatenate([tl[:, None], np.cos(f * w[:, None]), -np.sin(f * w[:, None])], -1).astype(np.float32)
    z = np.tile(z, (nseg, 1))
    tfull = np.tile(tl, nseg)
    first = (np.arange(T) % L == 0)
    c = {}
    c["ztab"] = np.ascontiguousarray(z.T)
    c["tcol"] = np.ascontiguousarray(tfull.reshape(16, 128).T)
    c["m0col"] = np.ascontiguousarray((~first).astype(np.float32).reshape(16, 128).T)
    c["d0col"] = np.ascontiguousarray(first.astype(np.float32).reshape(16, 128).T)
    mx = math.log(1e-2) / 0.3
    mn = math.log(1e-2) / 1.5
    deltas = np.abs(np.linspace(mn, mx, D, dtype=np.float32))
    c["ndelta"] = np.ascontiguousarray(np.tile(-deltas[None, :], (128, 1)).astype(np.float32))
    tt = np.arange(L, dtype=np.int64)[:, None]
    ff = np.arange(L, dtype=np.int64)[None, :]
    ang = (((2 * ff + 1) * tt) % (2 * N)).astype(np.float64) * (2.0 * np.pi / (2 * N))
    Fc = np.zeros((T, T), np.float32)
    Fsn = np.zeros((T, T), np.float32)
    for sgi in range(nseg):
        sl = slice(sgi * L, (sgi + 1) * L)
        Fc[sl, sl] = np.cos(ang)
        Fsn[sl, sl] = -np.sin(ang)
    F2 = np.stack([Fc, Fsn], 0)
    Fh = F2.reshape(2, 16, 128, 16, 128).transpose(3, 2, 0, 1, 4)
    c["Fh"] = np.ascontiguousarray(Fh).astype(NPBF)
    G = np.concatenate([Fc.T, Fsn.T], 0) * np.float32(2.0 / N)
    c["Gh"] = np.ascontiguousarray(G.reshape(32, 128, T)).astype(NPBF)
    _CACHE[key] = c
    return c


def core_assignment():
    return [("s", 0), ("s", 1), ("s", 2), ("s", 3), ("p", 0), ("p", 1), ("p", 0), ("p", 1)]


def prep_core(inp, kind, idx, cfg):
    f = lambda a: np.ascontiguousarray(np.asarray(a, dtype=np.float32))
    m = {}
    if kind == "s":
        m["xT"] = f(inp["x_sample"][idx].T)
        m["posT"] = _CACHE.setdefault("pos", _grid_pos_T())
        m["condc"] = _col(f(inp["c"][idx]))
        m["gstate"] = f(inp["state_gla"][idx].transpose(0, 1, 3, 2, 4))
        m["rstate"] = f(inp["state_rwkv"][idx].transpose(0, 1, 4, 2, 3))
        pm = 0.0
    else:
        xs = inp["x_prompt"][idx * 8:(idx + 1) * 8].reshape(T, D)
        m["xT"] = f(xs.T)
        m["posT"] = _CACHE.setdefault("zpos", np.zeros((D, T), np.float32))
        m["condc"] = _col(f(inp["c_ctx"]))
        m["gstate"] = _CACHE.setdefault("zg", np.zeros((2, 2, 64, 4, 128), np.float32))
        m["rstate"] = _CACHE.setdefault("zr", np.zeros((2, 2, 64, 8, 64), np.float32))
        pm = 1.0
    m.update(_hy_consts(kind))
    fl = np.zeros((128, 4), np.float32)
    fl[:, 0] = pm
    fl[:, 1] = -pm
    fl[:, 2] = 1.0 - pm
    m["flags"] = fl
    return m


def run(inputs, cfg):
    key = repr(sorted(cfg.items()))
    if key not in _CACHE:
        _CACHE[key] = build(cfg)
    B = _CACHE[key]
    sh = prep_shared(inputs, cfg)
    in_maps = []
    for kind, idx in core_assignment():
        m = dict(sh)
        m.update(prep_core(inputs, kind, idx, cfg))
        in_maps.append({k: v for k, v in m.items() if k in B.din})
    res = run_bass_kernel_spmd(B.nc, in_maps, core_ids=list(range(8)))
    return res.results


def assemble(results, inputs):
    y_sample = np.stack([results[b]["yT"].T for b in range(4)], 0).astype(np.float32)
    yp = [results[4 + c]["yT"].T.reshape(8, SEG, D) for c in range(2)]
    y_prompt = np.concatenate(yp, 0).astype(np.float32)
    if "gst_out" not in results[4]:
        return y_prompt, y_sample
    gs = np.concatenate([results[4 + c]["gst_out"].transpose(2, 0, 1, 4, 3, 5) for c in range(2)], 0).astype(np.float32)
    rs = np.concatenate([results[4 + c]["rst_out"].transpose(2, 0, 1, 4, 5, 3) for c in range(2)], 0).astype(np.float32)
    return y_prompt, y_sample, np.ascontiguousarray(gs), np.ascontiguousarray(rs)


def kernel(**inputs):
    inputs = {k: np.asarray(v) for k, v in inputs.items()}
    cfg = dict(mixers=True)
    results = run(inputs, cfg)
    return assemble(results, inputs)


PI = float(np.pi)


def make_ident(B):
    idf = B.sb("idf", [128, 128], F32)
    idb = B.sb("idb", [128, 128], BF16)
    B.memset("pool", idf[:], 0.0, writes=("idf",))
    B.P.op("pool", lambda e: e.affine_select(out=idf[:], in_=idf[:], pattern=[[-1, 128]], compare_op=ALU.not_equal,
                                             fill=1.0, base=0, channel_multiplier=1), reads=("idf",), writes=("idf",))
    B.cp("pool", idb[:], idf[:], reads=("idf",), writes=("idb",))
    B.idf, B.idb = idf, idb


def hyena_filters(B):
    nc = B.nc
    ztab = B.inp("ztab", [33, T])
    tcol = B.inp("tcol", [128, 16])
    m0col = B.inp("m0col", [128, 16])
    d0col = B.inp("d0col", [128, 16])
    ndelta = B.inp("ndelta", [128, D])
    hfw1 = B.inp("hfw1", [2, 33, 64])
    hfw2 = B.inp("hfw2", [2, 64, 64])
    hfw3 = B.inp("hfw3", [2, 64, 64])
    hfw4 = B.inp("hfw4", [2, 64, 2 * D])
    hffq = B.inp("hffq", [2, 64, 3])
    hffb = B.inp("hffb", [2, 64, 3])
    hbias = B.inp("hbias", [2, 128, D])
    Fh = B.inp("Fh", [16, 128, 2, 16, 128], BF16)
    B.Fh = Fh
    if B.cfg.get("debug"):
        HF = B.outp("HF", [2, 16, 128, 2, D], BF16)
    else:
        HF = B.scratch("HF", [2, 16, 128, 2, D], BF16)
    B.HF = HF
    with ExitStack() as ph:
        B.es, old = ph, B.es
        zt = B.sb("zt", [33, T], F32)
        tc = B.sb("tc", [128, 16], F32)
        m0 = B.sb("m0", [128, 16], F32)
        d0 = B.sb("d0", [128, 16], F32)
        ndl = B.sb("ndl", [128, D], F32)
        hA = B.sb("hA", [64, T], F32)
        hBf = B.sb("hBf", [64, T], F32)
        w1 = B.sb("w1", [33, 64], F32)
        w2 = B.sb("w2", [64, 64], F32)
        w3 = B.sb("w3", [64, 64], F32)
        w4 = B.sb("w4", [64, 2 * D], F32)
        fq = B.sb("fq", [64, 3], F32)
        fb = B.sb("fb", [64, 3], F32)
        hbb = B.sb("hbb", [128, D], F32)
        dec = B.sb("dec", [128, D], F32)
        hf = B.sb("hf", [128, D], F32)
        hb = B.sb("hb", [128, D], F32)
        arg = [B.sb("arg0", [64, 512], F32)] * 2
        wr = [B.sb("wr0", [64, 512], F32)] * 2
        HS = B.HB[:].rearrange("p k (h c) -> p (k h) c", c=D)
        HD = B.sb("HD", [128, 16, D], BF16)
        Fs = [B.sb("Fs%d" % i, [128, 2, 16, 128], BF16) for i in range(2)]
        hst = [B.sb("hst0", [128, 2, D], BF16)] * 2
        psm = [B.ps("psm%d" % i, [128, 512], F32) for i in range(2)]
        ps4 = [B.ps("ps4%d" % i, [128, 512], F32) for i in range(4)]
        B.dma("sp", zt[:], ztab, writes=("zt",))
        B.dma("sp", tc[:], tcol, writes=("tc",))
        B.dma("sp", m0[:], m0col, writes=("m0",))
        B.dma("sp", d0[:], d0col, writes=("d0",))
        B.dma("sp", ndl[:], ndelta, writes=("ndl",))
        fcnt = 0
        for i in range(2):
            B.dma("sp", w1[:], hfw1[i], writes=("w1",))
            B.dma("sp", w2[:], hfw2[i], writes=("w2",))
            B.dma("sp", w3[:], hfw3[i], writes=("w3",))
            B.dma("sp", w4[:], hfw4[i], writes=("w4",))
            B.dma("sp", fq[:], hffq[i], writes=("fq",))
            B.dma("sp", fb[:], hffb[i], writes=("fb",))
            B.dma("sp", hbb[:], hbias[i], writes=("hbb",))
            B.tt("dve", fb[:], fb[:], fq[:], ALU.mult, reads=("fb", "fq"), writes=("fb",))
            src = zt
            stok = "zt"
            wts = [(w1, "w1", 33), (w2, "w2", 64), (w3, "w3", 64)]
            dsts = [(hA, "hA"), (hBf, "hBf"), (hA, "hA")]
            mcnt = 0
            for li in range(3):
                wt, wtok, kk = wts[li]
                dst, dtok = dsts[li]
                for tt in range(NT):
                    tsl = slice(tt * 512, (tt + 1) * 512)
                    q = mcnt % 2
                    mcnt += 1
                    B.mm(psm[q][0:64, :], wt[0:kk, :], src[0:kk, tsl], True, True, reads=(wtok, stok + ":%d" % tt if stok != "zt" else "zt"),
                         writes=("psm%d" % q,))
                    a, w_ = arg[0], wr[0]
                    B.ts("dve", a[:], psm[q][0:64, :], fq[:, li:li + 1], fb[:, li:li + 1], ALU.mult, ALU.add,
                         reads=("psm%d" % q, "fq", "fb"), writes=("arg0",))
                    q = 0
                    for rep in range(2):
                        B.ts("dve", w_[:], a[:], PI, -2.0 * PI, ALU.is_gt, ALU.mult, reads=("arg%d" % q,), writes=("wr%d" % q,))
                        B.tt("dve", a[:], a[:], w_[:], ALU.add, reads=("arg%d" % q, "wr%d" % q), writes=("arg%d" % q,))
                        B.ts("dve", w_[:], a[:], -PI, 2.0 * PI, ALU.is_lt, ALU.mult, reads=("arg%d" % q,), writes=("wr%d" % q,))
                        B.tt("dve", a[:], a[:], w_[:], ALU.add, reads=("arg%d" % q, "wr%d" % q), writes=("arg%d" % q,))
                    B.act(dst[:, tsl], a[:], AF.Sin, reads=("arg%d" % q,), writes=(dtok + ":%d" % tt,))
                src, stok = dst, dtok
            for t16 in range(16):
                tt = t16 // 4
                B.act(dec[:], ndl[:], AF.Exp, scale=tc[:, t16:t16 + 1], reads=("ndl", "tc"), writes=("dec",))
                for cg in range(4):
                    B.mm(ps4[cg][:], hA[:, t16 * 128:(t16 + 1) * 128], w4[:, cg * 512:(cg + 1) * 512], True, True,
                         reads=("hA:%d" % tt, "w4"), writes=("ps4%d" % cg,))
                for cg in range(2):
                    csl = slice(cg * 512, (cg + 1) * 512)
                    B.tt("dve", hf[:, csl], ps4[cg][:], dec[:, csl], ALU.mult, reads=("ps4%d" % cg, "dec"), writes=("hf%d" % cg,))
                    B.stt("dve", hb[:, csl], ps4[2 + cg][:], m0[:, t16:t16 + 1], dec[:, csl], ALU.mult, ALU.mult,
                          reads=("ps4%d" % (2 + cg), "dec", "m0"), writes=("hb%d" % cg,))
                B.stt("dve", hf[:], hbb[:], d0[:, t16:t16 + 1], hf[:], ALU.mult, ALU.add, reads=("hbb", "d0", "hf0", "hf1"), writes=("hf0", "hf1"))
                B.tt("dve", HS[:, t16, :], hf[:], hb[:], ALU.add, reads=("hf0", "hf1", "hb0", "hb1"), writes=("HS:%d" % t16,))
                B.tt("dve", HD[:, t16, :], hf[:], hb[:], ALU.subtract, reads=("hf0", "hf1", "hb0", "hb1"), writes=("HD:%d" % t16,))
            for g in range(16):
                s = fcnt % 2
                fcnt += 1
                B.dma("sp", Fs[s][:], Fh[g], writes=("Fs%d" % s,))
                for part, (src_, stk) in enumerate(((HS, "HS"), (HD, "HD"))):
                    for hh in range(2):
                        pp = ps4[part * 2 + hh]
                        for t16 in range(16):
                            B.mm(pp[:], Fs[s][:, part, t16, :], src_[:, t16, hh * 512:(hh + 1) * 512], t16 == 0, t16 == 15,
                                 reads=("Fs%d" % s, stk + ":%d" % t16), writes=("ps4%d" % (part * 2 + hh),))
                        B.cp("act", hst[0][:, part, hh * 512:(hh + 1) * 512], pp[:], reads=("ps4%d" % (part * 2 + hh),), writes=("hst0",))
                B.dma("sp", HF[i, g], hst[0][:], reads=("hst0",), writes=("HF%d:%d" % (i, g),))
        B.fence()
    B.es = old


def hyena_mixer(B, l):
    i = l // 2
    X, HB, modc, flg = B.X, B.HB, B.modc, B.flg
    xtok, htok = B.xtok, B.htok
    if not hasattr(B, "hy_in"):
        B.hy_in = dict(
            hwin=B.inp("hwin", [2, 24, 128, 8, 128]),
            hcw=B.inp("hcw", [2, 128, 72]),
            hcb=B.inp("hcb", [2, 128, 24]),
            hwout=B.inp("hwout", [2, 8, 128, 8, 128]),
            Gh=B.inp("Gh", [32, 128, T], BF16),
            X0S=B.scratch("X0S", [8, 128, T], BF16),
        )
    hwin, hcw, hcb, hwout, Gh, X0S = (B.hy_in[k] for k in ("hwin", "hcw", "hcb", "hwout", "Gh", "X0S"))
    Fh, HF = B.Fh, B.HF
    Z = HB
    with ExitStack() as mix:
        B.es, old_mix = mix, B.es
        VX = B.sb("VX", [128, 16, D], BF16)
        with ExitStack() as ph:
            B.es = ph
            cw = B.sb("cw", [128, 72], F32)
            cb = B.sb("cb", [128, 24], F32)
            ncw = B.sb("ncw", [128, 72], F32)
            U = [B.sb("U%d" % k, [128, T + 2], F32) for k in range(3)]
            acc = [B.sb("acc%d" % k, [128, T], F32) for k in range(3)]
            vxb = [B.sb("vxb%d" % k, [128, T], BF16) for k in range(2)]
            x0b = [B.sb("x0b%d" % k, [128, T], BF16) for k in range(2)]
            wps = [B.sb("wps%d" % k, [128, 8, 128], BF16) for k in range(3)]
            psI = [B.ps("psI%d" % k, [128, 512], F32) for k in range(4)]
            psT = [B.ps("psT%d" % k, [128, 4, 128], BF16) for k in range(2)]
            B.dma("sp", cw[:], hcw[i], writes=("cw",))
            B.dma("sp", cb[:], hcb[i], writes=("cb",))
            B.ts("dve", ncw[:], cw[:], flg[:, 1:2], None, ALU.mult, reads=("cw", "flg"), writes=("ncw",))
            for k in range(3):
                B.memset("dve", U[k][:, 0:1], 0.0, writes=("U%d" % k,))
                B.memset("dve", U[k][:, T + 1:T + 2], 0.0, writes=("U%d" % k,))
            pc = 0
            tc_ = 0
            for j in range(8):
                for kind, c in ((0, 8 + j), (1, 16 + j), (2, j)):
                    Uk, ak = U[kind], acc[kind]
                    ut, at = "U%d" % kind, "acc%d" % kind
                    B.dma("pool", wps[kind][:], hwin[i, c], writes=("wps%d" % kind,))
                    for tt in range(NT):
                        tsl = slice(tt * 512, (tt + 1) * 512)
                        q = pc % 4
                        pc += 1
                        for k in range(8):
                            B.mm(psI[q][:], wps[kind][:, k, :], HB[:, k, tsl], k == 0, k == 7,
                                 reads=("wps%d" % kind, htok(k, tt)), writes=("psI%d" % q,))
                        B.cp("act", Uk[:, 1 + tt * 512:1 + (tt + 1) * 512], psI[q][:], reads=("psI%d" % q,), writes=(ut,))
                    B.act(ak[:], Uk[:, 1:T + 1], AF.Identity, bias=cb[:, c:c + 1], scale=cw[:, c * 3 + 1:c * 3 + 2],
                          reads=(ut, "cw", "cb"), writes=(at,))
                    B.stt("dve", ak[:], Uk[:, 0:T], cw[:, c * 3:c * 3 + 1], ak[:], ALU.mult, ALU.add, reads=(ut, at, "cw"), writes=(at,))
                    B.stt("dve", ak[:], Uk[:, 2:T + 2], cw[:, c * 3 + 2:c * 3 + 3], ak[:], ALU.mult, ALU.add, reads=(ut, at, "cw"), writes=(at,))
                    B.stt("dve", ak[:, SEG:T:SEG], Uk[:, SEG:T:SEG], ncw[:, c * 3:c * 3 + 1], ak[:, SEG:T:SEG], ALU.mult, ALU.add,
                          reads=(ut, at, "ncw"), writes=(at,))
                    B.stt("dve", ak[:, SEG - 1:T - 1:SEG], Uk[:, SEG + 1:T + 1:SEG], ncw[:, c * 3 + 2:c * 3 + 3], ak[:, SEG - 1:T - 1:SEG],
                          ALU.mult, ALU.add, reads=(ut, at, "ncw"), writes=(at,))
                    if kind == 1:
                        vb = vxb[j % 2]
                        vt = "vxb%d" % (j % 2)
                        B.tt("dve", vb[:], acc[1][:], acc[0][:], ALU.mult, reads=("acc0", "acc1"), writes=(vt,))
                        for t4 in range(4):
                            pq = tc_ % 2
                            tc_ += 1
                            for qq in range(4):
                                t16 = t4 * 4 + qq
                                B.tr(psT[pq][:, qq, :], vb[:, t16 * 128:(t16 + 1) * 128], B.idb[:], reads=(vt, "idb"), writes=("psT%d" % pq,))
                            B.cp("act", VX[:, t4 * 4:(t4 + 1) * 4, j * 128:(j + 1) * 128], psT[pq][:], reads=("psT%d" % pq,),
                                 writes=["VX:%d:%d" % (t4 * 4 + qq, j // 4) for qq in range(4)])
                    if kind == 2:
                        xb_ = x0b[j % 2]
                        xt_ = "x0b%d" % (j % 2)
                        B.cp("act", xb_[:], acc[2][:], reads=("acc2",), writes=(xt_,))
                        B.dma("sp", X0S[j], xb_[:], reads=(xt_,), writes=("X0S%d" % j,))
            B.fence()
        for half in range(2):
            hsl = slice(half * 512, (half + 1) * 512)
            with ExitStack() as ph2:
                B.es = ph2
                YS = B.sb("YS", [128, 32, 512], BF16)
                with ExitStack() as ph:
                    B.es = ph
                    Fs = [B.sb("Fsd%d" % k, [128, 2, 16, 128], BF16) for k in range(2)]
                    Hs = [B.sb("Hs%d" % k, [128, 2, 512], BF16) for k in range(2)]
                    tp = [B.sb("tp%d" % k, [128, 512], F32) for k in range(4)]
                    psV = [B.ps("psV%d" % k, [128, 512], F32) for k in range(4)]
                    for g in range(16):
                        s = g % 2
                        B.dma("sp", Fs[s][:], Fh[g], writes=("Fs%d" % s,))
                        B.dma("sp", Hs[s][:], HF[i, g][:, :, hsl], reads=("HF%d:%d" % (i, g),), writes=("Hs%d" % s,))
                        for part in range(2):
                            pp = psV[s * 2 + part]
                            for t16 in range(16):
                                B.mm(pp[:], Fs[s][:, part, t16, :], VX[:, t16, hsl], t16 == 0, t16 == 15,
                                     reads=("Fs%d" % s, "VX:%d:%d" % (t16, half)), writes=("psV%d" % (s * 2 + part),))
                        vre, vim = psV[s * 2], psV[s * 2 + 1]
                        rt, it = "psV%d" % (s * 2), "psV%d" % (s * 2 + 1)
                        B.tt("dve", tp[0][:], vre[:], Hs[s][:, 0, :], ALU.mult, reads=(rt, "Hs%d" % s), writes=("tp0",))
                        B.tt("dve", tp[1][:], vim[:], Hs[s][:, 1, :], ALU.mult, reads=(it, "Hs%d" % s), writes=("tp1",))
                        B.tt("dve", tp[2][:], vre[:], Hs[s][:, 1, :], ALU.mult, reads=(rt, "Hs%d" % s), writes=("tp2",))
                        B.tt("dve", tp[3][:], vim[:], Hs[s][:, 0, :], ALU.mult, reads=(it, "Hs%d" % s), writes=("tp3",))
                        B.tt("pool", YS[:, g, :], tp[0][:], tp[1][:], ALU.subtract, reads=("tp0", "tp1"), writes=("YS:%d" % g,))
                        B.tt("pool", YS[:, 16 + g, :], tp[2][:], tp[3][:], ALU.add, reads=("tp2", "tp3"), writes=("YS:%d" % (16 + g),))
                    B.fence()
                with ExitStack() as ph:
                    B.es = ph
                    Gs = [B.sb("Gs%d" % k, [128, 1024], BF16) for k in range(3)]
                    x0s = [B.sb("x0s%d" % k, [128, T], BF16) for k in range(4)]
                    psY = [B.ps("psY%d" % k, [128, 512], F32) for k in range(8)]
                    for cc in range(4):
                        B.dma("sp", x0s[cc][:], X0S[half * 4 + cc], reads=("X0S%d" % (half * 4 + cc),), writes=("x0s%d" % cc,))
                    gc = 0
                    for tp2 in range(2):
                        for fc in range(32):
                            s = gc % 3
                            gc += 1
                            B.dma("sp", Gs[s][:], Gh[fc][:, tp2 * 1024:(tp2 + 1) * 1024], writes=("Gs%d" % s,))
                            for t2 in range(2):
                                for cc in range(4):
                                    B.mm(psY[t2 * 4 + cc][:], YS[:, fc, cc * 128:(cc + 1) * 128], Gs[s][:, t2 * 512:(t2 + 1) * 512],
                                         fc == 0, fc == 31, reads=("YS:%d" % fc, "Gs%d" % s), writes=("psY%d" % (t2 * 4 + cc),))
                        for t2 in range(2):
                            tt = tp2 * 2 + t2
                            tsl = slice(tt * 512, (tt + 1) * 512)
                            for cc in range(4):
                                B.tt("dve", Z[:, half * 4 + cc, tsl], psY[t2 * 4 + cc][:], x0s[cc][:, tsl], ALU.mult,
                                     reads=("psY%d" % (t2 * 4 + cc), "x0s%d" % cc), writes=(htok(half * 4 + cc, tt),))
                    B.fence()
        if B.cfg.get("debug") and l == 1:
            dz = B.outp("dbgZ", [8, 128, T], BF16)
            for cc in range(8):
                B.dma("sp", dz[cc], Z[:, cc, :], reads=[htok(cc, tt) for tt in range(NT)], is_out=True)
            B.fence()
        with ExitStack() as ph:
            B.es = ph
            lnps = B.alloc_ln()
            wos = [B.sb("wos%d" % k, [128, 8, 128], BF16) for k in range(8)]
            otm = [B.sb("otm%d" % k, [128, 512], F32) for k in range(2)]
            psO = [B.ps("psO%d" % k, [128, 512], F32) for k in range(4)]
            for dch in range(8):
                B.dma("pool", wos[dch][:], hwout[i, dch], writes=("wos%d" % dch,))
            oc = [0]

            def OP(tt):
                tsl = slice(tt * 512, (tt + 1) * 512)
                for dch in range(8):
                    q = oc[0] % 4
                    oc[0] += 1
                    for cc in range(8):
                        B.mm(psO[q][:], wos[dch][:, cc, :], Z[:, cc, tsl], cc == 0, cc == 7, reads=("wos%d" % dch, htok(cc, tt)), writes=("psO%d" % q,))
                    c = l * 48 + 2 * 8 + dch
                    B.act(otm[q % 2][:], psO[q][:], AF.Identity, scale=modc[:, c:c + 1], reads=("psO%d" % q, "modc"), writes=("otm%d" % (q % 2),))
                    B.stt("dve", X[:, dch, tsl], X[:, dch, tsl], ALPHA, otm[q % 2][:], ALU.mult, ALU.add,
                          reads=(xtok(dch, tt), "otm%d" % (q % 2)), writes=(xtok(dch, tt),))
                    yield

            def LNt(tt):
                yield from B.ln_tile_gen(lnps, l, 0, tt, (l, 3))
            drain(OP(0))
            for tt in range(1, NT):
                drain(OP(tt), LNt(tt - 1))
            drain(LNt(NT - 1))
            B.fence()
    B.es = old_mix


CH = 64
NG = 45
EXPC = float(math.exp(-0.5))
GN_EPS = 64e-5
def _ab_groups():
    g = []
    for h in range(4):
        g.append(("gq%d" % h, 64 * h, 64))
    for h in range(4):
        g.append(("gk%d" % h, 256 + 64 * h, 64))
    for h in range(4):
        g.append(("gv%d" % h, 512 + 128 * h, 128))
    for h in range(4):
        g.append(("gg%d" % h, 1024 + 128 * h, 128))
    g.append(("glr", 1536, 32))
    r0 = 1568
    g.append(("rwl", r0 + 1536, 128))
    g.append(("ral", r0 + 1664, 128))
    g.append(("rgl0", r0 + 1792, 128))
    g.append(("rgl1", r0 + 1920, 32))
    for h in range(8):
        g.append(("rr%d" % h, r0 + 64 * h, 64))
        g.append(("rk%d" % h, r0 + 512 + 64 * h, 64))
        g.append(("rv%d" % h, r0 + 1024 + 64 * h, 64))
    assert len(g) == NG
    return g


AB_GROUPS = _ab_groups()


def ab_inputs(B):
    if hasattr(B, "ab_in"):
        return B.ab_in
    d = dict(
        abwin=B.inp("abwin", [2, NG, 128, 8, 128]),
        abmu=B.inp("abmu", [2, 128, NG * 2]),
        abwo_g=B.inp("abwo_g", [2, 8, 128, 4, 128]),
        abwo_r=B.inp("abwo_r", [2, 8, 64, 8, 128]),
        w2bd=B.inp("w2bd", [2, 128, 1024]),
        w0row=B.inp("w0row", [2, 1, 1024]),
        a2s=B.inp("a2s", [2, 128, 512]),
        a0c=B.inp("a0c", [2, 64, 16]),
        g2a=B.inp("g2a", [2, 128, 512]),
        g2b=B.inp("g2b", [2, 32, 512]),
        wdbd=B.inp("wdbd", [2, 32, 512]),
        gbrow=B.inp("gbrow", [2, 1, 512]),
        rcols=B.inp("rcols", [2, 64, 40]),
        gnw=B.inp("gnw", [2, 128, 1]),
        trimats=B.inp("trimats", [4, 128, 128]),
        cmasks=B.inp("cmasks", [4, 64, 512], BF16),
        identrep=B.inp("identrep", [64, 512], BF16),
        lvlmask=B.inp("lvlmask", [64, 6, 64], BF16),
        gstate=B.inp("gstate", [2, 2, 64, 4, 128]),
        rstate=B.inp("rstate", [2, 2, 64, 8, 64]),
        gst_out=B.outp("gst_out", [2, 2, NSEG, 64, 4, 128]),
        rst_out=B.outp("rst_out", [2, 2, NSEG, 64, 8, 64]),
        XS=B.scratch("XS", [8, 128, T], F32),
        GQ=B.scratch("GQ", [4, 64, T], BF16), GK=B.scratch("GK", [4, 64, T], BF16),
        GV=B.scratch("GV", [4, 128, T], BF16), GG=B.scratch("GG", [4, 128, T], BF16),
        LA=B.scratch("LA", [T, 512], F32), SG=B.scratch("SG", [T, 1024], F32),
        RR=B.scratch("RR", [8, 64, T], BF16), RK=B.scratch("RK", [8, 64, T], BF16), RV=B.scratch("RV", [8, 64, T], BF16),
        RKK=B.scratch("RKK", [8, 64, T], BF16), RBV=B.scratch("RBV", [8, 64, T], BF16), RG=B.scratch("RG", [8, 64, T], BF16),
        RA=B.scratch("RA", [2, 8, 64, T], BF16),
        YFR=B.scratch("YFR", [8, 64, T], F32), YFG=B.scratch("YFG", [4, 128, T], F32),
    )
    B.ab_in = d
    return d


def ab_phase1(B, l):
    i = l // 2
    A = ab_inputs(B)
    X, HB, flg = B.X, B.HB, B.flg
    xtok, htok = B.xtok, B.htok
    gidx = {g[0]: n for n, g in enumerate(AB_GROUPS)}
    with ExitStack() as ph:
        B.es, old = ph, B.es
        for k in range(8):
            B.dma("sp", A["XS"][k], X[:, k, :], reads=[xtok(k, tt) for tt in range(NT)], writes=("XS%d" % k,))
        B.fence()
        mu = B.sb("mu", [128, NG * 2], F32)
        c1 = B.sb("c1", [128, NG], F32)
        nmu = B.sb("nmu", [128, NG * 2], F32)
        rc = B.sb("rc", [64, 40], F32)
        a0 = B.sb("a0", [64, 16], F32)
        U = [B.sb("aU%d" % k, [128, T + 2], F32) for k in range(2)]
        acc = [X[:, k, :] for k in range(3)]
        ob = [X[:, 3 + k, 0:1024].bitcast(BF16) for k in range(3)]
        twl = X[:, 3, 1024:2048].bitcast(BF16)
        alb = X[:, 4, 1024:2048].bitcast(BF16)
        sg0 = X[:, 5, 1024:2048].bitcast(BF16)
        sg1 = B.sb("sg1", [32, T], BF16)
        lrb = B.sb("lrb", [32, T], BF16)
        wps = [B.sb("awps%d" % k, [128, 8, 128], BF16) for k in range(3)]
        w2 = B.sb("w2", [128, 1024], BF16)
        w0r = B.sb("w0r", [1, 1024], F32)
        a2 = B.sb("a2", [128, 512], BF16)
        g2a = B.sb("g2a", [128, 512], BF16)
        g2b = B.sb("g2b", [32, 512], BF16)
        wdb = B.sb("wdb", [32, 512], BF16)
        gbr = B.sb("gbr", [1, 512], F32)
        onesf = B.sb("onesf", [1, 128], F32)
        ones1 = B.sb("ones1", [128, 128], BF16)
        tmpa = [B.sb("tmpa%d" % k, [128, 512], F32) for k in range(2)]
        tmpb = [B.sb("tmpb%d" % k, [128, 512], BF16) for k in range(2)]
        tmpc = [B.sb("tmpc%d" % k, [128, 512], F32) for k in range(2)]
        tokb = [B.sb("tokb%d" % k, [128, 1024], F32) for k in range(2)]
        psI = [B.ps("apsI%d" % k, [128, 512], F32) for k in range(4)]
        psS = [B.ps("apsS%d" % k, [128, 512], F32) for k in range(4)]
        B.dma("sp", mu[:], A["abmu"][i], writes=("mu",))
        B.dma("sp", rc[:], A["rcols"][i], writes=("rc",))
        B.dma("sp", a0[:], A["a0c"][i], writes=("a0",))
        B.dma("pool", w2[:], A["w2bd"][i], writes=("w2",))
        B.dma("sp", w0r[:], A["w0row"][i], writes=("w0r",))
        B.dma("pool", a2[:], A["a2s"][i], writes=("a2",))
        B.dma("pool", g2a[:], A["g2a"][i], writes=("g2a",))
        B.dma("pool", g2b[:], A["g2b"][i], writes=("g2b",))
        B.dma("pool", wdb[:], A["wdbd"][i], writes=("wdb",))
        B.dma("sp", gbr[:], A["gbrow"][i], writes=("gbr",))
        B.memset("dve", onesf[:], 1.0, writes=("onesf",))
        B.memset("dve", ones1[:], 1.0, writes=("ones1",))
        muv = mu[:].rearrange("p (g two) -> p g two", two=2)
        B.tt("dve", c1[:], muv[:, :, 0], muv[:, :, 1], ALU.add, reads=("mu",), writes=("c1",))
        B.ts("dve", c1[:], c1[:], -1.0, 1.0, ALU.mult, ALU.add, reads=("c1",), writes=("c1",))
        B.ts("dve", nmu[:], mu[:], flg[:, 1:2], None, ALU.mult, reads=("mu", "flg"), writes=("nmu",))
        for k in range(2):
            B.memset("dve", U[k][:, 0:1], 0.0, writes=("aU%d" % k,))
            B.memset("dve", U[k][:, T + 1:T + 2], 0.0, writes=("aU%d" % k,))
        cnt = dict(pc=0, u=0, w=0, o=0, s=0, t=0)

        def inproj(gname, shift):
            n = gidx[gname]
            M = AB_GROUPS[n][2]
            s = cnt["w"] % 3
            cnt["w"] += 1
            B.dma("pool", wps[s][:], A["abwin"][i, n], writes=("awps%d" % s,))
            if shift:
                ui = cnt["u"] % 2
                cnt["u"] += 1
                Uk, ut = U[ui], "aU%d" % ui
            ai = cnt["o"] % 3
            cnt["o"] += 1
            ak, at = acc[ai], "aacc%d" % ai
            for tt in range(NT):
                tsl = slice(tt * 512, (tt + 1) * 512)
                q = cnt["pc"] % 4
                cnt["pc"] += 1
                for k in range(8):
                    B.mm(psI[q][0:M, :], wps[s][:, k, 0:M], HB[:, k, tsl], k == 0, k == 7,
                         reads=("awps%d" % s, htok(k, tt)), writes=("apsI%d" % q,))
                if shift:
                    B.cp("act", Uk[0:M, 1 + tt * 512:1 + (tt + 1) * 512], psI[q][0:M, :], reads=("apsI%d" % q,), writes=(ut,))
                else:
                    B.cp("act", ak[0:M, tsl], psI[q][0:M, :], reads=("apsI%d" % q,), writes=(at,))
            if shift:
                B.act(ak[0:M, :], Uk[0:M, 1:T + 1], AF.Identity, scale=c1[0:M, n:n + 1], reads=(ut, "c1"), writes=(at,))
                B.stt("dve", ak[0:M, :], Uk[0:M, 0:T], mu[0:M, 2 * n:2 * n + 1], ak[0:M, :], ALU.mult, ALU.add, reads=(ut, at, "mu"), writes=(at,))
                B.stt("dve", ak[0:M, :], Uk[0:M, 2:T + 2], mu[0:M, 2 * n + 1:2 * n + 2], ak[0:M, :], ALU.mult, ALU.add, reads=(ut, at, "mu"), writes=(at,))
                B.stt("dve", ak[0:M, SEG:T:SEG], Uk[0:M, SEG:T:SEG], nmu[0:M, 2 * n:2 * n + 1], ak[0:M, SEG:T:SEG], ALU.mult, ALU.add,
                      reads=(ut, at, "nmu"), writes=(at,))
                B.stt("dve", ak[0:M, SEG - 1:T - 1:SEG], Uk[0:M, SEG + 1:T + 1:SEG], nmu[0:M, 2 * n + 1:2 * n + 2], ak[0:M, SEG - 1:T - 1:SEG],
                      ALU.mult, ALU.add, reads=(ut, at, "nmu"), writes=(at,))
            return ak, at, M

        def outbuf():
            oi = cnt["s"] % 3
            cnt["s"] += 1
            return ob[oi], "aob%d" % oi

        for h in range(4):
            for nm, dst in (("gq", "GQ"), ("gk", "GK"), ("gv", "GV")):
                ak, at, M = inproj("%s%d" % (nm, h), False)
                o, ot = outbuf()
                B.cp("dve", o[0:M, :], ak[0:M, :], reads=(at,), writes=(ot,))
                B.dma("sp", A[dst][h], o[0:M, :], reads=(ot,), writes=("%s%d" % (dst, h),))
            ak, at, M = inproj("gg%d" % h, False)
            o, ot = outbuf()
            B.act(o[:], ak[:], AF.Silu, reads=(at,), writes=(ot,))
            B.dma("sp", A["GG"][h], o[:], reads=(ot,), writes=("GG%d" % h,))
        ak, at, M = inproj("glr", False)
        B.cp("dve", lrb[:], ak[0:32, :], reads=(at,), writes=("lrb",))
        for t16 in range(16):
            q = cnt["t"] % 2
            cnt["t"] += 1
            B.mm(psS[q][:], lrb[:, t16 * 128:(t16 + 1) * 128], wdb[:], True, False, reads=("lrb", "wdb"), writes=("apsS%d" % q,))
            B.mm(psS[q][:], onesf[:], gbr[:], False, True, reads=("onesf", "gbr"), writes=("apsS%d" % q,))
            tb = tokb[q]
            B.act(tb[:, 0:512], psS[q][:], AF.Sigmoid, reads=("apsS%d" % q,), writes=("tokb%d" % q,))
            B.act(tb[:, 0:512], tb[:, 0:512], AF.Ln, reads=("tokb%d" % q,), writes=("tokb%d" % q,))
            B.ts("dve", tb[:, 0:512], tb[:, 0:512], 1.0 / 16.0, None, ALU.mult, reads=("tokb%d" % q,), writes=("tokb%d" % q,))
            B.dma("sp", A["LA"][t16 * 128:(t16 + 1) * 128, :], tb[:, 0:512], reads=("tokb%d" % q,), writes=("LA%d" % t16,))
        ak, at, M = inproj("rwl", True)
        B.act(twl[:], ak[:], AF.Tanh, reads=(at,), writes=("twl",))
        for t16 in range(16):
            q = cnt["t"] % 2
            cnt["t"] += 1
            for hh in range(2):
                pp = psS[q * 2 + hh]
                B.mm(pp[:], twl[:, t16 * 128:(t16 + 1) * 128], w2[:, hh * 512:(hh + 1) * 512], True, False, reads=("twl", "w2"), writes=("apsS%d" % (q * 2 + hh),))
                B.mm(pp[:], onesf[:], w0r[:, hh * 512:(hh + 1) * 512], False, True, reads=("onesf", "w0r"), writes=("apsS%d" % (q * 2 + hh),))
                B.act(tokb[q][:, hh * 512:(hh + 1) * 512], pp[:], AF.Sigmoid, reads=("apsS%d" % (q * 2 + hh),), writes=("tokb%d" % q,))
            B.dma("sp", A["SG"][t16 * 128:(t16 + 1) * 128, :], tokb[q][:], reads=("tokb%d" % q,), writes=("SG%d" % t16,))
        ak, at, M = inproj("ral", True)
        B.cp("dve", alb[:], ak[:], reads=(at,), writes=("alb",))
        for d in range(2):
            for h in range(8):
                o, ot = outbuf()
                for tt in range(NT):
                    tsl = slice(tt * 512, (tt + 1) * 512)
                    q = cnt["pc"] % 4
                    cnt["pc"] += 1
                    B.mm(psI[q][0:64, :], a2[d * 64:(d + 1) * 64, h * 64:(h + 1) * 64], alb[d * 64:(d + 1) * 64, tsl], True, True,
                         reads=("a2", "alb"), writes=("apsI%d" % q,))
                    B.act(o[0:64, tsl], psI[q][0:64, :], AF.Sigmoid, bias=a0[:, d * 8 + h:d * 8 + h + 1], reads=("apsI%d" % q, "a0"), writes=(ot,))
                B.dma("sp", A["RA"][d, h], o[0:64, :], reads=(ot,), writes=("RA%d_%d" % (d, h),))
        ak, at, M = inproj("rgl0", True)
        B.act(sg0[:], ak[:], AF.Sigmoid, reads=(at,), writes=("sg0",))
        ak, at, M = inproj("rgl1", True)
        B.act(sg1[:], ak[0:32, :], AF.Sigmoid, reads=(at,), writes=("sg1",))
        for h in range(8):
            o, ot = outbuf()
            for tt in range(NT):
                tsl = slice(tt * 512, (tt + 1) * 512)
                q = cnt["pc"] % 4
                cnt["pc"] += 1
                B.mm(psI[q][0:64, :], g2a[:, h * 64:(h + 1) * 64], sg0[:, tsl], True, False, reads=("g2a", "sg0"), writes=("apsI%d" % q,))
                B.mm(psI[q][0:64, :], g2b[:, h * 64:(h + 1) * 64], sg1[:, tsl], False, True, reads=("g2b", "sg1"), writes=("apsI%d" % q,))
                B.cp("act", o[0:64, tsl], psI[q][0:64, :], reads=("apsI%d" % q,), writes=(ot,))
            B.dma("sp", A["RG"][h], o[0:64, :], reads=(ot,), writes=("RG%d" % h,))
        for h in range(8):
            ar, art, _ = inproj("rr%d" % h, True)
            akk, akt, _ = inproj("rk%d" % h, True)
            av, avt, _ = inproj("rv%d" % h, True)
            for src, st, dst in ((ar, art, "RR"), (akk, akt, "RK"), (av, avt, "RV")):
                o, ot = outbuf()
                B.cp("dve", o[0:64, :], src[0:64, :], reads=(st,), writes=(ot,))
                B.dma("sp", A[dst][h], o[0:64, :], reads=(ot,), writes=("%s%d" % (dst, h),))
            okk, okt = outbuf()
            obv, obt = outbuf()
            for tt in range(NT):
                tsl = slice(tt * 512, (tt + 1) * 512)
                ta, tat = tmpa[tt % 2], "tmpa%d" % (tt % 2)
                tb, tbt = tmpb[tt % 2], "tmpb%d" % (tt % 2)
                q = cnt["t"] % 4
                cnt["t"] += 1
                B.ts("dve", ta[0:64, :], akk[0:64, tsl], rc[:, h:h + 1], None, ALU.mult, reads=(akt, "rc"), writes=(tat,))
                B.act(tb[0:64, :], ta[0:64, :], AF.Square, reads=(tat,), writes=(tbt,))
                B.mm(psS[q][0:64, :], ones1[0:64, 0:64], tb[0:64, :], True, True, reads=("ones1", tbt), writes=("apsS%d" % q,))
                tcn, tct = tmpc[tt % 2], "tmpc%d" % (tt % 2)
                B.act(tcn[0:64, :], psS[q][0:64, :], AF.Ln, bias=B.eps_ln[0:64, 1:2], reads=("apsS%d" % q,), writes=(tct,))
                B.act(tcn[0:64, :], tcn[0:64, :], AF.Exp, scale=-0.5, reads=(tct,), writes=(tct,))
                B.tt("dve", okk[0:64, tsl], ta[0:64, :], tcn[0:64, :], ALU.mult, reads=(tat, tct), writes=(okt,))
                B.stt("dve", tb[0:64, :], ar[0:64, tsl], rc[:, 16 + h:17 + h], akk[0:64, tsl], ALU.mult, ALU.mult, reads=(art, akt, "rc"), writes=(tbt,))
                q2 = cnt["t"] % 4
                cnt["t"] += 1
                B.mm(psS[q2][0:64, :], ones1[0:64, 0:64], tb[0:64, :], True, True, reads=("ones1", tbt), writes=("apsS%d" % q2,))
                B.tt("dve", obv[0:64, tsl], psS[q2][0:64, :], av[0:64, tsl], ALU.mult, reads=("apsS%d" % q2, avt), writes=(obt,))
            B.dma("sp", A["RKK"][h], okk[0:64, :], reads=(okt,), writes=("RKK%d" % h,))
            B.dma("sp", A["RBV"][h], obv[0:64, :], reads=(obt,), writes=("RBV%d" % h,))
        B.fence()
    B.es = old


def ab_phase2(B, l):
    i = l // 2
    A = ab_inputs(B)
    HB, flg = B.HB, B.flg
    htok = B.htok
    with ExitStack() as ph:
        B.es, old = ph, B.es
        tri = B.sb("tri", [128, 4, 128], F32)
        msk2 = B.sb("msk", [128, 4, 64], BF16)
        idr2 = B.sb("idr", [128, 512], BF16)
        lvl2 = B.sb("lvl", [128, 6, 64], BF16)
        rc2 = B.sb("rc2", [128, 40], F32)
        omka2 = B.sb("omka", [128, 8], F32)
        msk, idr, lvl, rc, omka = msk2[0:64], idr2[0:64], lvl2[0:64], rc2[0:64], omka2[0:64]
        for hf_ in range(2):
            B.dma("sp", lvl2[hf_ * 64:(hf_ + 1) * 64], A["lvlmask"], writes=("lvl",))
            B.dma("sp", idr2[hf_ * 64:(hf_ + 1) * 64], A["identrep"], writes=("idr",))
            B.dma("sp", rc2[hf_ * 64:(hf_ + 1) * 64], A["rcols"][i], writes=("rc2",))
            for k in range(4):
                B.dma("sp", msk2[hf_ * 64:(hf_ + 1) * 64, k, :], A["cmasks"][k][:, 0:64], writes=("msk",))
        gnw = B.sb("gnw", [128, 1], F32)
        ones1 = B.sb("ones1b", [128, 128], BF16)
        for k in range(4):
            B.dma("sp", tri[:, k, :], A["trimats"][k], writes=("tri",))
        B.dma("sp", gnw[:], A["gnw"][i], writes=("gnw",))
        B.memset("dve", ones1[:], 1.0, writes=("ones1b",))
        B.ts("dve", omka2[:], rc2[:, 8:16], -1.0, 1.0, ALU.mult, ALU.add, reads=("rc2",), writes=("omka",))
        psD = [B.ps("psD%d" % k, [64, 4, 128], F32) for k in range(2)]
        psTr = [B.ps("psTr%d" % k, [64, 512], BF16) for k in range(2)]
        psA = [B.ps("psA%d" % k, [128, 512], F32) for k in range(4)]
        cnt = dict(a=0, t=0, d=0)

        def next_psA():
            q = cnt["a"] % 4
            cnt["a"] += 1
            return psA[q], "psA%d" % q

        for kind in ("g",):
            NH = 4 if kind == "g" else 8
            DV = 128 if kind == "g" else 64
            VP = DV
            W = NH * 64
            WV = NH * DV
            low = kind == "r"
            with ExitStack() as kp:
                B.es = kp
                names = ["q", "k", "v"] + (["kk", "a"] if low else [])
                ld = {}
                xslots = {"q": (6, 0), "k": (6, 1), "kk": (7, 0), "a": (7, 1)}
                for nm in names:
                    if nm == "v":
                        ld[nm] = [B.sb("ld%s%s" % (kind, nm), [VP, NH, SEG], BF16)] * 2
                    else:
                        xk, xh = xslots[nm]
                        xv_ = B.X[0:64, xk, xh * 1024:(xh + 1) * 1024].bitcast(BF16)
                        ld[nm] = [xv_[:, 0:NH * SEG].rearrange("p (h t) -> p h t", h=NH)] * 2
                dec = [B.sb("dec%s" % kind, [128, 2, W], F32)] * 2
                onm = ["RHO", "KT", "KH"] + (["KAP", "BT", "BH"] if low else [])
                opd = {}
                for oi_, nm in enumerate(onm):
                    opd[nm] = []
                    for z in range(2):
                        nbuf = oi_ * 2 + z
                        xv_ = B.X[0:64, nbuf // 2, (nbuf % 2) * 1024:(nbuf % 2 + 1) * 1024].bitcast(BF16)
                        opd[nm].append(xv_[:, 0:NH * SEG].rearrange("p (h t) -> p h t", h=NH))
                pC = [B.sb("pC%s%d" % (kind, z), [64, NH, 4], F32) for z in range(2)]
                pt = {nm: B.sb("pt%s%s" % (kind, nm), [64, NH, 128], F32) for nm in ("inc", "inv", "exc", "end", "b", "kd")}
                Hf = B.sb("Hf" + kind, [64, NH, DV], F32)
                Hb = B.sb("Hb" + kind, [64, NH, DV], BF16)
                if low:
                    YF = [B.HB[0:64, 4 + 2 * z:6 + 2 * z, :].rearrange("p k t -> p (k t)").bitcast(F32).rearrange("p (h t) -> p h t", h=NH) for z in range(2)]
                else:
                    YF = [B.HB[:, 4 + z, :].bitcast(F32).rearrange("p (h t) -> p h t", h=NH) for z in range(2)]
                cb = {}
                for nm, wdt in (("KHt", W), ("BHt", W), ("Vt", WV), ("M0", W), ("N0", W), ("Akk", W), ("Brk", W), ("Brb", W),
                                ("Nn", W), ("Mn", W), ("Tt", W), ("Xb", WV), ("Ub", WV)):
                    if not low and nm in ("BHt", "M0", "N0", "Akk", "Brb", "Nn", "Mn", "Tt", "Xb", "Ub"):
                        continue
                    nb = 4 if nm == "Tt" else 2
                    cb[nm] = [B.sb("cb%s%s%d" % (kind, nm, z), [64, wdt], BF16) for z in range(nb)]
                if low:
                    nrm = {nm: pt[pn][:].rearrange("p h t -> p (h t)")[:, 0:512] for nm, pn in (("a", "inc"), ("b", "inv"), ("c", "exc"))}
                    nrmt = dict(a="ptinc", b="ptinv", c="ptexc")
                else:
                    nrm = {nm: B.sb("nrm%s%s" % (kind, nm), [VP, 512], F32)[:] for nm in ("a", "b", "c")}
                    nrmt = dict(a="nrma", b="nrmb", c="nrmc")
                nrb = {nm: B.sb("nrb%s%s" % (kind, nm), [VP, 512], BF16) for nm in ("a", "b")}
                if low:
                    gl = {"g": ld["kk"], "bv": ld["a"]}
                    glt = {"g": "ldkk", "bv": "lda"}
                else:
                    gl = {"g": [B.sb("glgg", [VP, NH, SEG], BF16)] * 2}
                    glt = {"g": "glg"}
                src = dict(q=A["RR"] if low else A["GQ"], k=A["RK"] if low else A["GK"], v=A["RV"] if low else A["GV"])
                if low:
                    src["kk"] = A["RKK"]
                stin = A["rstate"] if low else A["gstate"]
                stout = A["rst_out"] if low else A["gst_out"]
                YFS = A["YFR"] if low else A["YFG"]
                DEC = A["SG"] if low else A["LA"]
                DW = 512 if low else 256
                sn = 0
                for d in range(2):
                    fwd = d == 0
                    triI, triE, triS = (0, 1, 2) if fwd else (3, 2, 1)
                    mSU, mIU, mSL = (0, 1, 2) if fwd else (2, 3, 0)
                    B.dma("sp", Hf[:], stin[i, d], writes=("Hf",))
                    B.cp("act", Hb[:], Hf[:], reads=("Hf",), writes=("Hb",))
                    segs = list(range(NSEG)) if fwd else list(range(NSEG - 1, -1, -1))
                    for seg in segs:
                        z = sn % 2
                        sn += 1
                        ssl = slice(seg * SEG, (seg + 1) * SEG)
                        L = {nm: ld[nm][z] for nm in names}
                        LT = {nm: "ld%s" % nm for nm in names}
                        for nm in names:
                            sa = A["RA"][d] if nm == "a" else src[nm]
                            B.dma("sp", L[nm][:], sa.rearrange("h p t -> p h t")[:, :, ssl], writes=(LT[nm],))
                        B.dma("sp", dec[z][:], DEC[ssl, d * DW:(d + 1) * DW].rearrange("(q p) c -> p q c", p=128), writes=("dec0",))
                        if not fwd:
                            B.dma("sp", YF[z][:], YFS.rearrange("h p t -> p h t")[:, :, ssl], writes=("YF%d" % z,))
                        O = {nm: opd[nm][z] for nm in onm}
                        OT = {nm: "op%s%d" % (nm, z) for nm in onm}
                        for tq in range(2):
                            qsl = slice(tq * 128, (tq + 1) * 128)
                            esc = -EXPC if low else 1.0
                            for var, trik, outs in ((0, triI, (("inc", esc), ("inv", -esc))), (1, triE, (("exc", esc),)), (2, triS, (("end", esc),))):
                                if not low and var == 1:
                                    continue
                                for hg in range(NH // 4):
                                    pd = psD[cnt["d"] % 2]
                                    pdt = "psD%d" % (cnt["d"] % 2)
                                    cnt["d"] += 1
                                    for h4 in range(4):
                                        h = hg * 4 + h4
                                        B.mm(pd[:, h4, :], dec[z][:, tq, h * 64:(h + 1) * 64], tri[:, trik, :], True, True,
                                             reads=("dec0", "tri"), writes=(pdt,))
                                    for onm_, sc in outs:
                                        B.act(pt[onm_][:, hg * 4:(hg + 1) * 4, :], pd[:], AF.Exp, scale=sc, reads=(pdt,), writes=("pt" + onm_,))
                            if low:
                                B.tt("dve", pt["b"][:], L["a"][:, :, qsl], L["kk"][:, :, qsl], ALU.mult, reads=(LT["a"], LT["kk"]), writes=("ptb",))
                                for h in range(NH):
                                    B.ts("dve", pt["kd"][:, h, :], L["a"][:, h, qsl], rc[:, 8 + h:9 + h], omka[:, h:h + 1], ALU.mult, ALU.add,
                                         reads=(LT["a"], "rc2", "omka"), writes=("ptkd",))
                                B.tt("dve", pt["kd"][:], pt["kd"][:], L["k"][:, :, qsl], ALU.mult, reads=("ptkd", LT["k"]), writes=("ptkd",))
                                kd = pt["kd"][:]
                                kdt = "ptkd"
                                B.tt("dve", O["RHO"][:, :, qsl], L["q"][:, :, qsl], pt["inc"][:], ALU.mult, reads=(LT["q"], "ptinc"), writes=(OT["RHO"],))
                                B.tt("pool", O["KAP"][:, :, qsl], L["kk"][:, :, qsl], pt["exc"][:], ALU.mult, reads=(LT["kk"], "ptexc"), writes=(OT["KAP"],))
                                B.stt("dve", O["BT"][:, :, qsl], pt["b"][:], -1.0, pt["inv"][:], ALU.mult, ALU.mult, reads=("ptb", "ptinv"), writes=(OT["BT"],))
                                B.stt("dve", O["BH"][:, :, qsl], pt["b"][:], -1.0, pt["end"][:], ALU.mult, ALU.mult, reads=("ptb", "ptend"), writes=(OT["BH"],))
                            else:
                                kd = L["k"][:, :, qsl]
                                kdt = LT["k"]
                                B.stt("dve", O["RHO"][:, :, qsl], L["q"][:, :, qsl], 0.125, pt["inc"][:], ALU.mult, ALU.mult,
                                      reads=(LT["q"], "ptinc"), writes=(OT["RHO"],))
                            B.tt("dve", O["KT"][:, :, qsl], kd, pt["inv"][:], ALU.mult, reads=(kdt, "ptinv"), writes=(OT["KT"],))
                            B.tt("pool", O["KH"][:, :, qsl], kd, pt["end"][:], ALU.mult, reads=(kdt, "ptend"), writes=(OT["KH"],))
                            ccol = 63 if fwd else 0
                            B.cp("dve", pC[z][:, :, tq * 2:(tq + 1) * 2], pt["inc"][:, :, ccol:128:64], reads=("ptinc",), writes=("pC%d" % z,))

                        def pre(c):
                            zz = c % 2
                            cs = slice(c * CH, (c + 1) * CH)
                            todo = [("KHt", O["KH"], OT["KH"], 64)] + ([("BHt", O["BH"], OT["BH"], 64)] if low else []) + [("Vt", L["v"], LT["v"], DV)]
                            for nm, sarr, stk, wd in todo:
                                pq = cnt["t"] % 2
                                cnt["t"] += 1
                                for h in range(NH):
                                    B.tr(psTr[pq][:, h * wd:(h + 1) * wd], sarr[:, h, cs], B.idb[0:(VP if nm == "Vt" else 64), 0:(VP if nm == "Vt" else 64)],
                                         reads=(stk, "idb"), writes=("psTr%d" % pq,))
                                B.cp("act", cb[nm][zz][:, 0:NH * wd], psTr[pq][:, 0:NH * wd], reads=("psTr%d" % pq,), writes=("cb%s%d" % (nm, zz),))
                            mats = [("Brk", "KT", "RHO", mIU)]
                            if low:
                                mats += [("M0", "BT", "KAP", mSU), ("N0", "KAP", "BT", mSL), ("Akk", "KT", "KAP", mSU), ("Brb", "BT", "RHO", mIU)]
                            for nm, la_, ra_, mk in mats:
                                pa, pat = next_psA()
                                for h in range(NH):
                                    B.mm(pa[0:64, h * 64:(h + 1) * 64], O[la_][:, h, cs], O[ra_][:, h, cs], True, True,
                                         reads=(OT[la_], OT[ra_]), writes=(pat,))
                                B.tt("dve", cb[nm][zz][:].rearrange("p (h t) -> p h t", h=NH), pa[0:64, 0:W].rearrange("p (h t) -> p h t", h=NH),
                                     msk[:, mk:mk + 1, :].to_broadcast([64, NH, 64]), ALU.mult, reads=(pat, "msk"), writes=("cb%s%d" % (nm, zz),))
                            if low:
                                tb0 = (c % 2) * 2

                                def v3(ap):
                                    return ap.rearrange("p (h t) -> p h t", h=NH)

                                def lm(k):
                                    return lvl[:, k:k + 1, :].to_broadcast([64, NH, 64])
                                M0v, N0v = v3(cb["M0"][zz][:]), v3(cb["N0"][zz][:])
                                m0t, n0t = "cbM0%d" % zz, "cbN0%d" % zz
                                Tm, Tmt = cb["Nn"][0], "cbNn0"
                                Tt, Ttt = cb["Tt"][tb0], "cbTt%d" % tb0
                                B.tt("dve", v3(Tm[:]), N0v, lm(0), ALU.mult, reads=(n0t, "lvl"), writes=(Tmt,))
                                B.tt("dve", Tm[:], Tm[:], idr[:, 0:W], ALU.add, reads=(Tmt, "idr"), writes=(Tmt,))
                                B.tt("dve", v3(Tt[:]), M0v, lm(0), ALU.mult, reads=(m0t, "lvl"), writes=(Ttt,))
                                B.tt("dve", Tt[:], Tt[:], idr[:, 0:W], ALU.add, reads=(Ttt, "idr"), writes=(Ttt,))
                                for lev in range(1, 6):
                                    Ml, Mlt = cb["Mn"][0], "cbMn0"
                                    Pb, Pbt = cb["Mn"][1], "cbMn1"
                                    B.tt("dve", v3(Ml[:]), M0v, lm(lev), ALU.mult, reads=(m0t, "lvl"), writes=(Mlt,))
                                    pa, pat = next_psA()
                                    for h in range(NH):
                                        hs_ = slice(h * 64, (h + 1) * 64)
                                        B.mm(pa[0:64, hs_], Ml[:, hs_], Tm[:, hs_], True, True, reads=(Mlt, Tmt), writes=(pat,))
                                    B.cp("act", Pb[:], pa[0:64, 0:W], reads=(pat,), writes=(Pbt,))
                                    if lev < 5:
                                        pa2, pat2 = next_psA()
                                        for h in range(NH):
                                            hs_ = slice(h * 64, (h + 1) * 64)
                                            B.mm(pa2[0:64, hs_], Tt[:, hs_], Pb[:, hs_], True, True, reads=(Ttt, Pbt), writes=(pat2,))
                                    pa3, pat3 = next_psA()
                                    for h in range(NH):
                                        hs_ = slice(h * 64, (h + 1) * 64)
                                        B.mm(pa3[0:64, hs_], Pb[:, hs_], Tt[:, hs_], True, True, reads=(Pbt, Ttt), writes=(pat3,))
                                    if lev < 5:
                                        Tn, Tnt = cb["Nn"][lev % 2], "cbNn%d" % (lev % 2)
                                        B.tt("dve", Tn[:], pa2[0:64, 0:W], Tm[:], ALU.add, reads=(pat2, Tmt), writes=(Tnt,))
                                    T2, T2t = cb["Tt"][tb0 + (lev % 2)], "cbTt%d" % (tb0 + (lev % 2))
                                    B.tt("dve", T2[:], pa3[0:64, 0:W], Tt[:], ALU.add, reads=(pat3, Ttt), writes=(T2t,))
                                    Tt, Ttt = T2, T2t
                                    if lev < 5:
                                        Tm, Tmt = Tn, Tnt
                                return (Tt, Ttt)
                            return None

                        def seq(c, tinfo):
                            zz = c % 2
                            cs = slice(c * CH, (c + 1) * CH)
                            Vt, Vtt = cb["Vt"][zz], "cbVt%d" % zz
                            KHt, KHtt = cb["KHt"][zz], "cbKHt%d" % zz
                            Brk, Brkt = cb["Brk"][zz], "cbBrk%d" % zz
                            if low:
                                Tt, Ttt = tinfo
                                Akk, Akkt = cb["Akk"][zz], "cbAkk%d" % zz
                                Brb, Brbt = cb["Brb"][zz], "cbBrb%d" % zz
                                BHt, BHtt = cb["BHt"][zz], "cbBHt%d" % zz
                                Xb, Xbt = cb["Xb"][zz], "cbXb%d" % zz
                                Ub, Ubt = cb["Ub"][zz], "cbUb%d" % zz
                                pa, pat = next_psA()
                                for h in range(NH):
                                    hs_ = slice(h * 64, (h + 1) * 64)
                                    B.mm(pa[0:64, hs_], O["KAP"][:, h, cs], Hb[:, h, :], True, False, reads=(OT["KAP"], "Hb"), writes=(pat,))
                                    B.mm(pa[0:64, hs_], Akk[:, hs_], Vt[:, hs_], False, True, reads=(Akkt, Vtt), writes=(pat,))
                                B.cp("act", Xb[:], pa[0:64, 0:WV], reads=(pat,), writes=(Xbt,))
                                pa, pat = next_psA()
                                for h in range(NH):
                                    hs_ = slice(h * 64, (h + 1) * 64)
                                    B.mm(pa[0:64, hs_], Tt[:, hs_], Xb[:, hs_], True, True, reads=(Ttt, Xbt), writes=(pat,))
                                B.cp("act", Ub[:], pa[0:64, 0:WV], reads=(pat,), writes=(Ubt,))
                            pa, pat = next_psA()
                            for h in range(NH):
                                hs_ = slice(h * 64, (h + 1) * 64)
                                vs_ = slice(h * DV, (h + 1) * DV)
                                B.mm(pa[0:VP, hs_], Hb[:, h, :], O["RHO"][:, h, cs], True, False, reads=("Hb", OT["RHO"]), writes=(pat,))
                                B.mm(pa[0:VP, hs_], Vt[:, vs_], Brk[:, hs_], False, not low, reads=(Vtt, Brkt), writes=(pat,))
                                if low:
                                    B.mm(pa[0:VP, hs_], Ub[:, vs_], Brb[:, hs_], False, True, reads=(Ubt, Brbt), writes=(pat,))
                            yv = pa[0:VP, 0:W].rearrange("p (h t) -> p h t", h=NH)
                            if fwd:
                                B.cp("act", YF[z][:, :, cs], yv, reads=(pat,), writes=("YF%d" % z,))
                            else:
                                B.tt("dve", YF[z][:, :, cs], yv, YF[z][:, :, cs], ALU.add, reads=(pat, "YF%d" % z), writes=("YF%d" % z,))
                            pa, pat = next_psA()
                            for h in range(NH):
                                hs_ = slice(h * 64, (h + 1) * 64)
                                vs_ = slice(h * DV, (h + 1) * DV)
                                B.mm(pa[0:64, vs_], KHt[:, hs_], Vt[:, vs_], True, not low, reads=(KHtt, Vtt), writes=(pat,))
                                if low:
                                    B.mm(pa[0:64, vs_], BHt[:, hs_], Ub[:, vs_], False, True, reads=(BHtt, Ubt), writes=(pat,))
                            B.tt("dve", Hf[:], Hf[:], pC[z][:, :, c:c + 1].to_broadcast([64, NH, DV]), ALU.mult, reads=("Hf", "pC%d" % z), writes=("Hf",))
                            B.tt("dve", Hf[:], Hf[:], pa[0:64, 0:WV].rearrange("p (h v) -> p h v", h=NH), ALU.add, reads=("Hf", pat), writes=("Hf",))
                            B.cp("act", Hb[:], Hf[:], reads=("Hf",), writes=("Hb",))

                        order = list(range(4)) if fwd else [3, 2, 1, 0]
                        tinfo = pre(order[0])
                        for ci, c in enumerate(order):
                            nxt = pre(order[ci + 1]) if ci + 1 < 4 else None
                            seq(c, tinfo)
                            tinfo = nxt
                        B.dma("sp", stout[i, d, seg], Hf[:], reads=("Hf",), is_out=True)
                        B.ts("dve", Hf[:], Hf[:], flg[0:64, 2:3], None, ALU.mult, reads=("Hf", "flg"), writes=("Hf",))
                        B.cp("act", Hb[:], Hf[:], reads=("Hf",), writes=("Hb",))
                        if fwd:
                            B.dma("sp", YFS.rearrange("h p t -> p h t")[:, :, ssl], YF[z][:], reads=("YF%d" % z,), writes=("YFS%d" % seg,))
                        else:
                            for nm in gl:
                                sa = A["RG"] if (low and nm == "g") else (A["RBV"] if nm == "bv" else A["GG"])
                                B.dma("sp", gl[nm][z][:], sa.rearrange("h p t -> p h t")[:, :, ssl], writes=(glt[nm],))
                            ab_finish_segment(B, l, kind, low, NH, VP, YF[z], "YF%d" % z, gl, glt, z, nrm, nrmt, nrb, rc, gnw, ones1, next_psA, seg)
                B.fence()
        ab_rwkv_parallel(B, l, dict(tri=tri, msk2=msk2, idr2=idr2, lvl2=lvl2, rc2=rc2, omka2=omka2, gnw=gnw, ones1=ones1,
                                    psD=psD, psTr=psTr, psA=psA))
        B.fence()
    B.es = old


def ab_finish_segment(B, l, kind, low, NH, VP, Y, Yt, gl, glt, z, nrm, nrmt, nrb, rc, gnw, ones1, next_psA, seg):
    ssl = slice(seg * SEG, (seg + 1) * SEG)
    for hp in range(NH // 2):
        yv = Y[:, hp * 2:hp * 2 + 2, :]
        a, b_, c_ = nrm["a"].rearrange("p (h t) -> p h t", h=2), nrm["b"].rearrange("p (h t) -> p h t", h=2), nrm["c"].rearrange("p (h t) -> p h t", h=2)
        ba, bb = nrb["a"][:].rearrange("p (h t) -> p h t", h=2), nrb["b"][:].rearrange("p (h t) -> p h t", h=2)
        B.act(bb, yv, AF.Square, reads=(Yt,), writes=("nrbb",))
        pq, pqt = next_psA()
        B.mm(pq[0:VP, :], ones1[0:VP, 0:VP], nrb["b"][:], True, True, reads=("ones1b", "nrbb"), writes=(pqt,))
        if low:
            B.cp("act", ba, yv, reads=(Yt,), writes=("nrba",))
            pm_, pmt = next_psA()
            B.mm(pm_[0:VP, :], ones1[0:VP, 0:VP], nrb["a"][:], True, True, reads=("ones1b", "nrba"), writes=(pmt,))
            B.act(nrm["a"], pm_[0:VP, :], AF.Identity, scale=1.0 / 64.0, reads=(pmt,), writes=(nrmt["a"],))
            B.tt("dve", nrm["b"], nrm["a"], nrm["a"], ALU.mult, reads=(nrmt["a"],), writes=(nrmt["b"],))
            B.stt("dve", nrm["b"], pq[0:VP, :], 1.0 / 64.0, nrm["b"], ALU.mult, ALU.subtract, reads=(pqt, nrmt["b"]), writes=(nrmt["b"],))
            B.act(nrm["b"], nrm["b"], AF.Ln, bias=B.eps_ln[0:VP, 2:3], reads=(nrmt["b"],), writes=(nrmt["b"],))
            B.act(nrm["b"], nrm["b"], AF.Exp, scale=-0.5, reads=(nrmt["b"],), writes=(nrmt["b"],))
            B.tt("dve", c_, yv, a, ALU.subtract, reads=(Yt, nrmt["a"]), writes=(nrmt["c"],))
            B.tt("dve", c_, c_, b_, ALU.mult, reads=(nrmt["c"], nrmt["b"]), writes=(nrmt["c"],))
            for hh in range(2):
                h = hp * 2 + hh
                B.ts("dve", c_[:, hh, :], c_[:, hh, :], rc[:, 24 + h:25 + h], rc[:, 32 + h:33 + h], ALU.mult, ALU.add, reads=(nrmt["c"], "rc2"), writes=(nrmt["c"],))
            B.tt("pool", c_, c_, gl["bv"][z][:, hp * 2:hp * 2 + 2, :], ALU.add, reads=(nrmt["c"], glt["bv"]), writes=(nrmt["c"],))
            B.tt("pool", B.ORW[:, hp * 2:hp * 2 + 2, ssl], c_, gl["g"][z][:, hp * 2:hp * 2 + 2, :], ALU.mult, reads=(nrmt["c"], glt["g"]), writes=("ORW%d" % seg,))
        else:
            B.act(nrm["b"], pq[0:VP, :], AF.Ln, scale=1.0 / 128.0, bias=B.eps_ln[0:VP, 0:1], reads=(pqt,), writes=(nrmt["b"],))
            B.act(nrm["b"], nrm["b"], AF.Exp, scale=-0.5, reads=(nrmt["b"],), writes=(nrmt["b"],))
            B.stt("dve", c_, yv, gnw[:, 0:1], b_, ALU.mult, ALU.mult, reads=(Yt, "gnw", nrmt["b"]), writes=(nrmt["c"],))
            for hh in range(2):
                h = hp * 2 + hh
                tt = seg // 2
                B.tt("pool", B.HB[:, h, ssl], c_[:, hh, :], gl["g"][z][:, h, :], ALU.mult, reads=(nrmt["c"], glt["g"]), writes=("HBg%d:%d" % (h, seg),))


def ab_phase3(B, l):
    i = l // 2
    A = ab_inputs(B)
    X, HB, modc = B.X, B.HB, B.modc
    xtok, htok = B.xtok, B.htok
    with ExitStack() as ph:
        B.es, old = ph, B.es
        for k in range(8):
            B.dma("sp", X[:, k, :], A["XS"][k], reads=("XS%d" % k,), writes=[xtok(k, tt) for tt in range(NT)])
        lnps = B.alloc_ln()
        wog = [B.sb("wog%d" % k, [128, 4, 128], BF16) for k in range(8)]
        wor = [B.sb("wor%d" % k, [64, 8, 128], BF16) for k in range(8)]
        otm = [B.sb("aotm%d" % k, [128, 512], F32) for k in range(2)]
        psO = [B.ps("apsO%d" % k, [128, 512], F32) for k in range(4)]
        for dch in range(8):
            B.dma("pool", wog[dch][:], A["abwo_g"][i, dch], writes=("wog%d" % dch,))
            B.dma("pool", wor[dch][:], A["abwo_r"][i, dch], writes=("wor%d" % dch,))
        oc = [0]

        def OP(tt):
            tsl = slice(tt * 512, (tt + 1) * 512)
            for dch in range(8):
                q = oc[0] % 4
                oc[0] += 1
                for h in range(4):
                    B.mm(psO[q][:], wog[dch][:, h, :], HB[:, h, tsl], h == 0, False, reads=("wog%d" % dch, htok(h, tt)), writes=("apsO%d" % q,))
                for h in range(8):
                    B.mm(psO[q][:], wor[dch][:, h, :], B.ORW[:, h, tsl], False, h == 7, reads=("wor%d" % dch,), writes=("apsO%d" % q,))
                c = l * 48 + 2 * 8 + dch
                B.act(otm[q % 2][:], psO[q][:], AF.Identity, scale=modc[:, c:c + 1], reads=("apsO%d" % q, "modc"), writes=("aotm%d" % (q % 2),))
                B.stt("dve", X[:, dch, tsl], X[:, dch, tsl], ALPHA, otm[q % 2][:], ALU.mult, ALU.add,
                      reads=(xtok(dch, tt), "aotm%d" % (q % 2)), writes=(xtok(dch, tt),))
                yield

        def LNt(tt):
            yield from B.ln_tile_gen(lnps, l, 0, tt, (l, 3))
        drain(OP(0))
        for tt in range(1, NT):
            drain(OP(tt), LNt(tt - 1))
        drain(LNt(NT - 1))
        B.fence()
    B.es = old


def ab_mixer(B, l):
    ab_phase1(B, l)
    with ExitStack() as mix:
        B.es, old = mix, B.es
        B.ORW = B.sb("ORW", [64, 8, T], BF16)
        ab_phase2(B, l)
        ab_phase3(B, l)
    B.es = old


def ab_rwkv_parallel(B, l, C):
    i = l // 2
    A = ab_inputs(B)
    P = B.P
    flg = B.flg
    NH, DV, W = 8, 64, 512
    tri, msk2, idr2, lvl2, rc2, omka2, ones1 = (C[k] for k in ("tri", "msk2", "idr2", "lvl2", "rc2", "omka2", "ones1"))
    psD, psTr, psA = C["psD"], C["psTr"], C["psA"]
    if "YBR" not in A:
        A["YBR"] = B.scratch("YBR", [8, 64, T], F32)
    with ExitStack() as kp:
        B.es, old = kp, B.es
        ldv = B.sb("ldrv", [128, NH, SEG], BF16)
        dec = [B.sb("decr%d" % d, [128, 2, W], F32) for d in range(2)]
        pC = B.sb("pCr", [128, NH, 4], F32)
        pt = {nm: B.sb("ptr" + nm, [128, NH, 128], F32) for nm in ("inc", "inv", "exc", "end", "b", "kd")}
        Hf = B.sb("Hfr", [128, NH, DV], F32)
        Hb = B.sb("Hbr", [128, NH, DV], BF16)
        cbn = [("KHt", 2), ("BHt", 2), ("Vt", 2), ("M0", 2), ("N0", 2), ("Akk", 2), ("Brk", 2), ("Brb", 2), ("Nn", 2), ("Mn", 2), ("Tt", 4), ("Xb", 2), ("Ub", 2)]
        cb = {nm: [B.sb("cbp%s%d" % (nm, z), [128, W], BF16) for z in range(nb)] for nm, nb in cbn}
        nrb = {nm: B.sb("nrbp" + nm, [64, 512], BF16) for nm in ("a", "b")}
        onm = ["RHO", "KT", "KH", "KAP", "BT", "BH"]
        names = ["q", "k", "v", "kk", "a"]
        src = dict(q=A["RR"], k=A["RK"], v=A["RV"], kk=A["RKK"])
        xslots = {"q": (6, 0), "k": (6, 1), "kk": (7, 0), "a": (7, 1)}

        def xview(po, k, half):
            return B.X[po:po + 64, k, half * 1024:(half + 1) * 1024].bitcast(BF16).rearrange("p (h t) -> p h t", h=NH)

        def stream(d):
            po = 64 * d
            ps_ = slice(po, po + 64)
            fwd = d == 0
            triI, triE, triS = (0, 1, 2) if fwd else (3, 2, 1)
            mSU, mIU, mSL = (0, 1, 2) if fwd else (2, 3, 0)
            msk, idr, lvl, rc, omka = msk2[ps_], idr2[ps_], lvl2[ps_], rc2[ps_], omka2[ps_]
            L = {nm: (ldv[ps_] if nm == "v" else xview(po, *xslots[nm])) for nm in names}
            opd = {nm: [xview(po, (oi_ * 2 + z) // 2, (oi_ * 2 + z) % 2) for z in range(2)] for oi_, nm in enumerate(onm)}
            YF = [B.HB[ps_, 4 + 2 * z:6 + 2 * z, :].rearrange("p k t -> p (k t)").bitcast(F32).rearrange("p (h t) -> p h t", h=NH) for z in range(2)]
            ptd = {nm: pt[nm][ps_] for nm in pt}
            Hfd, Hbd, pCd = Hf[ps_], Hb[ps_], pC[ps_]
            cbd = {nm: [t_[ps_] for t_ in cb[nm]] for nm in cb}
            YS = A["YFR"] if fwd else A["YBR"]
            pa_n = [0]

            def next_psA():
                q = 2 * d + (pa_n[0] % 2)
                pa_n[0] += 1
                return psA[q], "psA%d" % q
            pd, pdt = psD[d], "psD"
            pd2 = psD[d][:].rearrange("p a b -> p (a b)")
            SEQ_EVERY = B.cfg.get("seq_every", 1)
            pT, pTt = psTr[d], "psTr"
            idb = B.idb[ps_, po:po + 64]

            def v3(ap):
                return ap.rearrange("p (h t) -> p h t", h=NH)

            B.dma("sp", Hfd, A["rstate"][i, d], writes=("Hf",))
            B.cp("act", Hbd, Hfd, reads=("Hf",), writes=("Hb",))
            yield
            segs = list(range(NSEG)) if fwd else list(range(NSEG - 1, -1, -1))
            for sn, seg in enumerate(segs):
                z = sn % 2
                ssl = slice(seg * SEG, (seg + 1) * SEG)
                for nm in names:
                    sa = A["RA"][d] if nm == "a" else src[nm]
                    B.dma("sp", L[nm], sa.rearrange("h p t -> p h t")[:, :, ssl], writes=("ld" + nm,))
                B.dma("sp", dec[d][:], A["SG"][ssl, d * 512:(d + 1) * 512].rearrange("(q p) c -> p q c", p=128), writes=("dec",))
                O = {nm: opd[nm][z] for nm in onm}
                OT = {nm: "op%s%d" % (nm, z) for nm in onm}
                yield
                for tq in range(2):
                    qsl = slice(tq * 128, (tq + 1) * 128)
                    for var, trik, outs in ((0, triI, (("inc", -EXPC), ("inv", EXPC))), (1, triE, (("exc", -EXPC),)), (2, triS, (("end", -EXPC),))):
                        for hg in range(2):
                            for h4 in range(4):
                                h = hg * 4 + h4
                                B.mm(pd[:, h4, :], dec[d][:, tq, h * 64:(h + 1) * 64], tri[:, trik, :], True, True, reads=("dec",), writes=(pdt,))
                            for onm_, sc in outs:
                                B.act(ptd[onm_][:, hg * 4:(hg + 1) * 4, :], pd[:], AF.Exp, scale=sc, reads=(pdt,), writes=("pt" + onm_,))
                            yield
                    B.tt("dve", ptd["b"], L["a"][:, :, qsl], L["kk"][:, :, qsl], ALU.mult, reads=("lda", "ldkk"), writes=("ptb",))
                    for h in range(NH):
                        B.ts("dve", ptd["kd"][:, h, :], L["a"][:, h, qsl], rc[:, 8 + h:9 + h], omka[:, h:h + 1], ALU.mult, ALU.add,
                             reads=("lda",), writes=("ptkd",))
                    B.tt("dve", ptd["kd"], ptd["kd"], L["k"][:, :, qsl], ALU.mult, reads=("ptkd", "ldk"), writes=("ptkd",))
                    yield
                    B.tt("dve", O["RHO"][:, :, qsl], L["q"][:, :, qsl], ptd["inc"], ALU.mult, reads=("ldq", "ptinc"), writes=(OT["RHO"],))
                    B.tt("pool", O["KAP"][:, :, qsl], L["kk"][:, :, qsl], ptd["exc"], ALU.mult, reads=("ldkk", "ptexc"), writes=(OT["KAP"],))
                    B.stt("dve", O["BT"][:, :, qsl], ptd["b"], -1.0, ptd["inv"], ALU.mult, ALU.mult, reads=("ptb", "ptinv"), writes=(OT["BT"],))
                    yield
                    B.stt("dve", O["BH"][:, :, qsl], ptd["b"], -1.0, ptd["end"], ALU.mult, ALU.mult, reads=("ptb", "ptend"), writes=(OT["BH"],))
                    B.tt("pool", O["KT"][:, :, qsl], ptd["kd"], ptd["inv"], ALU.mult, reads=("ptkd", "ptinv"), writes=(OT["KT"],))
                    B.tt("pool", O["KH"][:, :, qsl], ptd["kd"], ptd["end"], ALU.mult, reads=("ptkd", "ptend"), writes=(OT["KH"],))
                    ccol = 63 if fwd else 0
                    B.cp("dve", pCd[:, :, tq * 2:(tq + 1) * 2], ptd["inc"][:, :, ccol:128:64], reads=("ptinc",), writes=("pC",))
                    yield

                def pre(c):
                    zz = c % 2
                    cs = slice(c * CH, (c + 1) * CH)
                    for nm, sarr, stk in (("KHt", O["KH"], OT["KH"]), ("BHt", O["BH"], OT["BH"]), ("Vt", L["v"], "ldv")):
                        for h in range(NH):
                            B.tr(pT[:, h * 64:(h + 1) * 64], sarr[:, h, cs], idb, reads=(stk,), writes=(pTt,))
                        B.cp("act", cbd[nm][zz], pT[:, 0:W], reads=(pTt,), writes=("cb%s%d" % (nm, zz),))
                        yield
                    for nm, la_, ra_, mk in (("Brk", "KT", "RHO", mIU), ("M0", "BT", "KAP", mSU), ("N0", "KAP", "BT", mSL),
                                             ("Akk", "KT", "KAP", mSU), ("Brb", "BT", "RHO", mIU)):
                        pa, pat = next_psA()
                        for h in range(NH):
                            B.mm(pa[0:64, h * 64:(h + 1) * 64], O[la_][:, h, cs], O[ra_][:, h, cs], True, True, reads=(OT[la_], OT[ra_]), writes=(pat,))
                        B.tt("dve", v3(cbd[nm][zz]), v3(pa[0:64, 0:W]), msk[:, mk:mk + 1, :].to_broadcast([64, NH, 64]), ALU.mult,
                             reads=(pat,), writes=("cb%s%d" % (nm, zz),))
                        yield
                    tb0 = (c % 2) * 2

                    def lm(k):
                        return lvl[:, k:k + 1, :].to_broadcast([64, NH, 64])
                    M0v, N0v = v3(cbd["M0"][zz]), v3(cbd["N0"][zz])
                    m0t, n0t = "cbM0%d" % zz, "cbN0%d" % zz
                    Tm, Tmt = cbd["Nn"][0], "cbNn0"
                    Tt, Ttt = cbd["Tt"][tb0], "cbTt%d" % tb0
                    B.tt("dve", v3(Tm), N0v, lm(0), ALU.mult, reads=(n0t,), writes=(Tmt,))
                    B.tt("pool", Tm, Tm, idr[:, 0:W], ALU.add, reads=(Tmt,), writes=(Tmt,))
                    B.tt("dve", v3(Tt), M0v, lm(0), ALU.mult, reads=(m0t,), writes=(Ttt,))
                    B.tt("pool", Tt, Tt, idr[:, 0:W], ALU.add, reads=(Ttt,), writes=(Ttt,))
                    yield
                    for lev in range(1, 6):
                        Ml, Mlt = cbd["Mn"][0], "cbMn0"
                        Pb, Pbt = cbd["Mn"][1], "cbMn1"
                        B.tt("dve", v3(Ml), M0v, lm(lev), ALU.mult, reads=(m0t,), writes=(Mlt,))
                        pa, pat = next_psA()
                        for h in range(NH):
                            hs_ = slice(h * 64, (h + 1) * 64)
                            B.mm(pa[0:64, hs_], Ml[:, hs_], Tm[:, hs_], True, True, reads=(Mlt, Tmt), writes=(pat,))
                        B.cp("act", Pb, pa[0:64, 0:W], reads=(pat,), writes=(Pbt,))
                        yield
                        if lev < 5:
                            pa2, pat2 = next_psA()
                            for h in range(NH):
                                hs_ = slice(h * 64, (h + 1) * 64)
                                B.mm(pa2[0:64, hs_], Tt[:, hs_], Pb[:, hs_], True, True, reads=(Ttt, Pbt), writes=(pat2,))
                            Tn, Tnt = cbd["Nn"][lev % 2], "cbNn%d" % (lev % 2)
                            B.tt("dve", Tn, pa2[0:64, 0:W], Tm, ALU.add, reads=(pat2, Tmt), writes=(Tnt,))
                            yield
                        pa3, pat3 = next_psA()
                        for h in range(NH):
                            hs_ = slice(h * 64, (h + 1) * 64)
                            B.mm(pa3[0:64, hs_], Pb[:, hs_], Tt[:, hs_], True, True, reads=(Pbt, Ttt), writes=(pat3,))
                        T2, T2t = cbd["Tt"][tb0 + (lev % 2)], "cbTt%d" % (tb0 + (lev % 2))
                        B.tt("dve", T2, pa3[0:64, 0:W], Tt, ALU.add, reads=(pat3, Ttt), writes=(T2t,))
                        Tt, Ttt = T2, T2t
                        if lev < 5:
                            Tm, Tmt = Tn, Tnt
                        yield
                    self_t[c] = (Tt, Ttt)

                def seq(c):
                    zz = c % 2
                    cs = slice(c * CH, (c + 1) * CH)
                    Tt, Ttt = self_t[c]
                    Vt, Vtt = cbd["Vt"][zz], "cbVt%d" % zz
                    KHt, KHtt = cbd["KHt"][zz], "cbKHt%d" % zz
                    BHt, BHtt = cbd["BHt"][zz], "cbBHt%d" % zz
                    Brk, Brkt = cbd["Brk"][zz], "cbBrk%d" % zz
                    Brb, Brbt = cbd["Brb"][zz], "cbBrb%d" % zz
                    Akk, Akkt = cbd["Akk"][zz], "cbAkk%d" % zz
                    Xb, Xbt = cbd["Xb"][zz], "cbXb%d" % zz
                    Ub, Ubt = cbd["Ub"][zz], "cbUb%d" % zz
                    pa, pat = pd2, "psD"
                    for h in range(NH):
                        hs_ = slice(h * 64, (h + 1) * 64)
                        B.mm(pa[0:64, hs_], O["KAP"][:, h, cs], Hbd[:, h, :], True, False, reads=(OT["KAP"], "Hb"), writes=(pat,))
                        B.mm(pa[0:64, hs_], Akk[:, hs_], Vt[:, hs_], False, True, reads=(Akkt, Vtt), writes=(pat,))
                    B.cp("act", Xb, pa[0:64, 0:W], reads=(pat,), writes=(Xbt,))
                    yield
                    pa, pat = pd2, "psD"
                    for h in range(NH):
                        hs_ = slice(h * 64, (h + 1) * 64)
                        B.mm(pa[0:64, hs_], Tt[:, hs_], Xb[:, hs_], True, True, reads=(Ttt, Xbt), writes=(pat,))
                    B.cp("act", Ub, pa[0:64, 0:W], reads=(pat,), writes=(Ubt,))
                    yield
                    pa, pat = pd2, "psD"
                    for h in range(NH):
                        hs_ = slice(h * 64, (h + 1) * 64)
                        B.mm(pa[0:64, hs_], Hbd[:, h, :], O["RHO"][:, h, cs], True, False, reads=("Hb", OT["RHO"]), writes=(pat,))
                        B.mm(pa[0:64, hs_], Vt[:, hs_], Brk[:, hs_], False, False, reads=(Vtt, Brkt), writes=(pat,))
                        B.mm(pa[0:64, hs_], Ub[:, hs_], Brb[:, hs_], False, True, reads=(Ubt, Brbt), writes=(pat,))
                    B.cp("act", YF[z][:, :, cs], v3(pa[0:64, 0:W]), reads=(pat,), writes=("YF%d" % z,))
                    yield
                    pa, pat = pd2, "psD"
                    for h in range(NH):
                        hs_ = slice(h * 64, (h + 1) * 64)
                        B.mm(pa[0:64, hs_], KHt[:, hs_], Vt[:, hs_], True, False, reads=(KHtt, Vtt), writes=(pat,))
                        B.mm(pa[0:64, hs_], BHt[:, hs_], Ub[:, hs_], False, True, reads=(BHtt, Ubt), writes=(pat,))
                    B.tt("dve", Hfd, Hfd, pCd[:, :, c:c + 1].to_broadcast([64, NH, DV]), ALU.mult, reads=("Hf", "pC"), writes=("Hf",))
                    B.tt("dve", Hfd, Hfd, v3(pa[0:64, 0:W]), ALU.add, reads=("Hf", pat), writes=("Hf",))
                    B.cp("act", Hbd, Hfd, reads=("Hf",), writes=("Hb",))
                    yield

                self_t = {}
                order = list(range(4)) if fwd else [3, 2, 1, 0]
                yield from pre(order[0])
                for ci, c in enumerate(order):
                    gs = seq(c)
                    if ci + 1 < 4:
                        k_ = 0
                        for _ in pre(order[ci + 1]):
                            yield
                            k_ += 1
                            if k_ % SEQ_EVERY == 0:
                                try:
                                    next(gs)
                                    yield
                                except StopIteration:
                                    pass
                    yield from gs
                B.dma("sp", A["rst_out"][i, d, seg], Hfd, reads=("Hf",), is_out=True)
                B.ts("dve", Hfd, Hfd, flg[ps_, 2:3], None, ALU.mult, reads=("Hf",), writes=("Hf",))
                B.cp("act", Hbd, Hfd, reads=("Hf",), writes=("Hb",))
                B.dma("sp", YS.rearrange("h p t -> p h t")[:, :, ssl], YF[z], reads=("YF%d" % z,), writes=("YS%d" % seg,))
                yield

        gens = [stream(0), stream(1)]
        alive = [0, 1]
        while alive:
            for d in list(alive):
                P.ns = d
                try:
                    next(gens[d])
                except StopIteration:
                    alive.remove(d)
        P.ns = None
        B.fence()
        YA = B.HB[0:64, 4:6, :].rearrange("p k t -> p (k t)").bitcast(F32).rearrange("p (h t) -> p h t", h=NH)
        YBv = B.HB[0:64, 6:8, :].rearrange("p k t -> p (k t)").bitcast(F32).rearrange("p (h t) -> p h t", h=NH)
        glg = xview(0, 7, 0)
        glbv = xview(0, 7, 1)
        nrm = {nm: pt[pn][0:64].rearrange("p h t -> p (h t)")[:, 0:512] for nm, pn in (("a", "inc"), ("b", "inv"), ("c", "exc"))}
        nrmt = dict(a="ptinc", b="ptinv", c="ptexc")
        pa_n = [0]

        def next_psA2():
            q = pa_n[0] % 4
            pa_n[0] += 1
            return psA[q], "psA%d" % q
        for seg in range(NSEG):
            ssl = slice(seg * SEG, (seg + 1) * SEG)
            B.dma("sp", YA, A["YFR"].rearrange("h p t -> p h t")[:, :, ssl], writes=("YA",))
            B.dma("sp", YBv, A["YBR"].rearrange("h p t -> p h t")[:, :, ssl], writes=("YB",))
            B.dma("sp", glg, A["RG"].rearrange("h p t -> p h t")[:, :, ssl], writes=("glg",))
            B.dma("sp", glbv, A["RBV"].rearrange("h p t -> p h t")[:, :, ssl], writes=("glbv",))
            B.tt("dve", YA, YA, YBv, ALU.add, reads=("YA", "YB"), writes=("YA",))
            ab_finish_segment(B, l, "r", True, NH, 64, YA, "YA", {"g": [glg] * 2, "bv": [glbv] * 2}, {"g": "glg", "bv": "glbv"}, 0,
                              nrm, nrmt, nrb, rc2[0:64], C["gnw"], ones1, next_psA2, seg)
        B.fence()
    B.es = old
```
